# Optimizing a Trainium2 kernel written in Bass

```python
import math
import jax
import jax.numpy as jnp
from jax import lax
import numpy as np

D_MODEL = 1024
BATCH = 8
SEQ = 2048
DEPTH = 4

GRID_W = 64
CTX_LEN = 256
N_MIXERS = 4
GROUP_W = D_MODEL // N_MIXERS
MIX_W = N_MIXERS * GROUP_W
HEAD_DIM = 64
N_FOURIER = GROUP_W // HEAD_DIM
N_DIFF = GROUP_W // HEAD_DIM
DIFF_QK = HEAD_DIM // 2
DIFF_V = HEAD_DIM
N_NA = GROUP_W // HEAD_DIM
NA_KH_MAX = 8
NA_KW = 16
N_GMLP = GROUP_W // HEAD_DIM
CHUNK = 128
D_FF = 2816
CONV_W = 3
ROPE_BASE = 10000.0
Q_BLOCK = 128
EPS = 1e-6
IN_W = GROUP_W + 3 * GROUP_W + 3 * GROUP_W + 2 * GROUP_W

kernel_name = "hybrid_fourier_diffattn_natten_gmlp_dit"


def rms_norm(x, g):
    xf = x.astype(jnp.float32)
    y = xf * lax.rsqrt(jnp.mean(xf * xf, axis=-1, keepdims=True) + EPS)
    return (y * g.astype(jnp.float32)).astype(x.dtype)


def axial_rope_tables(n, dim, dtype):
    n_freq = dim // 4
    freqs = ROPE_BASE ** (-jnp.arange(n_freq, dtype=jnp.float32) / n_freq)
    t = jnp.arange(n)
    row = (t // GRID_W).astype(jnp.float32)
    col = (t % GRID_W).astype(jnp.float32)
    ang = jnp.concatenate([row[:, None] * freqs, col[:, None] * freqs], axis=-1)
    return jnp.cos(ang).astype(dtype), jnp.sin(ang).astype(dtype)


def apply_rope(x, cos, sin):
    x1, x2 = jnp.split(x, 2, axis=-1)
    return jnp.concatenate([x1 * cos - x2 * sin, x1 * sin + x2 * cos], axis=-1)


def dense_attention(q, k, v):
    s = jnp.einsum('bhqd,bhkd->bhqk', q, k).astype(jnp.float32) * (q.shape[-1] ** -0.5)
    a = jax.nn.softmax(s, axis=-1).astype(v.dtype)
    return jnp.einsum('bhqk,bhkd->bhqd', a, v)


def fourier_mix(p):
    b, n, _ = p.shape
    g = p.reshape(b, n, N_FOURIER, HEAD_DIM).astype(jnp.float32)
    f = jnp.fft.fft2(g, axes=(1, 3), norm='ortho').real
    return f.reshape(b, n, GROUP_W).astype(p.dtype)


def diff_q(p, qn):
    b, n, _ = p.shape
    q = rms_norm(p[..., :GROUP_W].reshape(b, n, N_DIFF, 2, DIFF_QK), qn)
    return q.transpose(0, 2, 3, 1, 4)


def diff_kv(p, kn):
    b, n, _ = p.shape
    k = rms_norm(p[..., GROUP_W:2 * GROUP_W].reshape(b, n, N_DIFF, 2, DIFF_QK), kn)
    v = p[..., 2 * GROUP_W:].reshape(b, n, N_DIFF, DIFF_V)
    return k.transpose(0, 2, 3, 1, 4), v.transpose(0, 2, 1, 3)


def diff_core(q, k, v, lam):
    s = jnp.einsum('bhcqd,bhckd->bhcqk', q, k).astype(jnp.float32) * (DIFF_QK ** -0.5)
    a = jax.nn.softmax(s, axis=-1)
    w = (a[:, :, 0] - lam * a[:, :, 1]).astype(v.dtype)
    return jnp.einsum('bhqk,bhkd->bhqd', w, v)


def diff_finish(o, subln, lam_init):
    b, _, n, _ = o.shape
    o = rms_norm(o, subln) * (1.0 - lam_init)
    return o.transpose(0, 2, 1, 3).reshape(b, n, GROUP_W)


def diff_attention_latent(p, kc, vc, qn, kn, lam, subln, lam_init, cos, sin):
    b, n, _ = p.shape
    q = apply_rope(diff_q(p, qn), cos, sin)
    k, v = diff_kv(p, kn)
    k = apply_rope(k, cos, sin)
    k_all = jnp.concatenate([k, kc], axis=3)
    v_all = jnp.concatenate([v, vc], axis=2)
    nb = n // Q_BLOCK
    qb = jnp.moveaxis(q.reshape(b, N_DIFF, 2, nb, Q_BLOCK, DIFF_QK), 3, 0)
    o = lax.map(lambda qq: diff_core(qq, k_all, v_all, lam), qb)
    o = jnp.moveaxis(o, 0, 2).reshape(b, N_DIFF, n, DIFF_V)
    return diff_finish(o, subln, lam_init)


def na_q(p, qn):
    b, n, _ = p.shape
    return rms_norm(p[..., :GROUP_W].reshape(b, n, N_NA, HEAD_DIM), qn).transpose(0, 2, 1, 3)


def na_kv(p, kn):
    b, n, _ = p.shape
    k = rms_norm(p[..., GROUP_W:2 * GROUP_W].reshape(b, n, N_NA, HEAD_DIM), kn)
    v = p[..., 2 * GROUP_W:].reshape(b, n, N_NA, HEAD_DIM)
    return k.transpose(0, 2, 1, 3), v.transpose(0, 2, 1, 3)


def na_attention_latent(p, kc, vc, qn, kn, rpb):
    b, n, _ = p.shape
    rows = n // GRID_W
    kh = min(NA_KH_MAX, rows)
    n_loc = kh * NA_KW
    q = na_q(p, qn)
    k, v = na_kv(p, kn)
    r = jnp.arange(rows)
    r0 = jnp.clip(r - kh // 2, 0, rows - kh)
    row_off = r0[:, None] + jnp.arange(kh)[None, :] - r[:, None] + NA_KH_MAX - 1
    j = jnp.arange(GRID_W)
    c0 = jnp.clip(j - NA_KW // 2, 0, GRID_W - NA_KW)
    col_idx = c0[:, None] + jnp.arange(NA_KW)[None, :]
    col_off = col_idx - j[:, None] + NA_KW - 1
    rpb_c = rpb[:, :, col_off]
    kg = k.reshape(b, N_NA, rows, GRID_W, HEAD_DIM)
    vg = v.reshape(b, N_NA, rows, GRID_W, HEAD_DIM)
    scale = HEAD_DIM ** -0.5

    def row_block(args):
        qr, start, roff = args
        kr = lax.dynamic_slice_in_dim(kg, start, kh, axis=2)[:, :, :, col_idx]
        vr = lax.dynamic_slice_in_dim(vg, start, kh, axis=2)[:, :, :, col_idx]
        bias = jnp.transpose(rpb_c[:, roff], (0, 2, 1, 3)).astype(jnp.float32)
        s_loc = jnp.einsum('bhjd,bhajkd->bhjak', qr, kr).astype(jnp.float32) * scale + bias
        s_ctx = jnp.einsum('bhjd,bhcd->bhjc', qr, kc).astype(jnp.float32) * scale
        s = jnp.concatenate([s_loc.reshape(b, N_NA, GRID_W, n_loc), s_ctx], axis=-1)
        a = jax.nn.softmax(s, axis=-1).astype(vr.dtype)
        a_loc = a[..., :n_loc].reshape(b, N_NA, GRID_W, kh, NA_KW)
        return (jnp.einsum('bhjak,bhajkd->bhjd', a_loc, vr)
                + jnp.einsum('bhjc,bhcd->bhjd', a[..., n_loc:], vc))

    qg = jnp.moveaxis(q.reshape(b, N_NA, rows, GRID_W, HEAD_DIM), 2, 0)
    o = lax.map(row_block, (qg, r0, row_off))
    o = jnp.moveaxis(o, 0, 2).reshape(b, N_NA, n, HEAD_DIM)
    return o.transpose(0, 2, 1, 3).reshape(b, n, GROUP_W)


def na_attention_context(p, kc, vc, qn):
    b, n, _ = p.shape
    o = dense_attention(na_q(p, qn), kc, vc)
    return o.transpose(0, 2, 1, 3).reshape(b, n, GROUP_W)


def chunk_gmlp(p, gn, ws, bs):
    b, n, _ = p.shape
    u, v = jnp.split(jax.nn.gelu(p), 2, axis=-1)
    v = rms_norm(v.reshape(b, n // CHUNK, CHUNK, N_GMLP, HEAD_DIM), gn.reshape(N_GMLP, HEAD_DIM))
    s = jnp.einsum('gpq,bnqgc->bnpgc', ws, v) + jnp.swapaxes(bs, 0, 1)[:, :, None]
    return u * s.reshape(b, n, GROUP_W)


def conv_ffn(h, w_up, w_conv, b_conv, w_down):
    z = h @ w_up
    ch = z.shape[-1]
    z = lax.conv_general_dilated(z, w_conv[:, None, :].astype(z.dtype), window_strides=(1,),
                                 padding=((CONV_W // 2, CONV_W // 2),),
                                 dimension_numbers=('NWC', 'WIO', 'NWC'),
                                 feature_group_count=ch) + b_conv
    g, v = jnp.split(z, 2, axis=-1)
    return (jax.nn.silu(g) * v) @ w_down


def setup_inputs(seed: int = 0) -> dict:
    key = jax.random.key(seed)
    ks = jax.random.split(key, 24)
    f32 = jnp.float32

    def nrm(k, shape, s):
        return jax.random.normal(k, shape, f32) * s

    def gain(k, shape):
        return 1.0 + 0.02 * jax.random.normal(k, shape, f32)

    return {
        "x": nrm(ks[0], (BATCH, SEQ, D_MODEL), 1.0),
        "c": nrm(ks[1], (BATCH, D_MODEL), 1.0),
        "ctx": nrm(ks[2], (BATCH, CTX_LEN, D_MODEL), 1.0),
        "c_ctx": nrm(ks[3], (D_MODEL,), 1.0),
        "w_ada": nrm(ks[4], (DEPTH, D_MODEL, 6 * D_MODEL), 0.5 * D_MODEL ** -0.5),
        "b_ada": nrm(ks[5], (DEPTH, 6 * D_MODEL), 0.02),
        "g_mix": gain(ks[6], (DEPTH, D_MODEL)),
        "g_ffn": gain(ks[7], (DEPTH, D_MODEL)),
        "w_in": nrm(ks[8], (DEPTH, D_MODEL, IN_W), D_MODEL ** -0.5),
        "w_out": nrm(ks[9], (DEPTH, MIX_W, D_MODEL), MIX_W ** -0.5),
        "diff_qn": gain(ks[10], (DEPTH, DIFF_QK)),
        "diff_kn": gain(ks[11], (DEPTH, DIFF_QK)),
        "diff_lam": nrm(ks[12], (DEPTH, 4, DIFF_QK), 0.1),
        "diff_subln": gain(ks[13], (DEPTH, DIFF_V)),
        "na_qn": gain(ks[14], (DEPTH, HEAD_DIM)),
        "na_kn": gain(ks[15], (DEPTH, HEAD_DIM)),
        "na_rpb": nrm(ks[16], (DEPTH, N_NA, 2 * NA_KH_MAX - 1, 2 * NA_KW - 1), 0.02),
        "gmlp_norm": gain(ks[17], (DEPTH, GROUP_W)),
        "gmlp_ws": nrm(ks[18], (DEPTH, N_GMLP, CHUNK, CHUNK), CHUNK ** -0.5),
        "gmlp_b": nrm(ks[19], (DEPTH, N_GMLP, CHUNK), 0.02),
        "ffn_up": nrm(ks[20], (DEPTH, D_MODEL, 2 * D_FF), D_MODEL ** -0.5),
        "ffn_conv": nrm(ks[21], (DEPTH, CONV_W, 2 * D_FF), CONV_W ** -0.5),
        "ffn_conv_b": nrm(ks[22], (DEPTH, 2 * D_FF), 0.02),
        "ffn_down": nrm(ks[23], (DEPTH, D_FF, D_MODEL), D_FF ** -0.5),
    }


def reference(x, c, ctx, c_ctx, w_ada, b_ada, g_mix, g_ffn, w_in, w_out, diff_qn, diff_kn, diff_lam,
              diff_subln, na_qn, na_kn, na_rpb, gmlp_norm, gmlp_ws, gmlp_b, ffn_up, ffn_conv, ffn_conv_b,
              ffn_down):
    n = x.shape[1]
    G = GROUP_W
    cos, sin = axial_rope_tables(n, DIFF_QK, x.dtype)
    s_lat = jax.nn.silu(c)[:, None, :]
    s_ctx = jax.nn.silu(c_ctx)
    for li in range(DEPTH):
        ctx_out = li < DEPTH - 1
        lam_init = 0.8 - 0.6 * math.exp(-0.3 * li)
        lf = diff_lam[li].astype(jnp.float32)
        lam = jnp.exp(jnp.sum(lf[0] * lf[1])) - jnp.exp(jnp.sum(lf[2] * lf[3])) + lam_init
        m_lat = jnp.split(s_lat @ w_ada[li] + b_ada[li], 6, axis=-1)
        m_ctx = jnp.split(s_ctx @ w_ada[li] + b_ada[li], 6, axis=-1)

        h = rms_norm(x, g_mix[li]) * (1.0 + m_lat[1]) + m_lat[0]
        hc = rms_norm(ctx, g_mix[li]) * (1.0 + m_ctx[1]) + m_ctx[0]
        p = h @ w_in[li]
        pc = hc @ w_in[li]
        kc_d, vc_d = diff_kv(pc[..., G:4 * G], diff_kn[li])
        kc_n, vc_n = na_kv(pc[..., 4 * G:7 * G], na_kn[li])
        y = jnp.concatenate([
            fourier_mix(p[..., :G]),
            diff_attention_latent(p[..., G:4 * G], kc_d, vc_d, diff_qn[li], diff_kn[li], lam,
                                  diff_subln[li], lam_init, cos, sin),
            na_attention_latent(p[..., 4 * G:7 * G], kc_n, vc_n, na_qn[li], na_kn[li], na_rpb[li]),
            chunk_gmlp(p[..., 7 * G:], gmlp_norm[li], gmlp_ws[li], gmlp_b[li]),
        ], axis=-1)
        x = x + m_lat[2] * (y @ w_out[li])
        if ctx_out:
            yc = jnp.concatenate([
                fourier_mix(pc[..., :G]),
                diff_finish(diff_core(diff_q(pc[..., G:4 * G], diff_qn[li]), kc_d, vc_d, lam),
                            diff_subln[li], lam_init),
                na_attention_context(pc[..., 4 * G:7 * G], kc_n, vc_n, na_qn[li]),
                chunk_gmlp(pc[..., 7 * G:], gmlp_norm[li], gmlp_ws[li], gmlp_b[li]),
            ], axis=-1)
            ctx = ctx + m_ctx[2] * (yc @ w_out[li])

        h = rms_norm(x, g_ffn[li]) * (1.0 + m_lat[4]) + m_lat[3]
        x = x + m_lat[5] * conv_ffn(h, ffn_up[li], ffn_conv[li], ffn_conv_b[li], ffn_down[li])
        if ctx_out:
            hc = rms_norm(ctx, g_ffn[li]) * (1.0 + m_ctx[4]) + m_ctx[3]
            ctx = ctx + m_ctx[5] * conv_ffn(hc, ffn_up[li], ffn_conv[li], ffn_conv_b[li], ffn_down[li])
    return x
```

```python
import math
from contextlib import ExitStack
import numpy as np
import ml_dtypes
import concourse.bass as bass
import concourse.mybir as mybir
from concourse.bass_utils import run_bass_kernel_spmd

F32 = mybir.dt.float32
BF16 = mybir.dt.bfloat16
AF = mybir.ActivationFunctionType
ALU = mybir.AluOpType
AX = mybir.AxisListType

DEPTH = 4
FFN_VARIANT = 0
SKIPW = 0
NA_WARM = 0
PRECISE_FENCE = False
PE_WARM = 1
NA_ACT_RECIP = True
EVAC_POOL = 3
SES_ENGINES = ("act", "dve", "pool")
VCOPIES = 4
NT = 2304
EPS = 1e-6
NS = 888
O_BADA, O_GMIX, O_GFFN = 0, 48, 56
O_DQN, O_DKN, O_SUBLN, O_NQN, O_NKN, O_LAMI, O_OMLI = 64, 65, 66, 67, 68, 69, 70
O_LAM, O_GN, O_BST, O_CW, O_CB = 72, 200, 456, 712, 844
C_ONESD, C_B32, C_B64, C_PROT, C_CS64, C_C256, C_S256, NCB = 0, 128, 256, 384, 512, 1536, 2048, 2560
FFN_PASSES = [(0, 4), (4, 4), (8, 4), (12, 4), (16, 4), (20, 2)]


class Sched:
    ENGS = ("pe", "act", "dve", "pool", "sp")

    def __init__(self, nc, stack):
        self.nc, self.stack = nc, stack
        self.ops = {e: [] for e in self.ENGS}
        self.cnt = {}
        self.known = {e: {} for e in self.ENGS}
        self.last_w = {}
        self.readers = {}
        self.sems = {}
        self.fence_vals = {}
        self.stopped = False
        self.owner = {}
        self.scopes = []

    def open_scope(self):
        self.scopes.append({"keys": set(), "max": {}})

    def close_scope(self):
        sc = self.scopes.pop()
        for sk, v in sc["max"].items():
            if self.fence_vals.get(sk, 0) < v:
                self.fence_vals[sk] = v
        for k in sc["keys"]:
            self.owner.pop(k, None)
            self.last_w.pop(k, None)
            self.readers.pop(k, None)
        if self.scopes:
            up = self.scopes[-1]["max"]
            for sk, v in sc["max"].items():
                if up.get(sk, 0) < v:
                    up[sk] = v

    def sem(self, key):
        if key not in self.sems:
            self.sems[key] = self.stack.enter_context(self.nc.semaphore("s%d" % len(self.sems)))
        return self.sems[key]

    def op(self, eng, fn, reads=(), writes=(), dma=None):
        if self.stopped:
            return
        waits = {}

        def need(sk, val):
            if sk == ("eng", eng) and eng not in SES_ENGINES:
                return
            if self.known[eng].get(sk, 0) >= val:
                return
            if waits.get(sk, 0) < val:
                waits[sk] = val

        local = []
        for k in list(reads) + list(writes):
            if k not in self.owner:
                self.owner[k] = self.scopes[-1] if self.scopes else None
                if self.scopes:
                    self.scopes[-1]["keys"].add(k)
            if self.owner[k] is not None:
                local.append(self.owner[k])
        if local or not PRECISE_FENCE:
            for sk, v in self.fence_vals.items():
                need(sk, v)
        for k in reads:
            if k in self.last_w:
                need(*self.last_w[k])
        for k in writes:
            if k in self.last_w:
                need(*self.last_w[k])
            for sk, v in self.readers.get(k, {}).items():
                need(sk, v)
        for sk, v in waits.items():
            self.known[eng][sk] = v
        sk = ("dma", dma) if dma is not None else ("eng", eng)
        self.sem(sk)
        inc = 16 if dma is not None else 1
        self.cnt[sk] = self.cnt.get(sk, 0) + inc
        val = self.cnt[sk]
        for k in reads:
            d = self.readers.setdefault(k, {})
            d[sk] = max(d.get(sk, 0), val)
        for k in writes:
            self.last_w[k] = (sk, val)
            self.readers[k] = {}
        if PRECISE_FENCE:
            for scd in local:
                scd["max"][sk] = val
        else:
            for scd in self.scopes:
                scd["max"][sk] = val
        self.ops[eng].append((sorted(waits.items()), fn, sk, inc))

    def finish(self):
        self.ops["sp"].append((sorted(self.cnt.items()), None, None, 0))

    def emit(self, block):
        engs = {"pe": "tensor", "act": "scalar", "dve": "vector", "pool": "gpsimd", "sp": "sync"}

        def mk(ename):
            def body(e):
                for waits, fn, sk, inc in self.ops[ename]:
                    for wk, v in waits:
                        e.wait_ge(self.sems[wk], v)
                    if fn is not None:
                        fn(e).then_inc(self.sems[sk], inc)
            return body

        for ename, attr in engs.items():
            if self.ops[ename]:
                getattr(block, attr)(mk(ename))


def cols(tb):
    return (tb * 512, 512) if tb < 4 else (2048, 256)


def na_local(m):
    if 2 <= m <= 13:
        return m - 2, 5, 0
    e = {0: 0, 1: 1, 14: 2, 15: 3}[m]
    return (0 if m < 2 else 12), 4, 5 + 4 * e


class _Stop(Exception):
    pass


def build(n_layers, ctx_last=False, dbg=None, stop=None):
    L = n_layers
    nc = bass.Bass("TRN2", target_bir_lowering=False)
    D = lambda name, shape, dt, kind="ExternalInput": nc.dram_tensor(name, shape, dt, kind=kind).ap()
    xT_d = D("xT", [1024, 2048], F32)
    cxT_d = D("cxT", [1024, 256], F32)
    cpk_d = D("cpk", [128, 16], F32)
    cb_d = D("cbf", [128, NCB], BF16)
    rope_d = D("rope", [128, 2, 2048], F32)
    CN_d = D("CN", [2048, 2048], BF16)
    SN_d = D("SN", [2048, 2048], BF16)
    sp_d = D("spk", [L, 128, NS], F32)
    ws_d = D("wsT", [L, 128, 512], F32)
    mb_d = D("mb", [L, 128, 4 * 21 * 128], F32)
    wada_d = D("w_ada", [L, 1024, 6144], F32)
    win_d = D("w_in", [L, 1024, 2304], F32)
    wout_d = D("w_out", [L, 1024, 1024], F32)
    wup_d = D("w_up", [L, 1024, 5632], F32)
    wdn_d = D("w_dn", [L, 2816, 1024], F32)
    out_d = D("outT", [1024, 2048], F32, "ExternalOutput")
    cout_d = D("coutT", [1024, 256], F32, "ExternalOutput") if ctx_last else None
    dbg_outs = {}

    with ExitStack() as st:
        S = Sched(nc, st)

        tcnt = [0]

        class Scope(ExitStack):
            def __enter__(self):
                S.open_scope()
                return super().__enter__()

            def __exit__(self, *a):
                r = super().__exit__(*a)
                S.close_scope()
                return r

        def T(stack, name, shape, dt):
            tcnt[0] += 1
            return stack.enter_context(nc.sbuf_tensor("sb%d_%s" % (tcnt[0], name), shape, dt))

        PS = st.enter_context(nc.psum_tensor("PS", [128, 8, 512], F32))
        PSF = PS[:, :, :].rearrange("p b n -> p (b n)")
        xT = T(st, "xT", [128, 8, NT], F32)
        hT = T(st, "hT", [128, 8, NT], BF16)
        CB = T(st, "CB", [128, NCB], BF16)
        SPK = T(st, "SPK", [128, 1, NS], F32)
        MODS = T(st, "MOD", [128, 2, 48, 2], F32)
        AMS = T(st, "AM", [128, 2, 2, 8, 2], F32)
        ADAP = T(st, "ADAP", [128, 2, 64], F32)
        cpk = T(st, "cpk", [128, 8, 2], F32)
        sT = T(st, "sT", [128, 8, 2], BF16)
        lamt = T(st, "lamt", [128, 8], F32)

        bank_rr = [0]
        ov_early = out_d.rearrange("(c p) n -> p c n", p=128)
        stored_early = set()

        def bank(allowed=(0, 1, 2, 3, 4, 5, 6, 7)):
            bank_rr[0] += 1
            return allowed[bank_rr[0] % len(allowed)]

        def mm(out_ap, pairs, reads, writes, tile_position=None):
            def fn(e, pairs=pairs, out_ap=out_ap):
                n = len(pairs)
                for i, (l, r) in enumerate(pairs):
                    kw = {} if tile_position is None else {"tile_position": tile_position}
                    inst = e.matmul(out_ap, lhsT=l, rhs=r, start=(i == 0), stop=(i == n - 1), **kw)
                return inst
            S.op("pe", fn, reads, writes)

        def mm_multi(groups, reads, writes):
            def fn(e, groups=groups):
                for out_ap, pairs, tp in groups:
                    n = len(pairs)
                    for i, (l, r) in enumerate(pairs):
                        kw = {} if tp is None else {"tile_position": tp}
                        inst = e.matmul(out_ap, lhsT=l, rhs=r, start=(i == 0), stop=(i == n - 1), **kw)
                return inst
            S.op("pe", fn, reads, writes)

        def act(out, in_, func, reads, writes, **kw):
            S.op("act", lambda e: e.activation(out=out, in_=in_, func=func, **kw), reads, writes)

        def dma(eng, out, in_, reads, writes, sem):
            S.op(eng, lambda e: e.dma_start(out=out, in_=in_), reads, writes, dma=sem)

        def tt(eng, out, in0, in1, op, reads, writes):
            S.op(eng, lambda e: e.tensor_tensor(out=out, in0=in0, in1=in1, op=op), reads, writes)

        def stt(out, in0, scalar, in1, op0, op1, reads, writes):
            S.op("dve", lambda e: e.scalar_tensor_tensor(out=out, in0=in0, scalar=scalar, in1=in1, op0=op0, op1=op1),
                 reads, writes)

        def ts(eng, out, in0, s1, s2, op0, op1, reads, writes):
            if op1 is None:
                S.op(eng, lambda e: e.tensor_scalar(out=out, in0=in0, scalar1=s1, scalar2=None, op0=op0), reads, writes)
            else:
                S.op(eng, lambda e: e.tensor_scalar(out=out, in0=in0, scalar1=s1, scalar2=s2, op0=op0, op1=op1), reads, writes)

        def copy(eng, out, in_, reads, writes):
            if eng == "act":
                S.op("act", lambda e: e.activation(out=out, in_=in_, func=AF.Identity), reads, writes)
            else:
                S.op(eng, lambda e: e.tensor_copy(out=out, in_=in_), reads, writes)

        def recip(out, in_, reads, writes):
            S.op("dve", lambda e: e.reciprocal(out=out, in_=in_), reads, writes)

        def dbg_tap(name, ap, shape, dt, reads):
            if dbg is None or name not in dbg or S.stopped:
                return
            d = D("dbg_" + name, list(shape), dt, "ExternalOutput")
            dbg_outs[name] = d
            dma("sp", d, ap, reads, [], "dbg_" + name)

        ev_rr = [0]

        def ev_eng():
            ev_rr[0] += 1
            return "act" if ev_rr[0] % 2 else "dve"

        def xkeys(tb):
            return [("x", c, tb) for c in range(8)]

        def hkeys(tb):
            return [("h", c, tb) for c in range(8)]

        def tt_hkeys(t):
            return hkeys(t // 4 if t < 16 else 4)

        dma("sp", CB[:, :], cb_d[:, :], [], ["cb"], "cb")
        dma("sp", cpk[:, :, :].rearrange("p c v -> p (c v)"), cpk_d[:, :], [], ["cpk"], "cpk")
        xv = xT_d.rearrange("(c p) n -> p c n", p=128)
        cv = cxT_d.rearrange("(c p) n -> p c n", p=128)
        for c in range(8):
            dma("sp", xT[:, c, 0:2048], xv[:, c, :], [], [("x", c, tb) for tb in range(4)], "xin%d" % c)
        dma("sp", xT[:, :, 2048:2304], cv[:, :, :], [], xkeys(4), "cin")
        act(sT[:, :, :], cpk[:, :, :], AF.Silu, ["cpk"], ["sT"])
        ONESD = CB[:, C_ONESD:C_ONESD + 128]
        B32 = CB[:, C_B32:C_B32 + 128]
        B64 = CB[:, C_B64:C_B64 + 128]
        PROT = CB[:, C_PROT:C_PROT + 128]

        def checkpoint(k):
            if stop is not None and stop == k:
                S.stopped = True

        for li in range(L):
          try:
              ctx_out = (li < L - 1) or ctx_last
              tbs_all = [0, 1, 2, 3, 4]
              tbs_o = tbs_all if ctx_out else [0, 1, 2, 3]
              ntt_o = 18 if ctx_out else 16
              sp = SPK[:, 0, :]

              def spc(o, n=1, sp=sp):
                  return sp[:, o:o + n]
              dma("sp", SPK[:, 0, :], sp_d[li, :, :], [], [("sp", 0)], "sp0")
              SPR = [("sp", 0)]

              MOD = MODS[:, li % 2]
              AM = AMS[:, li % 2]
              MODK = ("mod", li % 2)
              AMK = ("am", li % 2)

              def ada_load(lt, jg, wa):
                  wav = wada_d[lt].rearrange("(c p) n -> p c n", p=128)
                  dma("pool", wa[jg % len(wa)][:, :, :], wav[:, :, jg * 512:(jg + 1) * 512], [], [("wa", jg % len(wa))], "wa%d" % (jg % len(wa)))

              def ada_params(lt):
                  dma("sp", ADAP[:, lt % 2, :], sp_d[lt, :, 0:64], [], [("adap", lt % 2)], "adap%d" % (lt % 2))

              def ada_piece(lt, jg, wa, bk):
                  wb = wa[jg % len(wa)]
                  groups = []
                  for j4 in range(4):
                      groups.append((PS[:, bk, j4 * 2:j4 * 2 + 2],
                                     [(wb[:, c, j4 * 128:(j4 + 1) * 128], sT[:, c, :]) for c in range(8)], None))
                  mm_multi(groups, [("wa", jg % len(wa)), "sT"], [("pb", bk)])
                  psm = PS[:, bk, 0:8].rearrange("p (j v) -> p j v", v=2)
                  for v in range(2):
                      tt("dve", MODS[:, lt % 2, jg * 4:(jg + 1) * 4, v], psm[:, :, v], ADAP[:, lt % 2, O_BADA + jg * 4:O_BADA + (jg + 1) * 4], ALU.add,
                         [("pb", bk), ("adap", lt % 2)], [("mod", lt % 2)])

              def ada_finish(lt):
                  for which, (so, go) in enumerate(((8, O_GMIX), (32, O_GFFN))):
                      for v in range(2):
                          stt(AMS[:, lt % 2, which, :, v], MODS[:, lt % 2, so:so + 8, v], 1.0, ADAP[:, lt % 2, go:go + 8], ALU.add, ALU.mult,
                              [("mod", lt % 2), ("adap", lt % 2)], [("am", lt % 2)])

              if li == 0:
                  with Scope() as sc:
                      wa0 = [T(sc, "wa%d" % i, [128, 8, 512], BF16) for i in range(2)]
                      ada_params(0)
                      for jg in range(12):
                          ada_load(0, jg, wa0)
                          ada_piece(0, jg, wa0, 7)
                      ada_finish(0)
              checkpoint(1)
              dbg_tap("mod%d" % li, MOD[:, :, :], [128, 48, 2], F32, [MODK])

              def shift_ap(which, c, v):
                  s = (0, 24)[which] + c
                  return MOD[:, s, v:v + 1]

              def gate_ap(which, c, v):
                  s = (16, 40)[which] + c
                  return MOD[:, s, v:v + 1]

              def norm_phase(which, tbs):
                  with Scope() as sc:
                      sq = [T(sc, "sq%d" % i, [128, 8, 512], BF16) for i in range(2)]
                      rstd = [T(sc, "rstd%d" % i, [128, 512], F32) for i in range(2)]
                      tmp = [T(sc, "ntmp%d" % i, [128, 512], F32) for i in range(3)]

                      def n_a(k):
                          tb = tbs[k]
                          c0, n = cols(tb)
                          q2 = k % 2
                          act(sq[q2][:, :, 0:n], xT[:, :, c0:c0 + n], AF.Square, xkeys(tb), [("sq", q2)])
                          bk = bank()
                          mm(PS[:, bk, 0:n], [(ONESD, sq[q2][:, c, 0:n]) for c in range(8)], [("sq", q2), "cb"], [("pb", bk)])
                          act(rstd[q2][:, 0:n], PS[:, bk, 0:n], AF.Ln, [("pb", bk)], [("rstd", q2)], bias=EPS)
                          act(rstd[q2][:, 0:n], rstd[q2][:, 0:n], AF.Exp, [("rstd", q2)], [("rstd", q2)], scale=-0.5)

                      def n_b(k):
                          tb = tbs[k]
                          c0, n = cols(tb)
                          v = 0 if tb < 4 else 1
                          q2 = k % 2
                          for c in range(8):
                              t = tmp[c % 3]
                              tk = ("ntmp", c % 3)
                              tt("dve", t[:, 0:n], xT[:, c, c0:c0 + n], rstd[q2][:, 0:n], ALU.mult, [("x", c, tb), ("rstd", q2)], [tk])
                              act(hT[:, c, c0:c0 + n], t[:, 0:n], AF.Identity, [tk, AMK, MODK], [("h", c, tb)],
                                  scale=AM[:, which, c, v:v + 1], bias=shift_ap(which, c, v))

                      n_a(0)
                      for k in range(len(tbs)):
                          if k + 1 < len(tbs):
                              n_a(k + 1)
                          n_b(k)

              norm_phase(0, tbs_all)
              checkpoint(2)
              dbg_tap("h1_%d" % li, hT[:, :, :], [128, 8, NT], BF16, [k for tb in range(5) for k in hkeys(tb)])

              winv = win_d[li].rearrange("(c p) n -> p c n", p=128)
              woutv = wout_d[li].rearrange("(c p) n -> p c n", p=128)

              def wout_partial(sc, ym, mix, ykeys_fn):
                  wout_multi(sc, [(ym, ykeys_fn)], mix)

              def wout_multi(sc, yms, mix):
                  nch = 2 * len(yms)
                  wo = T(sc, "wo", [128, nch, 1024], BF16)
                  wtmp = [T(sc, "wtmp%d" % i, [128, 512], F32) for i in range(3)]
                  wev = [0]
                  for hlf in range(2):
                      dma("pool", wo[:, :, hlf * 512:(hlf + 1) * 512], woutv[:, 2 * mix:2 * mix + nch, hlf * 512:(hlf + 1) * 512],
                          [], ["wo"], "wo")
                  for tb in tbs_o:
                      c0, n = cols(tb)
                      v = 0 if tb < 4 else 1
                      for oc in range(8):
                          bk = bank()
                          pairs = []
                          rkeys = ["wo"]
                          for yi, (ym_, kf) in enumerate(yms):
                              for c in range(2):
                                  pairs.append((wo[:, 2 * yi + c, oc * 128:(oc + 1) * 128], ym_[:, c, c0:c0 + n]))
                              rkeys += kf(tb)
                          mm(PS[:, bk, 0:n], pairs, rkeys, [("pb", bk)])
                          wev[0] += 1
                          if EVAC_POOL and wev[0] % EVAC_POOL == 0:
                              ei = (wev[0] // EVAC_POOL) % 3
                              act(wtmp[ei][:, 0:n], PS[:, bk, 0:n], AF.Identity, [("pb", bk), MODK], [("wtmp", ei)], scale=gate_ap(0, oc, v))
                              tt("pool", xT[:, oc, c0:c0 + n], xT[:, oc, c0:c0 + n], wtmp[ei][:, 0:n], ALU.add, [("wtmp", ei), ("x", oc, tb)], [("x", oc, tb)])
                          else:
                              stt(xT[:, oc, c0:c0 + n], PS[:, bk, 0:n], gate_ap(0, oc, v), xT[:, oc, c0:c0 + n], ALU.mult, ALU.add,
                                  [("pb", bk), MODK, ("x", oc, tb)], [("x", oc, tb)])

              def proj_fm(wt, wkey, wcol, tbs, evac):
                  for tb in tbs:
                      c0, n = cols(tb)
                      bk = bank()
                      mm(PS[:, bk, 0:n], [(wt[:, c, wcol:wcol + 128], hT[:, c, c0:c0 + n]) for c in range(8)],
                         [wkey] + hkeys(tb), [("pb", bk)])
                      evac(bk, tb, c0, n)

              scAB = Scope()
              scAB.__enter__()
              ymA_t = T(scAB, "ymA", [128, 2, NT], BF16)
              with Scope() as sc:
                  ym = ymA_t
                  pf = T(sc, "pf", [128, 2, NT], BF16)
                  AB = T(sc, "AB", [128, 18, 512], BF16)
                  wf = T(sc, "wf", [128, 8, 256], BF16)
                  CNb = [T(sc, "CNb%d" % i, [128, 16, 256], BF16) for i in range(2)]
                  SNb = [T(sc, "SNb%d" % i, [128, 16, 256], BF16) for i in range(2)]
                  dma("pool", wf[:, :, :], winv[:, :, 0:256], [], ["wf"], "wf")
                  for oc in range(2):
                      def ev(bk, tb, c0, n, oc=oc):
                          copy(ev_eng(), pf[:, oc, c0:c0 + n], PS[:, bk, 0:n], [("pb", bk)], [("pf", oc, tb)])
                      proj_fm(wf, "wf", oc * 128, tbs_o, ev)
                  for t in range(ntt_o):
                      tb = t // 4 if t < 16 else 4
                      bk = bank()
                      mm(PS[:, bk, :], [(pf[:, c, t * 128:(t + 1) * 128], CB[:, C_CS64 + c * 512:C_CS64 + (c + 1) * 512]) for c in range(2)],
                         [("pf", 0, tb), ("pf", 1, tb), "cb"], [("pb", bk)])
                      copy(ev_eng(), AB[:, t, :], PS[:, bk, :], [("pb", bk)], [("AB", t)])
                  CNv = CN_d.rearrange("(t p) k -> p t k", p=128)
                  SNv = SN_d.rearrange("(t p) k -> p t k", p=128)
                  def ft_load(kb):
                      dma("sp", CNb[kb % 2][:, :, :], CNv[:, :, kb * 256:(kb + 1) * 256], [], [("CNb", kb % 2)], "CNb%d" % (kb % 2))
                      dma("sp", SNb[kb % 2][:, :, :], SNv[:, :, kb * 256:(kb + 1) * 256], [], [("SNb", kb % 2)], "SNb%d" % (kb % 2))
                  for kb in range(8):
                      if kb == 0:
                          ft_load(0)
                          ft_load(1)
                      for lc in range(2):
                          bk = bank()
                          pairs = []
                          for t in range(16):
                              pairs.append((AB[:, t, lc * 128:(lc + 1) * 128], CNb[kb % 2][:, t, :]))
                              pairs.append((AB[:, t, 256 + lc * 128:256 + (lc + 1) * 128], SNb[kb % 2][:, t, :]))
                          mm(PS[:, bk, 0:256], pairs, [("AB", t) for t in range(16)] + [("CNb", kb % 2), ("SNb", kb % 2)], [("pb", bk)])
                          copy(ev_eng(), ym[:, lc, kb * 256:(kb + 1) * 256], PS[:, bk, 0:256], [("pb", bk)], [("ymA", kb // 2)])
                      if kb + 2 < 8:
                          ft_load(kb + 2)
                  if ctx_out:
                      for lc in range(2):
                          bk = bank()
                          pairs = []
                          for t in range(2):
                              pairs.append((AB[:, 16 + t, lc * 128:(lc + 1) * 128], CB[:, C_C256 + t * 256:C_C256 + (t + 1) * 256]))
                              pairs.append((AB[:, 16 + t, 256 + lc * 128:256 + (lc + 1) * 128], CB[:, C_S256 + t * 256:C_S256 + (t + 1) * 256]))
                          mm(PS[:, bk, 0:256], pairs, [("AB", 16), ("AB", 17), "cb"], [("pb", bk)])
                          copy(ev_eng(), ym[:, lc, 2048:2304], PS[:, bk, 0:256], [("pb", bk)], [("ymA", 4)])
                  dbg_tap("yA_%d" % li, ym[:, :, :], [128, 2, NT], BF16, [("ymA", k) for k in range(5)])

              checkpoint(3)
              with Scope() as sc:
                  ym = T(sc, "ymB", [128, 2, NT], BF16)
                  qk = T(sc, "qk", [128, 4, NT], BF16)
                  VA = T(sc, "VA", [128, 18, 4, 128], BF16)
                  S.op("pool", lambda e: e.memset(VA[:, :, :, :].rearrange("p a h d -> p (a h d)"), 1.0), [], ["VAones"])
                  with Scope() as sc2:
                      wqk = T(sc2, "wqk", [128, 8, 512], BF16)
                      wv = T(sc2, "wvB", [128, 8, 256], BF16)
                      ROPE = [T(sc2, "ROPE%d" % i, [128, 2, 512], F32) for i in range(2)]
                      sqb = [T(sc2, "sqb0", [128, 512], BF16)] * 2
                      ub = [T(sc2, "ub%d" % i, [128, 512], BF16) for i in range(2)]
                      rs = [T(sc2, "rsB%d" % i, [128, 512], F32) for i in range(2)]
                      t1 = [T(sc2, "t1B0", [128, 512], F32)] * 2
                      t2 = [T(sc2, "t2B0", [128, 512], F32)] * 2
                      dma("pool", wqk[:, :, :], winv[:, :, 256:768], [], ["wqk"], "wqk")
                      dma("pool", wv[:, :, :], winv[:, :, 768:1024], [], ["wvB"], "wvB")
                      it = 0
                      blocks = []
                      for tb in tbs_all:
                          for oi in range(4):
                              if oi < 2 and tb == 4 and not ctx_out:
                                  continue
                              blocks.append((tb, oi))
                      rope_loaded = set()
                      pbank = {}

                      def qk_a(bi):
                          tb, oi = blocks[bi]
                          c0, n = cols(tb)
                          i2 = bi % 2
                          rp = ROPE[tb % 2]
                          rkk = ("rope", tb % 2)
                          if tb < 4 and tb not in rope_loaded:
                              rope_loaded.add(tb)
                              dma("sp", rp[:, :, :], rope_d[:, :, c0:c0 + n], [], [rkk], "rope%d" % (tb % 2))
                          gcol = O_DQN if oi < 2 else O_DKN
                          bk = bank()
                          mm(PS[:, bk, 0:n], [(wqk[:, c, oi * 128:(oi + 1) * 128], hT[:, c, c0:c0 + n]) for c in range(8)],
                             ["wqk"] + hkeys(tb), [("pb", bk)])
                          act(sqb[i2][:, 0:n], PS[:, bk, 0:n], AF.Square, [("pb", bk)], [("sqb", 0)])
                          act(ub[i2][:, 0:n], PS[:, bk, 0:n], AF.Identity, [("pb", bk)] + SPR, [("ub", i2)], scale=spc(gcol))
                          bk2 = bank()
                          mm(PS[:, bk2, 0:n], [(B32, sqb[i2][:, 0:n])], [("sqb", 0), "cb"], [("pb", bk2)])
                          if tb < 4:
                              bk3 = bank()
                              pbank[bi] = bk3
                              mm(PS[:, bk3, 0:n], [(PROT, ub[i2][:, 0:n])], [("ub", i2), "cb"], [("pb", bk3)])
                          act(rs[i2][:, 0:n], PS[:, bk2, 0:n], AF.Ln, [("pb", bk2)], [("rsB", i2)], bias=EPS)
                          act(rs[i2][:, 0:n], rs[i2][:, 0:n], AF.Exp, [("rsB", i2)], [("rsB", i2)], scale=-0.5)

                      def qk_b(bi):
                          tb, oi = blocks[bi]
                          c0, n = cols(tb)
                          i2 = bi % 2
                          rp = ROPE[tb % 2]
                          rkk = ("rope", tb % 2)
                          if tb < 4:
                              bk3 = pbank[bi]
                              tt("dve", t1[i2][:, 0:n], ub[i2][:, 0:n], rp[:, 0, 0:n], ALU.mult, [("ub", i2), rkk], [("t1B", 0)])
                              tt("dve", t2[i2][:, 0:n], PS[:, bk3, 0:n], rp[:, 1, 0:n], ALU.mult, [("pb", bk3), rkk], [("t2B", 0)])
                              tt("pool", t1[i2][:, 0:n], t1[i2][:, 0:n], t2[i2][:, 0:n], ALU.add, [("t1B", 0), ("t2B", 0)], [("t1B", 0)])
                              tt("dve", qk[:, oi, c0:c0 + n], t1[i2][:, 0:n], rs[i2][:, 0:n], ALU.mult, [("t1B", 0), ("rsB", i2)], [("qk", oi, tb)])
                          else:
                              tt("dve", qk[:, oi, c0:c0 + n], ub[i2][:, 0:n], rs[i2][:, 0:n], ALU.mult, [("ub", i2), ("rsB", i2)], [("qk", oi, tb)])

                      qk_a(0)
                      for bi in range(len(blocks)):
                          if bi + 1 < len(blocks):
                              qk_a(bi + 1)
                          qk_b(bi)
                      checkpoint(3.1)
                      for t in range(18):
                          bk = bank()
                          mm(PS[:, bk, 0:256], [(hT[:, c, t * 128:(t + 1) * 128], wv[:, c, :]) for c in range(8)],
                             ["wvB"] + tt_hkeys(t), [("pb", bk)])
                          for hh in range(VCOPIES):
                              off = 0 if hh % 2 == 0 else 64
                              copy("dve", VA[:, t, hh, off:off + 64], PS[:, bk, hh * 64:(hh + 1) * 64],
                                   [("pb", bk), "VAones"], [("VA", t, 0)])
                  dbg_tap("qkB_%d" % li, qk[:, :, :], [128, 4, NT], BF16, [("qk", oi, tb) for oi in range(4) for tb in range(5) if not (oi < 2 and tb == 4 and not ctx_out)])
                  checkpoint(3.2)
                  with Scope() as sc2:
                      lp = T(sc2, "lp", [128, 64], F32)
                      tt("dve", lp[:, 0:32], spc(O_LAM, 32), spc(O_LAM + 32, 32), ALU.mult, SPR, ["lp"])
                      tt("dve", lp[:, 32:64], spc(O_LAM + 64, 32), spc(O_LAM + 96, 32), ALU.mult, SPR, ["lp"])
                      S.op("dve", lambda e: e.tensor_reduce(out=lamt[:, 0:2], in_=lp[:, :].rearrange("p (a b) -> p a b", b=32), axis=AX.X, op=ALU.add),
                           ["lp"], ["lamt"])
                      act(lamt[:, 2:4], lamt[:, 0:2], AF.Exp, ["lamt"], ["lamt"])
                      tt("dve", lamt[:, 4:5], lamt[:, 2:3], lamt[:, 3:4], ALU.subtract, ["lamt"], ["lamt"])
                      tt("dve", lamt[:, 5:6], lamt[:, 4:5], spc(O_LAMI), ALU.add, ["lamt"] + SPR, ["lamt"])
                      ts("dve", lamt[:, 6:7], lamt[:, 5:6], -1.0, None, ALU.mult, None, ["lamt"], ["lamt"])
                  NEGLAM = lamt[:, 6:7]
                  checkpoint(3.3)
                  with Scope() as sc2:
                      Eb = [T(sc2, "Eb%d" % i, [128, 2, 512], BF16) for i in range(3)]
                      R = [T(sc2, "Rr%d" % i, [128, 512], F32) for i in range(2)]
                      o1 = T(sc2, "o1", [128, 512], F32)
                      o2 = T(sc2, "o2", [128, 512], F32)
                      OP = T(sc2, "OP", [128, 512], F32)
                      sqo = T(sc2, "sqo", [128, 512], BF16)
                      rso = T(sc2, "rso", [128, 512], F32)
                      SCALE = 32.0 ** -0.5
                      steps = []
                      pairs_l = [(qb, hp, 18 if qb < 4 else 2) for qb in tbs_o for hp in range(2)]
                      for pj, (qb, hp, nk_) in enumerate(pairs_l):
                          kcs = list(range(18)) if qb < 4 else [16, 17]
                          for hl in range(2):
                              for ki, kc in enumerate(kcs):
                                  steps.append(("kv", qb, hp, hl, ki, kc, len(kcs)))
                          nxt = pairs_l[pj + 1][2] if pj + 1 < len(pairs_l) else 0
                          steps.append(("subln", qb, hp, 10 if nxt == 18 else 0))
                      order = []
                      pending = []
                      for st_ in steps:
                          pending = [(d - 1, x) for d, x in pending]
                          if st_[0] == "subln":
                              pending.append((st_[3], st_))
                          else:
                              order.append(st_)
                          for d, x in list(pending):
                              if d <= 0:
                                  order.append(x)
                                  pending.remove((d, x))
                      order += [x for _, x in pending]
                      steps = order
                      wa_n = None
                      if li + 1 < L:
                          wa_n = [T(sc2, "wan0", [128, 8, 512], BF16)]
                          ada_params(li + 1)
                          ada_load(li + 1, 0, wa_n)
                          order = []
                          nkv = 0
                          jg_n = 0
                          for st_ in steps:
                              order.append(st_)
                              if st_[0] == "kv":
                                  nkv += 1
                                  if nkv % 22 == 0 and jg_n < 12:
                                      order.append(("ada", jg_n))
                                      jg_n += 1
                          while jg_n < 12:
                              order.append(("ada", jg_n))
                              jg_n += 1
                          steps = order

                      skip_warm = set()
                      for i_, st_ in enumerate(steps):
                          if st_[0] == "ada":
                              skip_warm.update(range(i_ + 1, i_ + 1 + SKIPW))

                      def stage1(i):
                          stp = steps[i]
                          sb = (0, 2)[i % 2]
                          if stp[0] == "kv":
                              _, qb, hp, hl, ki, kc, nk = stp
                              q0, qn = cols(qb)
                              pb = hl * 64
                              eb = Eb[i % 3]
                              ek = ("Eb", i % 3)
                              ktb = kc // 4 if kc < 16 else 4
                              groups = []
                              for c2 in range(2):
                                  p0 = pb + 32 * c2
                                  groups.append((PS[:, sb + c2, 0:qn],
                                                 [(qk[p0:p0 + 32, 2 + hp, kc * 128:(kc + 1) * 128], qk[p0:p0 + 32, hp, q0:q0 + qn])],
                                                 (p0, 0)))
                              if PE_WARM and i not in skip_warm:
                                  groups = [(PS[:, sb + c2, 0:512], [(VA[:, 0, 0, :], qk[:, 0, 0:512])], None) for c2 in range(PE_WARM)] + groups
                              mm_multi(groups, [("qk", 2 + hp, ktb), ("qk", hp, qb)], [("pb", sb), ("pb", sb + 1)])
                              act(eb[:, :, 0:qn], PS[:, sb:sb + 2, 0:qn], AF.Exp, [("pb", sb), ("pb", sb + 1)], [ek], scale=SCALE)
                      def stage2(i):
                          stp = steps[i]
                          sb = (0, 2)[i % 2]
                          if stp[0] == "ada":
                              jg = stp[1]
                              ada_piece(li + 1, jg, wa_n, sb)
                              if jg + 1 < 12:
                                  ada_load(li + 1, jg + 1, wa_n)
                              if jg == 11:
                                  ada_finish(li + 1)
                              return
                          if stp[0] != "kv":
                              _, qb, hp, _d = stp
                              q0, qn = cols(qb)
                              act(sqo[:, 0:qn], OP[:, 0:qn], AF.Square, [("OP", 0), ("OP", 1)], ["sqo"])
                              mm(PS[:, sb, 0:qn], [(B64, sqo[:, 0:qn])], ["sqo", "cb"], [("pb", sb)])
                              act(rso[:, 0:qn], PS[:, sb, 0:qn], AF.Ln, [("pb", sb)], ["rso"], bias=EPS)
                              act(rso[:, 0:qn], rso[:, 0:qn], AF.Exp, ["rso"], ["rso"], scale=-0.5)
                              stt(OP[:, 0:qn], OP[:, 0:qn], spc(O_OMLI), rso[:, 0:qn], ALU.mult, ALU.mult,
                                  [("OP", 0), ("OP", 1), "rso"] + SPR, [("OP", 0), ("OP", 1)])
                              act(ym[:, hp, q0:q0 + qn], OP[:, 0:qn], AF.Identity, [("OP", 0), ("OP", 1)] + SPR, [("ymB", qb)], scale=spc(O_SUBLN))
                              return
                          _, qb, hp, hl, ki, kc, nk = stp
                          q0, qn = cols(qb)
                          h = 2 * hp + hl
                          po, pz = (0, 64) if hl == 0 else (64, 0)
                          accb = (4, 5) if hl == 0 else (6, 7)
                          eb = Eb[i % 3]
                          ek = ("Eb", i % 3)
                          groups = [(PS[:, accb[c2], 0:qn], VA[:, kc, h, :], eb[:, c2, 0:qn]) for c2 in range(2)]

                          def fn(e, groups=groups, first=(ki == 0), last=(ki == nk - 1)):
                              for out_ap, l, r in groups:
                                  inst = e.matmul(out_ap, lhsT=l, rhs=r, start=first, stop=last)
                              return inst
                          S.op("pe", fn, [ek, ("VA", kc, 0), "VAones"], [("pb", accb[0]), ("pb", accb[1])])
                          if ki == nk - 1:
                              recip(R[0][po:po + 64, 0:qn], PS[pz:pz + 64, accb[0], 0:qn], [("pb", accb[0])], [("Rr", 0)])
                              recip(R[1][po:po + 64, 0:qn], PS[pz:pz + 64, accb[1], 0:qn], [("pb", accb[1])], [("Rr", 1)])
                              tt("dve", o1[po:po + 64, 0:qn], PS[po:po + 64, accb[0], 0:qn], R[0][po:po + 64, 0:qn], ALU.mult,
                                 [("pb", accb[0]), ("Rr", 0)], ["o1"])
                              tt("dve", o2[po:po + 64, 0:qn], PS[po:po + 64, accb[1], 0:qn], R[1][po:po + 64, 0:qn], ALU.mult,
                                 [("pb", accb[1]), ("Rr", 1)], ["o2"])
                              stt(OP[po:po + 64, 0:qn], o2[po:po + 64, 0:qn], NEGLAM[po:po + 64, :], o1[po:po + 64, 0:qn], ALU.mult, ALU.add,
                                  ["o1", "o2", "lamt"], [("OP", hl)])

                      nst = len(steps)
                      for i in range(nst + 2):
                          if i < nst:
                              stage1(i)
                          if i >= 2:
                              stage2(i - 2)
                  dbg_tap("yB_%d" % li, ym[:, :, :], [128, 2, NT], BF16, [("ymB", k) for k in tbs_o])
                  wout_multi(sc, [(ymA_t, lambda tb: [("ymA", tb)]), (ym, lambda tb: [("ymB", tb)])], 0)
              scAB.__exit__(None, None, None)

              checkpoint(4)
              scCD = Scope()
              scCD.__enter__()
              ymC_t = T(scCD, "ymC", [128, 2, NT], BF16)
              with Scope() as sc:
                  ym = ymC_t
                  wna = T(sc, "wna", [128, 8, 768], BF16)
                  dma("pool", wna[:, :, 0:512], winv[:, :, 1024:1536], [], ["wna"], "wna")
                  dma("pool", wna[:, :, 512:768], winv[:, :, 1536:1792], [], ["wna"], "wna")
                  mbv = mb_d[li].rearrange("p (h i q) -> p h i q", h=4, i=21)
                  for ci in range(2):
                      with Scope() as sc2:
                          qn_t = T(sc2, "qnC", [128, NT], BF16)
                          kn_t = T(sc2, "knC", [128, NT], BF16)
                          VN = T(sc2, "VN", [128, 18, 2, 128], BF16)
                          MB = T(sc2, "MB", [128, 2, 21, 128], BF16)
                          sqb = [T(sc2, "sqC%d" % i, [128, 512], BF16) for i in range(2)]
                          rs = [T(sc2, "rsC%d" % i, [128, 512], F32) for i in range(2)]
                          Es = [T(sc2, "EsC%d" % i, [128, 640], F32) for i in range(3)]
                          Eb = [T(sc2, "EbC%d" % i, [128, 896], BF16) for i in range(3)]
                          Rn = [T(sc2, "RnC%d" % i, [128, 128], F32) for i in range(3)]
                          S.op("pool", lambda e, VN=VN: e.memset(VN[:, :, :, :].rearrange("p a h d -> p (a h d)"), 1.0), [], ["VNones"])
                          for hl in range(2):
                              dma("pool", MB[:, hl, :, :], mbv[:, 2 * ci + hl, :, :], [], ["MB"], "MB")
                          it = 0
                          for isq in (True, False):
                              wcol = (0 if isq else 256) + ci * 128
                              dst = qn_t if isq else kn_t
                              gcol = O_NQN if isq else O_NKN
                              for tb in (tbs_o if isq else tbs_all):
                                  c0, n = cols(tb)
                                  i2 = it % 2
                                  it += 1
                                  bk = bank()
                                  mm(PS[:, bk, 0:n], [(wna[:, c, wcol:wcol + 128], hT[:, c, c0:c0 + n]) for c in range(8)],
                                     ["wna"] + hkeys(tb), [("pb", bk)])
                                  act(sqb[i2][:, 0:n], PS[:, bk, 0:n], AF.Square, [("pb", bk)], [("sqC", i2)])
                                  bk2 = bank()
                                  mm(PS[:, bk2, 0:n], [(B64, sqb[i2][:, 0:n])], [("sqC", i2), "cb"], [("pb", bk2)])
                                  act(rs[i2][:, 0:n], PS[:, bk2, 0:n], AF.Ln, [("pb", bk2)], [("rsC", i2)], bias=EPS)
                                  act(rs[i2][:, 0:n], rs[i2][:, 0:n], AF.Exp, [("rsC", i2)], [("rsC", i2)], scale=-0.5)
                                  stt(dst[:, c0:c0 + n], PS[:, bk, 0:n], spc(gcol), rs[i2][:, 0:n], ALU.mult, ALU.mult,
                                      [("pb", bk), ("rsC", i2)] + SPR, [("qnC" if isq else "knC", tb)])
                          for t in range(18):
                              bk = bank()
                              mm(PS[:, bk, 0:128], [(hT[:, c, t * 128:(t + 1) * 128], wna[:, c, 512 + ci * 128:512 + (ci + 1) * 128]) for c in range(8)],
                                 ["wna"] + tt_hkeys(t), [("pb", bk)])
                              copy("dve", VN[:, t, 0, 0:64], PS[:, bk, 0:64], [("pb", bk), "VNones"], [("VN", t, 0)])
                              copy("dve", VN[:, t, 1, 64:128], PS[:, bk, 64:128], [("pb", bk), "VNones"], [("VN", t, 1)])
                          nblk = 18 if ctx_out else 16
                          nsteps = []
                          for m in range(nblk):
                              for hl in range(2):
                                  nsteps.append((m, hl))

                          def na_info(m):
                              if m < 16:
                                  a0, nl, id0 = na_local(m)
                                  return nl, id0, list(range(a0, a0 + nl)) + [16, 17], m // 4
                              return 0, 0, [16, 17], 4

                          def na_stage1(i):
                              m, hl = nsteps[i]
                              nl, id0, chunks, qtb = na_info(m)
                              nch = len(chunks)
                              pb = hl * 64
                              s2 = i % 3
                              sbk = (0, 2, 4)[s2]
                              psS = PSF[:, sbk * 512:sbk * 512 + 1024]
                              groups = []
                              rk = [("qnC", qtb)]
                              for j, a in enumerate(chunks):
                                  groups.append((psS[:, j * 128:(j + 1) * 128],
                                                 [(kn_t[pb:pb + 64, a * 128:(a + 1) * 128], qn_t[pb:pb + 64, m * 128:(m + 1) * 128])],
                                                 (pb, 0)))
                                  rk.append(("knC", a // 4 if a < 16 else 4))
                              if NA_WARM:
                                  groups = [(PS[:, sbk, 0:512], [(VN[:, 0, 0, :], kn_t[:, 0:512])], None) for _ in range(NA_WARM)] + groups
                              mm_multi(groups, rk + ["VNones", ("VN", 0, 0)], [("pb", sbk), ("pb", sbk + 1)])
                              if nl:
                                  stt(Es[s2][:, 0:nl * 128], psS[:, 0:nl * 128], 0.125,
                                      MB[:, hl, id0:id0 + nl, :].rearrange("p i q -> p (i q)"), ALU.mult, ALU.add,
                                      [("pb", sbk), ("pb", sbk + 1), "MB"], [("EsC", s2)])
                                  act(Eb[s2][:, 0:nl * 128], Es[s2][:, 0:nl * 128], AF.Exp, [("EsC", s2)], [("EbC", s2)])
                              act(Eb[s2][:, nl * 128:nch * 128], psS[:, nl * 128:nch * 128], AF.Exp, [("pb", sbk), ("pb", sbk + 1)], [("EbC", s2)], scale=0.125)

                          def na_stage2(i):
                              m, hl = nsteps[i]
                              nl, id0, chunks, qtb = na_info(m)
                              po, pz = (0, 64) if hl == 0 else (64, 0)
                              s2 = i % 3
                              abk = (6, 7)[i % 2]
                              pairs = [(VN[:, a, hl, :], Eb[s2][:, j * 128:(j + 1) * 128]) for j, a in enumerate(chunks)]
                              mm(PS[:, abk, 0:128], pairs, [("EbC", s2), "VNones"] + [("VN", a, hl) for a in chunks], [("pb", abk)])
                              if NA_ACT_RECIP:
                                  act(Rn[s2][pz:pz + 64, :], PS[pz:pz + 64, abk, 0:128], AF.Ln, [("pb", abk)], [("RnC", s2)])
                                  act(Rn[s2][pz:pz + 64, :], Rn[s2][pz:pz + 64, :], AF.Exp, [("RnC", s2)], [("RnC", s2)], scale=-1.0)
                                  return
                              else:
                                  recip(Rn[s2][po:po + 64, :], PS[pz:pz + 64, abk, 0:128], [("pb", abk)], [("RnC", s2)])
                                  tt("dve", ym[po:po + 64, ci, m * 128:(m + 1) * 128], PS[po:po + 64, abk, 0:128], Rn[s2][po:po + 64, :], ALU.mult,
                                     [("pb", abk), ("RnC", s2)], [("ymC", qtb, ci, hl)])

                          def na_stage2b(i):
                              if not NA_ACT_RECIP:
                                  return
                              m, hl = nsteps[i]
                              nl, id0, chunks, qtb = na_info(m)
                              po, pz = (0, 64) if hl == 0 else (64, 0)
                              s2 = i % 3
                              abk = (6, 7)[i % 2]
                              tt("dve", ym[po:po + 64, ci, m * 128:(m + 1) * 128], PS[po:po + 64, abk, 0:128], Rn[s2][pz:pz + 64, :], ALU.mult,
                                 [("pb", abk), ("RnC", s2)], [("ymC", qtb, ci, hl)])

                          for i in range(len(nsteps) + 2):
                              if i >= 2:
                                  na_stage2(i - 2)
                              if i < len(nsteps):
                                  na_stage1(i)
                              if i >= 2:
                                  na_stage2b(i - 2)
                  dbg_tap("yC_%d" % li, ym[:, :, :], [128, 2, NT], BF16, [("ymC", k, ci, hl) for k in tbs_o for ci in range(2) for hl in range(2)])

              checkpoint(5)
              with Scope() as sc:
                  ym = T(sc, "ymD", [128, 2, NT], BF16)
                  uT = T(sc, "uT", [128, 2, NT], BF16)
                  wg = T(sc, "wg", [128, 8, 512], BF16)
                  wsb = T(sc, "wsb", [128, 4, 128], BF16)
                  vg = [T(sc, "vg%d" % i, [128, 256], F32) for i in range(3)]
                  vsq = [T(sc, "vsq%d" % i, [128, 256], F32) for i in range(3)]
                  vss = [T(sc, "vss%d" % i, [128, 4], F32) for i in range(3)]
                  vn = [T(sc, "vn%d" % i, [128, 256], BF16) for i in range(3)]
                  gt = [[T(sc, "gt%d%d" % (i, j), [128, 128], F32) for j in range(2)] for i in range(3)]
                  dma("pool", wg[:, :, :], winv[:, :, 1792:2304], [], ["wg"], "wg")
                  dma("pool", wsb[:, :, :].rearrange("p g q -> p (g q)"), ws_d[li, :, :], [], ["wsb"], "wsb")
                  for oc in range(2):
                      def ev(bk, tb, c0, n, oc=oc):
                          act(uT[:, oc, c0:c0 + n], PS[:, bk, 0:n], AF.Gelu_apprx_tanh, [("pb", bk)], [("uT", oc, tb)])
                      proj_fm(wg, "wg", oc * 128, tbs_o, ev)
                  def gm_a(t):
                      i2 = t % 3
                      bk = bank((0, 1, 2, 3))
                      mm(PS[:, bk, 0:256], [(hT[:, c, t * 128:(t + 1) * 128], wg[:, c, 256:512]) for c in range(8)],
                         ["wg"] + tt_hkeys(t), [("pb", bk)])
                      act(vg[i2][:, :], PS[:, bk, 0:256], AF.Gelu_apprx_tanh, [("pb", bk)], [("vg", i2)])
                      tt("pool", vsq[i2][:, :], vg[i2][:, :], vg[i2][:, :], ALU.mult, [("vg", i2)], [("vsq", i2)])
                      S.op("dve", lambda e, i2=i2: e.tensor_reduce(out=vss[i2][:, :], in_=vsq[i2][:, :].rearrange("p (g d) -> p g d", d=64), axis=AX.X, op=ALU.add),
                           [("vsq", i2)], [("vss", i2)])

                  def gm_a2(t):
                      i2 = t % 3
                      act(vss[i2][:, :], vss[i2][:, :], AF.Ln, [("vss", i2)], [("vss", i2)], scale=1.0 / 64.0, bias=EPS)
                      act(vss[i2][:, :], vss[i2][:, :], AF.Exp, [("vss", i2)], [("vss", i2)], scale=-0.5)
                      for g in range(4):
                          stt(vn[i2][:, g * 64:(g + 1) * 64], vg[i2][:, g * 64:(g + 1) * 64], vss[i2][:, g:g + 1], spc(O_GN + g * 64, 64),
                              ALU.mult, ALU.mult, [("vg", i2), ("vss", i2)] + SPR, [("vn", i2)])

                  def gm_b(t):
                      i2 = t % 3
                      tb = t // 4 if t < 16 else 4
                      bk2 = bank((4, 5, 6, 7))
                      groups = []
                      for g in range(4):
                          pp = (g % 2) * 64
                          groups.append((PS[pp:pp + 64, bk2, (g // 2) * 128:(g // 2) * 128 + 128],
                                         [(vn[i2][:, g * 64:(g + 1) * 64], wsb[:, g, :])], (0, pp)))
                      mm_multi(groups, [("vn", i2), "wsb"], [("pb", bk2)])
                      for cj in range(2):
                          gk = ("gt", cj, i2)
                          tt("dve", gt[i2][cj][:, :], PS[:, bk2, cj * 128:(cj + 1) * 128], sp[:, O_BST + cj * 128:O_BST + (cj + 1) * 128], ALU.add,
                             [("pb", bk2)] + SPR, [gk])
                          tt("dve", ym[:, cj, t * 128:(t + 1) * 128], gt[i2][cj][:, :], uT[:, cj, t * 128:(t + 1) * 128], ALU.mult,
                             [gk, ("uT", cj, tb)], [("ymD", tb, cj)])

                  for k in range(ntt_o + 2):
                      if k >= 2:
                          gm_b(k - 2)
                      if 1 <= k <= ntt_o:
                          gm_a2(k - 1)
                      if k < ntt_o:
                          gm_a(k)
                  dbg_tap("yD_%d" % li, ym[:, :, :], [128, 2, NT], BF16, [("ymD", k, cj) for k in tbs_o for cj in range(2)])
                  wout_multi(sc, [(ymC_t, lambda tb: [("ymC", tb, ci, hl) for ci in range(2) for hl in range(2)]),
                                  (ym, lambda tb: [("ymD", tb, 0), ("ymD", tb, 1)])], 2)
              scCD.__exit__(None, None, None)
              dbg_tap("xmid_%d" % li, xT[:, :, :], [128, 8, NT], F32, [k for tb in range(5) for k in xkeys(tb)])

              checkpoint(6)
              norm_phase(1, tbs_o)
              wupv = wup_d[li].rearrange("(c p) n -> p c n", p=128)
              wdnv = wdn_d[li].rearrange("(j p) n -> p j n", p=128)
              segs = [(0, 2048)] + ([(2048, 256)] if ctx_out else [])
              with Scope() as sc:
                  hid = T(sc, "hid", [128, 4, NT], BF16)
                  wdn = T(sc, "wdn", [128, 4, 1024], BF16)
                  wup = [[T(sc, "wup%d%d" % (i, hv), [128, 8, 512], BF16) for hv in range(2)] for i in range(2)]
                  accg = T(sc, "accg", [128, 2048], F32)
                  accv = T(sc, "accv", [128, 2048], F32)
                  sg = T(sc, "sg", [128, 2048], BF16)
                  cacc = T(sc, "cacc", [128, 2, 4, 256], F32)
                  csg = T(sc, "csg", [128, 4, 256], BF16)
                  evi = [0]
                  for pi, (j0, nj) in enumerate(FFN_PASSES):
                      wu = wup[pi % 2]
                      for hv in range(2):
                          cs = hv * 2816 + j0 * 128
                          dma("pool", wu[hv][:, :, 0:nj * 128], wupv[:, :, cs:cs + nj * 128], [], [("wup", pi % 2, hv)], "wup%d%d" % (pi % 2, hv))
                      for hlf in range(2):
                          dma("pool", wdn[:, 0:nj, hlf * 512:(hlf + 1) * 512], wdnv[:, j0:j0 + nj, hlf * 512:(hlf + 1) * 512], [], ["wdn"], "wdn")
                      for jj in range(nj):
                          j = j0 + jj
                          for hv in range(2):
                              jc = hv * 22 + j
                              acc = accg if hv == 0 else accv
                              rb = hv * 4
                              for (s0, sn) in [(0, 2048)]:
                                  an = "accg" if hv == 0 else "accv"
                                  row = PSF[:, rb * 512:rb * 512 + sn]
                                  pieces = [(0, 1024), (1024, 2048)] if sn == 2048 else [(0, sn)]
                                  cw = lambda tap, jc=jc: sp[:, O_CW + jc * 3 + tap:O_CW + jc * 3 + tap + 1]
                                  bkey = lambda col: ("pb", rb + col // 512)
                                  for pidx, (p0_, p1_) in enumerate(pieces):
                                      groups = []
                                      rk = [("wup", pi % 2, hv)]
                                      wk_ = []
                                      for c0_ in range(p0_, p1_, 512):
                                          w = min(512, p1_ - c0_)
                                          groups.append((PS[:, rb + c0_ // 512, 0:w],
                                                         [(wu[hv][:, c, jj * 128:(jj + 1) * 128], hT[:, c, s0 + c0_:s0 + c0_ + w]) for c in range(8)], None))
                                          rk += hkeys((s0 + c0_) // 512 if s0 < 2048 else 4)
                                          wk_.append(bkey(c0_))
                                      mm_multi(groups, rk, wk_)
                                  for pidx, (p0_, p1_) in enumerate(pieces):
                                      ak = (an, s0, pidx)
                                      pbk = [bkey(c0_) for c0_ in range(p0_, p1_, 512)]
                                      act(acc[:, s0 + p0_:s0 + p1_], row[:, p0_:p1_], AF.Identity, pbk + SPR, [ak], scale=cw(1), bias=sp[:, O_CB + jc:O_CB + jc + 1])
                                      lo = max(p0_, 1)
                                      stt(acc[:, s0 + lo:s0 + p1_], row[:, lo - 1:p1_ - 1], cw(0), acc[:, s0 + lo:s0 + p1_], ALU.mult, ALU.add,
                                          pbk + [bkey(lo - 1), ak] + SPR, [ak])
                                  for pidx, (p0_, p1_) in enumerate(pieces):
                                      ak = (an, s0, pidx)
                                      pbk = [bkey(c0_) for c0_ in range(p0_, p1_, 512)]
                                      hi = min(p1_, sn - 1)
                                      stt(acc[:, s0 + p0_:s0 + hi], row[:, p0_ + 1:hi + 1], cw(2), acc[:, s0 + p0_:s0 + hi], ALU.mult, ALU.add,
                                          pbk + [bkey(hi), ak] + SPR, [ak])
                          for (s0, sn) in [(0, 2048)]:
                              pieces = [(0, 1024), (1024, 2048)] if sn == 2048 else [(0, sn)]
                              for pidx, (p0_, p1_) in enumerate(pieces):
                                  act(sg[:, s0 + p0_:s0 + p1_], accg[:, s0 + p0_:s0 + p1_], AF.Silu, [("accg", s0, pidx)], [("sg", s0, pidx)])
                                  tt("dve", hid[:, jj, s0 + p0_:s0 + p1_], sg[:, s0 + p0_:s0 + p1_], accv[:, s0 + p0_:s0 + p1_], ALU.mult,
                                     [("sg", s0, pidx), ("accv", s0, pidx)], [("hid", jj, s0)])
                      if ctx_out:
                          for jj in range(nj):
                              j = j0 + jj
                              for hv in range(2):
                                  jc = hv * 22 + j
                                  bk = bank()
                                  cac = cacc[:, hv, jj, :]
                                  ck = ("cacc", hv, jj)
                                  cwc = lambda tap, jc=jc: sp[:, O_CW + jc * 3 + tap:O_CW + jc * 3 + tap + 1]
                                  mm(PS[:, bk, 0:256], [(wu[hv][:, c, jj * 128:(jj + 1) * 128], hT[:, c, 2048:2304]) for c in range(8)],
                                     [("wup", pi % 2, hv)] + hkeys(4), [("pb", bk)])
                                  act(cac, PS[:, bk, 0:256], AF.Identity, [("pb", bk)] + SPR, [ck], scale=cwc(1), bias=sp[:, O_CB + jc:O_CB + jc + 1])
                                  stt(cac[:, 1:256], PS[:, bk, 0:255], cwc(0), cac[:, 1:256], ALU.mult, ALU.add, [("pb", bk), ck] + SPR, [ck])
                                  stt(cac[:, 0:255], PS[:, bk, 1:256], cwc(2), cac[:, 0:255], ALU.mult, ALU.add, [("pb", bk), ck] + SPR, [ck])
                          for jj in range(nj):
                              act(csg[:, jj, :], cacc[:, 0, jj, :], AF.Silu, [("cacc", 0, jj)], [("csg", jj)])
                              tt("dve", hid[:, jj, 2048:2304], csg[:, jj, :], cacc[:, 1, jj, :], ALU.mult, [("csg", jj), ("cacc", 1, jj)], [("hid", jj, 2048)])
                      for tb in tbs_o:
                          c0, n = cols(tb)
                          v = 0 if tb < 4 else 1
                          for oc in range(8):
                              bk = bank()
                              mm(PS[:, bk, 0:n], [(wdn[:, jj, oc * 128:(oc + 1) * 128], hid[:, jj, c0:c0 + n]) for jj in range(nj)],
                                 ["wdn"] + [("hid", jj, 0 if tb < 4 else 2048) for jj in range(nj)], [("pb", bk)])
                              evi[0] += 1
                              if False:
                                  ei = (evi[0] // EVAC_POOL) % 3
                                  act(etmp[ei][:, 0:n], PS[:, bk, 0:n], AF.Identity, [("pb", bk), MODK], [("etmp", ei)], scale=gate_ap(1, oc, v))
                                  tt("pool", xT[:, oc, c0:c0 + n], xT[:, oc, c0:c0 + n], etmp[ei][:, 0:n], ALU.add, [("etmp", ei), ("x", oc, tb)], [("x", oc, tb)])
                              else:
                                  stt(xT[:, oc, c0:c0 + n], PS[:, bk, 0:n], gate_ap(1, oc, v), xT[:, oc, c0:c0 + n], ALU.mult, ALU.add,
                                      [("pb", bk), MODK, ("x", oc, tb)], [("x", oc, tb)])
                              if li == L - 1 and pi == len(FFN_PASSES) - 1 and tb < 4 and not S.stopped:
                                  dma("sp", ov_early[:, oc, c0:c0 + n], xT[:, oc, c0:c0 + n], [("x", oc, tb)], [], "xo_%d_%d" % (oc, tb))
                                  stored_early.add((oc, tb))
              dbg_tap("xout_%d" % li, xT[:, :, :], [128, 8, NT], F32, [k for tb in range(5) for k in xkeys(tb)])

          except _Stop:
            pass
        S.stopped = False
        ov = out_d.rearrange("(c p) n -> p c n", p=128)
        for c in range(8):
            for tb in range(4):
                if (c, tb) not in stored_early:
                    dma("sp", ov[:, c, tb * 512:(tb + 1) * 512], xT[:, c, tb * 512:(tb + 1) * 512], [("x", c, tb)], [], "xout%d_%d" % (c, tb))
        if ctx_last:
            cov = cout_d.rearrange("(c p) n -> p c n", p=128)
            dma("sp", cov[:, :, :], xT[:, :, 2048:2304], xkeys(4), [], "cout")
        S.finish()
        with nc.Block() as block:
            S.emit(block)
    return nc, dbg_outs


def _bf(a):
    return np.ascontiguousarray(a.astype(np.float32)).astype(ml_dtypes.bfloat16)


def host_constants():
    p = np.arange(128)
    cb = np.zeros((128, NCB), np.float32)
    cb[:, C_ONESD:C_ONESD + 128] = 1.0 / 1024.0
    cb[:, C_B32:C_B32 + 128] = (p[:, None] // 32 == p[None, :] // 32) / 32.0
    cb[:, C_B64:C_B64 + 128] = (p[:, None] // 64 == p[None, :] // 64) / 64.0
    prot = np.zeros((128, 128), np.float32)
    for m in range(128):
        d = m % 32
        if d < 16:
            prot[m + 16, m] = -1.0
        else:
            prot[m - 16, m] = 1.0
    cb[:, C_PROT:C_PROT + 128] = prot
    for c in range(2):
        ch = c * 128 + p
        grp, cc = ch // 64, ch % 64
        l = np.arange(64)
        ang = 2 * np.pi * np.outer(cc, l) / 64.0
        blkc = np.zeros((128, 256), np.float32)
        blks = np.zeros((128, 256), np.float32)
        for i in range(128):
            blkc[i, grp[i] * 64:(grp[i] + 1) * 64] = np.cos(ang[i]) / 8.0
            blks[i, grp[i] * 64:(grp[i] + 1) * 64] = -np.sin(ang[i]) / 8.0
        cb[:, C_CS64 + c * 512:C_CS64 + c * 512 + 256] = blkc
        cb[:, C_CS64 + c * 512 + 256:C_CS64 + (c + 1) * 512] = blks
    t = np.arange(256)
    a256 = 2 * np.pi * ((np.outer(t, t)) % 256) / 256.0
    c256, s256 = np.cos(a256) / 16.0, np.sin(a256) / 16.0
    for tt_ in range(2):
        cb[:, C_C256 + tt_ * 256:C_C256 + (tt_ + 1) * 256] = c256[tt_ * 128:(tt_ + 1) * 128]
        cb[:, C_S256 + tt_ * 256:C_S256 + (tt_ + 1) * 256] = s256[tt_ * 128:(tt_ + 1) * 128]
    n = np.arange(2048, dtype=np.int64)
    aN = 2 * np.pi * ((np.outer(n, n)) % 2048).astype(np.float64) / 2048.0
    sc = 1.0 / math.sqrt(2048.0)
    CN = _bf(np.cos(aN) * sc)
    SN = _bf(np.sin(aN) * sc)
    freqs = (10000.0 ** (-np.arange(8, dtype=np.float32) / 8)).astype(np.float32)
    tok = np.arange(2048)
    row = (tok // 64).astype(np.float32)
    col = (tok % 64).astype(np.float32)
    ang = np.concatenate([row[:, None] * freqs, col[:, None] * freqs], axis=-1).astype(np.float32)
    slot = (p % 32) % 16
    rope = np.stack([np.cos(ang)[:, slot].T, np.sin(ang)[:, slot].T], axis=1).astype(np.float32)
    return _bf(cb), np.ascontiguousarray(rope), CN, SN


def na_mask_index():
    def one(m, a):
        k = np.arange(128)
        kr, kc = 2 * a + k // 64, k % 64
        qr, qc = 2 * m + k // 64, k % 64
        r0 = np.clip(qr - 4, 0, 24)
        c0 = np.clip(qc - 8, 0, 48)
        okr = (kr[:, None] >= r0[None, :]) & (kr[:, None] <= r0[None, :] + 7)
        okc = (kc[:, None] >= c0[None, :]) & (kc[:, None] <= c0[None, :] + 15)
        ro = kr[:, None] - qr[None, :] + 7
        co = kc[:, None] - qc[None, :] + 15
        return np.where(okr & okc, ro * 31 + co, 15 * 31)
    ids = [one(5, 5 + d) for d in range(-2, 3)]
    for m, a0 in ((0, 0), (1, 0), (14, 12), (15, 12)):
        ids += [one(m, a) for a in range(a0, a0 + 4)]
    return np.stack(ids)


def layer_pack(li, inp, midx):
    p = np.arange(128)
    sp = np.zeros((128, NS), np.float32)
    sp[:, O_BADA:O_BADA + 48] = inp["b_ada"][li].reshape(48, 128).T
    sp[:, O_GMIX:O_GMIX + 8] = inp["g_mix"][li].reshape(8, 128).T
    sp[:, O_GFFN:O_GFFN + 8] = inp["g_ffn"][li].reshape(8, 128).T
    sp[:, O_DQN] = inp["diff_qn"][li][p % 32]
    sp[:, O_DKN] = inp["diff_kn"][li][p % 32]
    sp[:, O_SUBLN] = inp["diff_subln"][li][p % 64]
    sp[:, O_NQN] = inp["na_qn"][li][p % 64]
    sp[:, O_NKN] = inp["na_kn"][li][p % 64]
    lam_init = 0.8 - 0.6 * math.exp(-0.3 * li)
    sp[:, O_LAMI] = lam_init
    sp[:, O_OMLI] = 1.0 - lam_init
    sp[:, O_LAM:O_LAM + 128] = inp["diff_lam"][li].reshape(1, 128)
    sp[:, O_GN:O_GN + 256] = inp["gmlp_norm"][li].reshape(1, 256)
    gb = inp["gmlp_b"][li]
    for cj in range(2):
        sp[:, O_BST + cj * 128:O_BST + (cj + 1) * 128] = gb[cj * 2 + p // 64, :]
    sp[:, O_CW:O_CW + 132] = inp["ffn_conv"][li].reshape(3, 44, 128).transpose(2, 1, 0).reshape(128, 132)
    sp[:, O_CB:O_CB + 44] = inp["ffn_conv_b"][li].reshape(44, 128).T
    wsT = np.ascontiguousarray(inp["gmlp_ws"][li].transpose(2, 0, 1)).reshape(128, 512)
    rp = np.concatenate([inp["na_rpb"][li].reshape(4, 465), np.full((4, 1), -30000.0, np.float32)], axis=1)
    mb = rp[:, midx]
    mb = np.ascontiguousarray(mb.transpose(2, 0, 1, 3)).reshape(128, 4 * 21 * 128)
    return sp, wsT, mb


_CACHE = {}


def _get_prog(n_layers, ctx_last):
    key = (n_layers, ctx_last)
    if key not in _CACHE:
        _CACHE[key] = build(n_layers, ctx_last)[0]
    return _CACHE[key]


def _consts():
    if "c" not in _CACHE:
        _CACHE["c"] = host_constants() + (na_mask_index(),)
    return _CACHE["c"]


FUSED = True


def kernel(**inp):
    inp = {k: np.asarray(v, dtype=np.float32) for k, v in inp.items()}
    cbf, rope, CN, SN, midx = _consts()
    B = inp["x"].shape[0]
    packs = [layer_pack(li, inp, midx) for li in range(DEPTH)]
    xT = [np.ascontiguousarray(inp["x"][b].T) for b in range(B)]
    cT = [np.ascontiguousarray(inp["ctx"][b].T) for b in range(B)]
    cpk = []
    for b in range(B):
        a = np.stack([inp["c"][b].reshape(8, 128).T, inp["c_ctx"].reshape(8, 128).T], axis=-1)
        cpk.append(np.ascontiguousarray(a.reshape(128, 16)))

    def layer_inputs(lis):
        return {
            "spk": np.stack([packs[li][0] for li in lis]), "wsT": np.stack([packs[li][1] for li in lis]),
            "mb": np.stack([packs[li][2] for li in lis]),
            "w_ada": inp["w_ada"][lis], "w_in": inp["w_in"][lis], "w_out": inp["w_out"][lis],
            "w_up": inp["ffn_up"][lis], "w_dn": inp["ffn_down"][lis],
        }
    common = {"cbf": cbf, "rope": rope, "CN": CN, "SN": SN}
    if FUSED:
        nc = _get_prog(DEPTH, False)
        lw = layer_inputs(list(range(DEPTH)))
        in_maps = [dict(common, xT=xT[b], cxT=cT[b], cpk=cpk[b], **lw) for b in range(B)]
        res = run_bass_kernel_spmd(nc, in_maps, core_ids=list(range(B)))
        outs = [r["outT"] for r in res.results]
    else:
        nc = _get_prog(1, True)
        for li in range(DEPTH):
            lw = layer_inputs([li])
            in_maps = [dict(common, xT=xT[b], cxT=cT[b], cpk=cpk[b], **lw) for b in range(B)]
            res = run_bass_kernel_spmd(nc, in_maps, core_ids=list(range(B)))
            xT = [np.ascontiguousarray(r["outT"]) for r in res.results]
            cT = [np.ascontiguousarray(r["coutT"]) for r in res.results]
        outs = xT
    return np.stack([np.ascontiguousarray(o.T) for o in outs]).astype(np.float32)
```

```python
import math
from contextlib import ExitStack
import numpy as np
import ml_dtypes
import concourse.bass as bass
import concourse.mybir as mybir
from concourse.bass_utils import run_bass_kernel_spmd

F32 = mybir.dt.float32
BF16 = mybir.dt.bfloat16
AF = mybir.ActivationFunctionType
ALU = mybir.AluOpType
AX = mybir.AxisListType

DEPTH = 4
FFN_VARIANT = 0
SKIPW = 0
NA_WARM = 0
PRECISE_FENCE = False
PE_WARM = 1
NA_ACT_RECIP = True
EVAC_POOL = 3
SES_ENGINES = ("act", "dve", "pool")
VCOPIES = 4
NT = 2304
EPS = 1e-6
NS = 888
O_BADA, O_GMIX, O_GFFN = 0, 48, 56
O_DQN, O_DKN, O_SUBLN, O_NQN, O_NKN, O_LAMI, O_OMLI = 64, 65, 66, 67, 68, 69, 70
O_LAM, O_GN, O_BST, O_CW, O_CB = 72, 200, 456, 712, 844
C_ONESD, C_B32, C_B64, C_PROT, C_CS64, C_C256, C_S256, NCB = 0, 128, 256, 384, 512, 1536, 2048, 2560
FFN_PASSES = [(0, 4), (4, 4), (8, 4), (12, 4), (16, 4), (20, 2)]


class Sched:
    ENGS = ("pe", "act", "dve", "pool", "sp")

    def __init__(self, nc, stack):
        self.nc, self.stack = nc, stack
        self.ops = {e: [] for e in self.ENGS}
        self.cnt = {}
        self.known = {e: {} for e in self.ENGS}
        self.last_w = {}
        self.readers = {}
        self.sems = {}
        self.fence_vals = {}
        self.stopped = False
        self.owner = {}
        self.scopes = []

    def open_scope(self):
        self.scopes.append({"keys": set(), "max": {}})

    def close_scope(self):
        sc = self.scopes.pop()
        for sk, v in sc["max"].items():
            if self.fence_vals.get(sk, 0) < v:
                self.fence_vals[sk] = v
        for k in sc["keys"]:
            self.owner.pop(k, None)
            self.last_w.pop(k, None)
            self.readers.pop(k, None)
        if self.scopes:
            up = self.scopes[-1]["max"]
            for sk, v in sc["max"].items():
                if up.get(sk, 0) < v:
                    up[sk] = v

    def sem(self, key):
        if key not in self.sems:
            self.sems[key] = self.stack.enter_context(self.nc.semaphore("s%d" % len(self.sems)))
        return self.sems[key]

    def op(self, eng, fn, reads=(), writes=(), dma=None):
        if self.stopped:
            return
        waits = {}

        def need(sk, val):
            if sk == ("eng", eng) and eng not in SES_ENGINES:
                return
            if self.known[eng].get(sk, 0) >= val:
                return
            if waits.get(sk, 0) < val:
                waits[sk] = val

        local = []
        for k in list(reads) + list(writes):
            if k not in self.owner:
                self.owner[k] = self.scopes[-1] if self.scopes else None
                if self.scopes:
                    self.scopes[-1]["keys"].add(k)
            if self.owner[k] is not None:
                local.append(self.owner[k])
        if local or not PRECISE_FENCE:
            for sk, v in self.fence_vals.items():
                need(sk, v)
        for k in reads:
            if k in self.last_w:
                need(*self.last_w[k])
        for k in writes:
            if k in self.last_w:
                need(*self.last_w[k])
            for sk, v in self.readers.get(k, {}).items():
                need(sk, v)
        for sk, v in waits.items():
            self.known[eng][sk] = v
        sk = ("dma", dma) if dma is not None else ("eng", eng)
        self.sem(sk)
        inc = 16 if dma is not None else 1
        self.cnt[sk] = self.cnt.get(sk, 0) + inc
        val = self.cnt[sk]
        for k in reads:
            d = self.readers.setdefault(k, {})
            d[sk] = max(d.get(sk, 0), val)
        for k in writes:
            self.last_w[k] = (sk, val)
            self.readers[k] = {}
        if PRECISE_FENCE:
            for scd in local:
                scd["max"][sk] = val
        else:
            for scd in self.scopes:
                scd["max"][sk] = val
        self.ops[eng].append((sorted(waits.items()), fn, sk, inc))

    def finish(self):
        self.ops["sp"].append((sorted(self.cnt.items()), None, None, 0))

    def emit(self, block):
        engs = {"pe": "tensor", "act": "scalar", "dve": "vector", "pool": "gpsimd", "sp": "sync"}

        def mk(ename):
            def body(e):
                for waits, fn, sk, inc in self.ops[ename]:
                    for wk, v in waits:
                        e.wait_ge(self.sems[wk], v)
                    if fn is not None:
                        fn(e).then_inc(self.sems[sk], inc)
            return body

        for ename, attr in engs.items():
            if self.ops[ename]:
                getattr(block, attr)(mk(ename))


def cols(tb):
    return (tb * 512, 512) if tb < 4 else (2048, 256)


def na_local(m):
    if 2 <= m <= 13:
        return m - 2, 5, 0
    e = {0: 0, 1: 1, 14: 2, 15: 3}[m]
    return (0 if m < 2 else 12), 4, 5 + 4 * e


class _Stop(Exception):
    pass


def build(n_layers, ctx_last=False, dbg=None, stop=None):
    L = n_layers
    nc = bass.Bass("TRN2", target_bir_lowering=False)
    D = lambda name, shape, dt, kind="ExternalInput": nc.dram_tensor(name, shape, dt, kind=kind).ap()
    xT_d = D("xT", [1024, 2048], F32)
    cxT_d = D("cxT", [1024, 256], F32)
    cpk_d = D("cpk", [128, 16], F32)
    cb_d = D("cbf", [128, NCB], BF16)
    rope_d = D("rope", [128, 2, 2048], F32)
    CN_d = D("CN", [2048, 2048], BF16)
    SN_d = D("SN", [2048, 2048], BF16)
    sp_d = D("spk", [L, 128, NS], F32)
    ws_d = D("wsT", [L, 128, 512], F32)
    mb_d = D("mb", [L, 128, 4 * 21 * 128], F32)
    wada_d = D("w_ada", [L, 1024, 6144], F32)
    win_d = D("w_in", [L, 1024, 2304], F32)
    wout_d = D("w_out", [L, 1024, 1024], F32)
    wup_d = D("w_up", [L, 1024, 5632], F32)
    wdn_d = D("w_dn", [L, 2816, 1024], F32)
    out_d = D("outT", [1024, 2048], F32, "ExternalOutput")
    cout_d = D("coutT", [1024, 256], F32, "ExternalOutput") if ctx_last else None
    dbg_outs = {}

    with ExitStack() as st:
        S = Sched(nc, st)

        tcnt = [0]

        class Scope(ExitStack):
            def __enter__(self):
                S.open_scope()
                return super().__enter__()

            def __exit__(self, *a):
                r = super().__exit__(*a)
                S.close_scope()
                return r

        def T(stack, name, shape, dt):
            tcnt[0] += 1
            return stack.enter_context(nc.sbuf_tensor("sb%d_%s" % (tcnt[0], name), shape, dt))

        PS = st.enter_context(nc.psum_tensor("PS", [128, 8, 512], F32))
        PSF = PS[:, :, :].rearrange("p b n -> p (b n)")
        xT = T(st, "xT", [128, 8, NT], F32)
        hT = T(st, "hT", [128, 8, NT], BF16)
        CB = T(st, "CB", [128, NCB], BF16)
        SPK = T(st, "SPK", [128, 1, NS], F32)
        MODS = T(st, "MOD", [128, 2, 48, 2], F32)
        AMS = T(st, "AM", [128, 2, 2, 8, 2], F32)
        ADAP = T(st, "ADAP", [128, 2, 64], F32)
        cpk = T(st, "cpk", [128, 8, 2], F32)
        sT = T(st, "sT", [128, 8, 2], BF16)
        lamt = T(st, "lamt", [128, 8], F32)

        bank_rr = [0]
        ov_early = out_d.rearrange("(c p) n -> p c n", p=128)
        stored_early = set()

        def bank(allowed=(0, 1, 2, 3, 4, 5, 6, 7)):
            bank_rr[0] += 1
            return allowed[bank_rr[0] % len(allowed)]

        def mm(out_ap, pairs, reads, writes, tile_position=None):
            def fn(e, pairs=pairs, out_ap=out_ap):
                n = len(pairs)
                for i, (l, r) in enumerate(pairs):
                    kw = {} if tile_position is None else {"tile_position": tile_position}
                    inst = e.matmul(out_ap, lhsT=l, rhs=r, start=(i == 0), stop=(i == n - 1), **kw)
                return inst
            S.op("pe", fn, reads, writes)

        def mm_multi(groups, reads, writes):
            def fn(e, groups=groups):
                for out_ap, pairs, tp in groups:
                    n = len(pairs)
                    for i, (l, r) in enumerate(pairs):
                        kw = {} if tp is None else {"tile_position": tp}
                        inst = e.matmul(out_ap, lhsT=l, rhs=r, start=(i == 0), stop=(i == n - 1), **kw)
                return inst
            S.op("pe", fn, reads, writes)

        def act(out, in_, func, reads, writes, **kw):
            S.op("act", lambda e: e.activation(out=out, in_=in_, func=func, **kw), reads, writes)

        def dma(eng, out, in_, reads, writes, sem):
            S.op(eng, lambda e: e.dma_start(out=out, in_=in_), reads, writes, dma=sem)

        def tt(eng, out, in0, in1, op, reads, writes):
            S.op(eng, lambda e: e.tensor_tensor(out=out, in0=in0, in1=in1, op=op), reads, writes)

        def stt(out, in0, scalar, in1, op0, op1, reads, writes):
            S.op("dve", lambda e: e.scalar_tensor_tensor(out=out, in0=in0, scalar=scalar, in1=in1, op0=op0, op1=op1),
                 reads, writes)

        def ts(eng, out, in0, s1, s2, op0, op1, reads, writes):
            if op1 is None:
                S.op(eng, lambda e: e.tensor_scalar(out=out, in0=in0, scalar1=s1, scalar2=None, op0=op0), reads, writes)
            else:
                S.op(eng, lambda e: e.tensor_scalar(out=out, in0=in0, scalar1=s1, scalar2=s2, op0=op0, op1=op1), reads, writes)

        def copy(eng, out, in_, reads, writes):
            if eng == "act":
                S.op("act", lambda e: e.activation(out=out, in_=in_, func=AF.Identity), reads, writes)
            else:
                S.op(eng, lambda e: e.tensor_copy(out=out, in_=in_), reads, writes)

        def recip(out, in_, reads, writes):
            S.op("dve", lambda e: e.reciprocal(out=out, in_=in_), reads, writes)

        def dbg_tap(name, ap, shape, dt, reads):
            if dbg is None or name not in dbg or S.stopped:
                return
            d = D("dbg_" + name, list(shape), dt, "ExternalOutput")
            dbg_outs[name] = d
            dma("sp", d, ap, reads, [], "dbg_" + name)

        ev_rr = [0]

        def ev_eng():
            ev_rr[0] += 1
            return "act" if ev_rr[0] % 2 else "dve"

        def xkeys(tb):
            return [("x", c, tb) for c in range(8)]

        def hkeys(tb):
            return [("h", c, tb) for c in range(8)]

        def tt_hkeys(t):
            return hkeys(t // 4 if t < 16 else 4)

        dma("sp", CB[:, :], cb_d[:, :], [], ["cb"], "cb")
        dma("sp", cpk[:, :, :].rearrange("p c v -> p (c v)"), cpk_d[:, :], [], ["cpk"], "cpk")
        xv = xT_d.rearrange("(c p) n -> p c n", p=128)
        cv = cxT_d.rearrange("(c p) n -> p c n", p=128)
        for c in range(8):
            dma("sp", xT[:, c, 0:2048], xv[:, c, :], [], [("x", c, tb) for tb in range(4)], "xin%d" % c)
        dma("sp", xT[:, :, 2048:2304], cv[:, :, :], [], xkeys(4), "cin")
        act(sT[:, :, :], cpk[:, :, :], AF.Silu, ["cpk"], ["sT"])
        ONESD = CB[:, C_ONESD:C_ONESD + 128]
        B32 = CB[:, C_B32:C_B32 + 128]
        B64 = CB[:, C_B64:C_B64 + 128]
        PROT = CB[:, C_PROT:C_PROT + 128]

        def checkpoint(k):
            if stop is not None and stop == k:
                S.stopped = True

        for li in range(L):
          try:
              ctx_out = (li < L - 1) or ctx_last
              tbs_all = [0, 1, 2, 3, 4]
              tbs_o = tbs_all if ctx_out else [0, 1, 2, 3]
              ntt_o = 18 if ctx_out else 16
              sp = SPK[:, 0, :]

              def spc(o, n=1, sp=sp):
                  return sp[:, o:o + n]
              dma("sp", SPK[:, 0, :], sp_d[li, :, :], [], [("sp", 0)], "sp0")
              SPR = [("sp", 0)]

              MOD = MODS[:, li % 2]
              AM = AMS[:, li % 2]
              MODK = ("mod", li % 2)
              AMK = ("am", li % 2)

              def ada_load(lt, jg, wa):
                  wav = wada_d[lt].rearrange("(c p) n -> p c n", p=128)
                  dma("pool", wa[jg % len(wa)][:, :, :], wav[:, :, jg * 512:(jg + 1) * 512], [], [("wa", jg % len(wa))], "wa%d" % (jg % len(wa)))

              def ada_params(lt):
                  dma("sp", ADAP[:, lt % 2, :], sp_d[lt, :, 0:64], [], [("adap", lt % 2)], "adap%d" % (lt % 2))

              def ada_piece(lt, jg, wa, bk):
                  wb = wa[jg % len(wa)]
                  groups = []
                  for j4 in range(4):
                      groups.append((PS[:, bk, j4 * 2:j4 * 2 + 2],
                                     [(wb[:, c, j4 * 128:(j4 + 1) * 128], sT[:, c, :]) for c in range(8)], None))
                  mm_multi(groups, [("wa", jg % len(wa)), "sT"], [("pb", bk)])
                  psm = PS[:, bk, 0:8].rearrange("p (j v) -> p j v", v=2)
                  for v in range(2):
                      tt("dve", MODS[:, lt % 2, jg * 4:(jg + 1) * 4, v], psm[:, :, v], ADAP[:, lt % 2, O_BADA + jg * 4:O_BADA + (jg + 1) * 4], ALU.add,
                         [("pb", bk), ("adap", lt % 2)], [("mod", lt % 2)])

              def ada_finish(lt, whichs=(0, 1)):
                  for which, (so, go) in enumerate(((8, O_GMIX), (32, O_GFFN))):
                      if which not in whichs:
                          continue
                      for v in range(2):
                          stt(AMS[:, lt % 2, which, :, v], MODS[:, lt % 2, so:so + 8, v], 1.0, ADAP[:, lt % 2, go:go + 8], ALU.add, ALU.mult,
                              [("mod", lt % 2), ("adap", lt % 2)], [("am", lt % 2)])

              if li == 0:
                  scP = Scope()
                  scP.__enter__()
                  wa0 = [T(scP, "wa%d" % i, [128, 8, 512], BF16) for i in range(6)]
                  ada_params(0)
                  for jg in range(4):
                      ada_load(0, jg, wa0)
                  for jg in range(4):
                      ada_piece(0, jg, wa0, 7)
                  ada_finish(0, (0,))
                  for jg in range(4, 10):
                      ada_load(0, jg, wa0)
              checkpoint(1)
              dbg_tap("mod%d" % li, MOD[:, :, :], [128, 48, 2], F32, [MODK])

              def shift_ap(which, c, v):
                  s = (0, 24)[which] + c
                  return MOD[:, s, v:v + 1]

              def gate_ap(which, c, v):
                  s = (16, 40)[which] + c
                  return MOD[:, s, v:v + 1]

              def norm_phase(which, tbs):
                  with Scope() as sc:
                      sq = [T(sc, "sq%d" % i, [128, 8, 512], BF16) for i in range(2)]
                      rstd = [T(sc, "rstd%d" % i, [128, 512], F32) for i in range(2)]
                      tmp = [T(sc, "ntmp%d" % i, [128, 512], F32) for i in range(3)]

                      def n_a(k):
                          tb = tbs[k]
                          c0, n = cols(tb)
                          q2 = k % 2
                          act(sq[q2][:, :, 0:n], xT[:, :, c0:c0 + n], AF.Square, xkeys(tb), [("sq", q2)])
                          bk = bank()
                          mm(PS[:, bk, 0:n], [(ONESD, sq[q2][:, c, 0:n]) for c in range(8)], [("sq", q2), "cb"], [("pb", bk)])
                          act(rstd[q2][:, 0:n], PS[:, bk, 0:n], AF.Ln, [("pb", bk)], [("rstd", q2)], bias=EPS)
                          act(rstd[q2][:, 0:n], rstd[q2][:, 0:n], AF.Exp, [("rstd", q2)], [("rstd", q2)], scale=-0.5)

                      def n_b(k):
                          tb = tbs[k]
                          c0, n = cols(tb)
                          v = 0 if tb < 4 else 1
                          q2 = k % 2
                          for c in range(8):
                              t = tmp[c % 3]
                              tk = ("ntmp", c % 3)
                              tt("dve", t[:, 0:n], xT[:, c, c0:c0 + n], rstd[q2][:, 0:n], ALU.mult, [("x", c, tb), ("rstd", q2)], [tk])
                              act(hT[:, c, c0:c0 + n], t[:, 0:n], AF.Identity, [tk, AMK, MODK], [("h", c, tb)],
                                  scale=AM[:, which, c, v:v + 1], bias=shift_ap(which, c, v))

                      n_a(0)
                      for k in range(len(tbs)):
                          if k + 1 < len(tbs):
                              n_a(k + 1)
                          n_b(k)

              norm_phase(0, tbs_all)
              if li == 0:
                  for jg in range(4, 12):
                      ada_piece(0, jg, wa0, 7)
                      if jg + 6 < 12:
                          ada_load(0, jg + 6, wa0)
                  ada_finish(0, (1,))
                  scP.__exit__(None, None, None)
              checkpoint(2)
              dbg_tap("h1_%d" % li, hT[:, :, :], [128, 8, NT], BF16, [k for tb in range(5) for k in hkeys(tb)])

              winv = win_d[li].rearrange("(c p) n -> p c n", p=128)
              woutv = wout_d[li].rearrange("(c p) n -> p c n", p=128)

              def wout_partial(sc, ym, mix, ykeys_fn):
                  wout_multi(sc, [(ym, ykeys_fn)], mix)

              def wout_multi(sc, yms, mix):
                  nch = 2 * len(yms)
                  wo = T(sc, "wo", [128, nch, 1024], BF16)
                  wtmp = [T(sc, "wtmp%d" % i, [128, 512], F32) for i in range(3)]
                  wev = [0]
                  for hlf in range(2):
                      dma("pool", wo[:, :, hlf * 512:(hlf + 1) * 512], woutv[:, 2 * mix:2 * mix + nch, hlf * 512:(hlf + 1) * 512],
                          [], ["wo"], "wo")
                  for tb in tbs_o:
                      c0, n = cols(tb)
                      v = 0 if tb < 4 else 1
                      for oc in range(8):
                          bk = bank()
                          pairs = []
                          rkeys = ["wo"]
                          for yi, (ym_, kf) in enumerate(yms):
                              for c in range(2):
                                  pairs.append((wo[:, 2 * yi + c, oc * 128:(oc + 1) * 128], ym_[:, c, c0:c0 + n]))
                              rkeys += kf(tb)
                          mm(PS[:, bk, 0:n], pairs, rkeys, [("pb", bk)])
                          wev[0] += 1
                          if EVAC_POOL and wev[0] % EVAC_POOL == 0:
                              ei = (wev[0] // EVAC_POOL) % 3
                              act(wtmp[ei][:, 0:n], PS[:, bk, 0:n], AF.Identity, [("pb", bk), MODK], [("wtmp", ei)], scale=gate_ap(0, oc, v))
                              tt("pool", xT[:, oc, c0:c0 + n], xT[:, oc, c0:c0 + n], wtmp[ei][:, 0:n], ALU.add, [("wtmp", ei), ("x", oc, tb)], [("x", oc, tb)])
                          else:
                              stt(xT[:, oc, c0:c0 + n], PS[:, bk, 0:n], gate_ap(0, oc, v), xT[:, oc, c0:c0 + n], ALU.mult, ALU.add,
                                  [("pb", bk), MODK, ("x", oc, tb)], [("x", oc, tb)])

              def proj_fm(wt, wkey, wcol, tbs, evac):
                  for tb in tbs:
                      c0, n = cols(tb)
                      bk = bank()
                      mm(PS[:, bk, 0:n], [(wt[:, c, wcol:wcol + 128], hT[:, c, c0:c0 + n]) for c in range(8)],
                         [wkey] + hkeys(tb), [("pb", bk)])
                      evac(bk, tb, c0, n)

              with Scope() as sc:
                  ym = T(sc, "ymA", [128, 2, NT], BF16)
                  pf = T(sc, "pf", [128, 2, NT], BF16)
                  AB = T(sc, "AB", [128, 18, 512], BF16)
                  wf = T(sc, "wf", [128, 8, 256], BF16)
                  CNb = [T(sc, "CNb%d" % i, [128, 16, 256], BF16) for i in range(2)]
                  SNb = [T(sc, "SNb%d" % i, [128, 16, 256], BF16) for i in range(2)]
                  dma("pool", wf[:, :, :], winv[:, :, 0:256], [], ["wf"], "wf")
                  for oc in range(2):
                      def ev(bk, tb, c0, n, oc=oc):
                          copy(ev_eng(), pf[:, oc, c0:c0 + n], PS[:, bk, 0:n], [("pb", bk)], [("pf", oc, tb)])
                      proj_fm(wf, "wf", oc * 128, tbs_o, ev)
                  for t in range(ntt_o):
                      tb = t // 4 if t < 16 else 4
                      bk = bank()
                      mm(PS[:, bk, :], [(pf[:, c, t * 128:(t + 1) * 128], CB[:, C_CS64 + c * 512:C_CS64 + (c + 1) * 512]) for c in range(2)],
                         [("pf", 0, tb), ("pf", 1, tb), "cb"], [("pb", bk)])
                      copy(ev_eng(), AB[:, t, :], PS[:, bk, :], [("pb", bk)], [("AB", t)])
                  CNv = CN_d.rearrange("(t p) k -> p t k", p=128)
                  SNv = SN_d.rearrange("(t p) k -> p t k", p=128)
                  def ft_load(kb):
                      dma("sp", CNb[kb % 2][:, :, :], CNv[:, :, kb * 256:(kb + 1) * 256], [], [("CNb", kb % 2)], "CNb%d" % (kb % 2))
                      dma("sp", SNb[kb % 2][:, :, :], SNv[:, :, kb * 256:(kb + 1) * 256], [], [("SNb", kb % 2)], "SNb%d" % (kb % 2))
                  for kb in range(8):
                      if kb == 0:
                          ft_load(0)
                          ft_load(1)
                      for lc in range(2):
                          bk = bank()
                          pairs = []
                          for t in range(16):
                              pairs.append((AB[:, t, lc * 128:(lc + 1) * 128], CNb[kb % 2][:, t, :]))
                              pairs.append((AB[:, t, 256 + lc * 128:256 + (lc + 1) * 128], SNb[kb % 2][:, t, :]))
                          mm(PS[:, bk, 0:256], pairs, [("AB", t) for t in range(16)] + [("CNb", kb % 2), ("SNb", kb % 2)], [("pb", bk)])
                          copy(ev_eng(), ym[:, lc, kb * 256:(kb + 1) * 256], PS[:, bk, 0:256], [("pb", bk)], [("ymA", kb // 2)])
                      if kb + 2 < 8:
                          ft_load(kb + 2)
                  if ctx_out:
                      for lc in range(2):
                          bk = bank()
                          pairs = []
                          for t in range(2):
                              pairs.append((AB[:, 16 + t, lc * 128:(lc + 1) * 128], CB[:, C_C256 + t * 256:C_C256 + (t + 1) * 256]))
                              pairs.append((AB[:, 16 + t, 256 + lc * 128:256 + (lc + 1) * 128], CB[:, C_S256 + t * 256:C_S256 + (t + 1) * 256]))
                          mm(PS[:, bk, 0:256], pairs, [("AB", 16), ("AB", 17), "cb"], [("pb", bk)])
                          copy(ev_eng(), ym[:, lc, 2048:2304], PS[:, bk, 0:256], [("pb", bk)], [("ymA", 4)])
                  dbg_tap("yA_%d" % li, ym[:, :, :], [128, 2, NT], BF16, [("ymA", k) for k in range(5)])
                  wout_partial(sc, ym, 0, lambda tb: [("ymA", tb)])

              checkpoint(3)
              with Scope() as sc:
                  ym = T(sc, "ymB", [128, 2, NT], BF16)
                  qk = T(sc, "qk", [128, 4, NT], BF16)
                  VA = T(sc, "VA", [128, 18, 4, 128], BF16)
                  S.op("pool", lambda e: e.memset(VA[:, :, :, :].rearrange("p a h d -> p (a h d)"), 1.0), [], ["VAones"])
                  with Scope() as sc2:
                      wqk = T(sc2, "wqk", [128, 8, 512], BF16)
                      wv = T(sc2, "wvB", [128, 8, 256], BF16)
                      ROPE = [T(sc2, "ROPE%d" % i, [128, 2, 512], F32) for i in range(2)]
                      sqb = [T(sc2, "sqb%d" % i, [128, 512], BF16) for i in range(2)]
                      ub = [T(sc2, "ub%d" % i, [128, 512], BF16) for i in range(2)]
                      rs = [T(sc2, "rsB%d" % i, [128, 512], F32) for i in range(2)]
                      t1 = [T(sc2, "t1B%d" % i, [128, 512], F32) for i in range(2)]
                      t2 = [T(sc2, "t2B%d" % i, [128, 512], F32) for i in range(2)]
                      dma("pool", wqk[:, :, :], winv[:, :, 256:768], [], ["wqk"], "wqk")
                      dma("pool", wv[:, :, :], winv[:, :, 768:1024], [], ["wvB"], "wvB")
                      it = 0
                      blocks = []
                      for tb in tbs_all:
                          for oi in range(4):
                              if oi < 2 and tb == 4 and not ctx_out:
                                  continue
                              blocks.append((tb, oi))
                      rope_loaded = set()
                      pbank = {}

                      def qk_a(bi):
                          tb, oi = blocks[bi]
                          c0, n = cols(tb)
                          i2 = bi % 2
                          rp = ROPE[tb % 2]
                          rkk = ("rope", tb % 2)
                          if tb < 4 and tb not in rope_loaded:
                              rope_loaded.add(tb)
                              dma("sp", rp[:, :, :], rope_d[:, :, c0:c0 + n], [], [rkk], "rope%d" % (tb % 2))
                          gcol = O_DQN if oi < 2 else O_DKN
                          bk = bank()
                          mm(PS[:, bk, 0:n], [(wqk[:, c, oi * 128:(oi + 1) * 128], hT[:, c, c0:c0 + n]) for c in range(8)],
                             ["wqk"] + hkeys(tb), [("pb", bk)])
                          act(sqb[i2][:, 0:n], PS[:, bk, 0:n], AF.Square, [("pb", bk)], [("sqb", i2)])
                          act(ub[i2][:, 0:n], PS[:, bk, 0:n], AF.Identity, [("pb", bk)] + SPR, [("ub", i2)], scale=spc(gcol))
                          bk2 = bank()
                          mm(PS[:, bk2, 0:n], [(B32, sqb[i2][:, 0:n])], [("sqb", i2), "cb"], [("pb", bk2)])
                          if tb < 4:
                              bk3 = bank()
                              pbank[bi] = bk3
                              mm(PS[:, bk3, 0:n], [(PROT, ub[i2][:, 0:n])], [("ub", i2), "cb"], [("pb", bk3)])
                          act(rs[i2][:, 0:n], PS[:, bk2, 0:n], AF.Ln, [("pb", bk2)], [("rsB", i2)], bias=EPS)
                          act(rs[i2][:, 0:n], rs[i2][:, 0:n], AF.Exp, [("rsB", i2)], [("rsB", i2)], scale=-0.5)

                      def qk_b(bi):
                          tb, oi = blocks[bi]
                          c0, n = cols(tb)
                          i2 = bi % 2
                          rp = ROPE[tb % 2]
                          rkk = ("rope", tb % 2)
                          if tb < 4:
                              bk3 = pbank[bi]
                              tt("dve", t1[i2][:, 0:n], ub[i2][:, 0:n], rp[:, 0, 0:n], ALU.mult, [("ub", i2), rkk], [("t1B", i2)])
                              tt("dve", t2[i2][:, 0:n], PS[:, bk3, 0:n], rp[:, 1, 0:n], ALU.mult, [("pb", bk3), rkk], [("t2B", i2)])
                              tt("pool", t1[i2][:, 0:n], t1[i2][:, 0:n], t2[i2][:, 0:n], ALU.add, [("t1B", i2), ("t2B", i2)], [("t1B", i2)])
                              tt("dve", qk[:, oi, c0:c0 + n], t1[i2][:, 0:n], rs[i2][:, 0:n], ALU.mult, [("t1B", i2), ("rsB", i2)], [("qk", oi, tb)])
                          else:
                              tt("dve", qk[:, oi, c0:c0 + n], ub[i2][:, 0:n], rs[i2][:, 0:n], ALU.mult, [("ub", i2), ("rsB", i2)], [("qk", oi, tb)])

                      qk_a(0)
                      for bi in range(len(blocks)):
                          if bi + 1 < len(blocks):
                              qk_a(bi + 1)
                          qk_b(bi)
                      checkpoint(3.1)
                      for t in range(18):
                          bk = bank()
                          mm(PS[:, bk, 0:256], [(hT[:, c, t * 128:(t + 1) * 128], wv[:, c, :]) for c in range(8)],
                             ["wvB"] + tt_hkeys(t), [("pb", bk)])
                          for hh in range(VCOPIES):
                              off = 0 if hh % 2 == 0 else 64
                              copy("dve", VA[:, t, hh, off:off + 64], PS[:, bk, hh * 64:(hh + 1) * 64],
                                   [("pb", bk), "VAones"], [("VA", t, 0)])
                  dbg_tap("qkB_%d" % li, qk[:, :, :], [128, 4, NT], BF16, [("qk", oi, tb) for oi in range(4) for tb in range(5) if not (oi < 2 and tb == 4 and not ctx_out)])
                  checkpoint(3.2)
                  with Scope() as sc2:
                      lp = T(sc2, "lp", [128, 64], F32)
                      tt("dve", lp[:, 0:32], spc(O_LAM, 32), spc(O_LAM + 32, 32), ALU.mult, SPR, ["lp"])
                      tt("dve", lp[:, 32:64], spc(O_LAM + 64, 32), spc(O_LAM + 96, 32), ALU.mult, SPR, ["lp"])
                      S.op("dve", lambda e: e.tensor_reduce(out=lamt[:, 0:2], in_=lp[:, :].rearrange("p (a b) -> p a b", b=32), axis=AX.X, op=ALU.add),
                           ["lp"], ["lamt"])
                      act(lamt[:, 2:4], lamt[:, 0:2], AF.Exp, ["lamt"], ["lamt"])
                      tt("dve", lamt[:, 4:5], lamt[:, 2:3], lamt[:, 3:4], ALU.subtract, ["lamt"], ["lamt"])
                      tt("dve", lamt[:, 5:6], lamt[:, 4:5], spc(O_LAMI), ALU.add, ["lamt"] + SPR, ["lamt"])
                      ts("dve", lamt[:, 6:7], lamt[:, 5:6], -1.0, None, ALU.mult, None, ["lamt"], ["lamt"])
                  NEGLAM = lamt[:, 6:7]
                  checkpoint(3.3)
                  with Scope() as sc2:
                      Eb = [T(sc2, "Eb%d" % i, [128, 2, 512], BF16) for i in range(3)]
                      R = [T(sc2, "Rr%d" % i, [128, 512], F32) for i in range(2)]
                      o1 = T(sc2, "o1", [128, 512], F32)
                      o2 = T(sc2, "o2", [128, 512], F32)
                      OP = T(sc2, "OP", [128, 512], F32)
                      sqo = T(sc2, "sqo", [128, 512], BF16)
                      rso = T(sc2, "rso", [128, 512], F32)
                      SCALE = 32.0 ** -0.5
                      steps = []
                      pairs_l = [(qb, hp, 18 if qb < 4 else 2) for qb in tbs_o for hp in range(2)]
                      for pj, (qb, hp, nk_) in enumerate(pairs_l):
                          kcs = list(range(18)) if qb < 4 else [16, 17]
                          for hl in range(2):
                              for ki, kc in enumerate(kcs):
                                  steps.append(("kv", qb, hp, hl, ki, kc, len(kcs)))
                          nxt = pairs_l[pj + 1][2] if pj + 1 < len(pairs_l) else 0
                          steps.append(("subln", qb, hp, 10 if nxt == 18 else 0))
                      order = []
                      pending = []
                      for st_ in steps:
                          pending = [(d - 1, x) for d, x in pending]
                          if st_[0] == "subln":
                              pending.append((st_[3], st_))
                          else:
                              order.append(st_)
                          for d, x in list(pending):
                              if d <= 0:
                                  order.append(x)
                                  pending.remove((d, x))
                      order += [x for _, x in pending]
                      steps = order
                      wa_n = None
                      if li + 1 < L:
                          wa_n = [T(sc2, "wan%d" % i, [128, 8, 512], BF16) for i in range(2)]
                          ada_params(li + 1)
                          ada_load(li + 1, 0, wa_n)
                          ada_load(li + 1, 1, wa_n)
                          order = []
                          nkv = 0
                          jg_n = 0
                          for st_ in steps:
                              order.append(st_)
                              if st_[0] == "kv":
                                  nkv += 1
                                  if nkv % 22 == 0 and jg_n < 12:
                                      order.append(("ada", jg_n))
                                      jg_n += 1
                          while jg_n < 12:
                              order.append(("ada", jg_n))
                              jg_n += 1
                          steps = order

                      skip_warm = set()
                      for i_, st_ in enumerate(steps):
                          if st_[0] == "ada":
                              skip_warm.update(range(i_ + 1, i_ + 1 + SKIPW))

                      def stage1(i):
                          stp = steps[i]
                          sb = (0, 2)[i % 2]
                          if stp[0] == "kv":
                              _, qb, hp, hl, ki, kc, nk = stp
                              q0, qn = cols(qb)
                              pb = hl * 64
                              eb = Eb[i % 3]
                              ek = ("Eb", i % 3)
                              ktb = kc // 4 if kc < 16 else 4
                              groups = []
                              for c2 in range(2):
                                  p0 = pb + 32 * c2
                                  groups.append((PS[:, sb + c2, 0:qn],
                                                 [(qk[p0:p0 + 32, 2 + hp, kc * 128:(kc + 1) * 128], qk[p0:p0 + 32, hp, q0:q0 + qn])],
                                                 (p0, 0)))
                              if PE_WARM and i not in skip_warm:
                                  groups = [(PS[:, sb + c2, 0:512], [(VA[:, 0, 0, :], qk[:, 0, 0:512])], None) for c2 in range(PE_WARM)] + groups
                              mm_multi(groups, [("qk", 2 + hp, ktb), ("qk", hp, qb)], [("pb", sb), ("pb", sb + 1)])
                              act(eb[:, :, 0:qn], PS[:, sb:sb + 2, 0:qn], AF.Exp, [("pb", sb), ("pb", sb + 1)], [ek], scale=SCALE)
                      def stage2(i):
                          stp = steps[i]
                          sb = (0, 2)[i % 2]
                          if stp[0] == "ada":
                              jg = stp[1]
                              ada_piece(li + 1, jg, wa_n, sb)
                              if jg + 2 < 12:
                                  ada_load(li + 1, jg + 2, wa_n)
                              if jg == 11:
                                  ada_finish(li + 1)
                              return
                          if stp[0] != "kv":
                              _, qb, hp, _d = stp
                              q0, qn = cols(qb)
                              act(sqo[:, 0:qn], OP[:, 0:qn], AF.Square, [("OP", 0), ("OP", 1)], ["sqo"])
                              mm(PS[:, sb, 0:qn], [(B64, sqo[:, 0:qn])], ["sqo", "cb"], [("pb", sb)])
                              act(rso[:, 0:qn], PS[:, sb, 0:qn], AF.Ln, [("pb", sb)], ["rso"], bias=EPS)
                              act(rso[:, 0:qn], rso[:, 0:qn], AF.Exp, ["rso"], ["rso"], scale=-0.5)
                              stt(OP[:, 0:qn], OP[:, 0:qn], spc(O_OMLI), rso[:, 0:qn], ALU.mult, ALU.mult,
                                  [("OP", 0), ("OP", 1), "rso"] + SPR, [("OP", 0), ("OP", 1)])
                              act(ym[:, hp, q0:q0 + qn], OP[:, 0:qn], AF.Identity, [("OP", 0), ("OP", 1)] + SPR, [("ymB", qb)], scale=spc(O_SUBLN))
                              return
                          _, qb, hp, hl, ki, kc, nk = stp
                          q0, qn = cols(qb)
                          h = 2 * hp + hl
                          po, pz = (0, 64) if hl == 0 else (64, 0)
                          accb = (4, 5) if hl == 0 else (6, 7)
                          eb = Eb[i % 3]
                          ek = ("Eb", i % 3)
                          groups = [(PS[:, accb[c2], 0:qn], VA[:, kc, h, :], eb[:, c2, 0:qn]) for c2 in range(2)]

                          def fn(e, groups=groups, first=(ki == 0), last=(ki == nk - 1)):
                              for out_ap, l, r in groups:
                                  inst = e.matmul(out_ap, lhsT=l, rhs=r, start=first, stop=last)
                              return inst
                          S.op("pe", fn, [ek, ("VA", kc, 0), "VAones"], [("pb", accb[0]), ("pb", accb[1])])
                          if ki == nk - 1:
                              recip(R[0][po:po + 64, 0:qn], PS[pz:pz + 64, accb[0], 0:qn], [("pb", accb[0])], [("Rr", 0)])
                              recip(R[1][po:po + 64, 0:qn], PS[pz:pz + 64, accb[1], 0:qn], [("pb", accb[1])], [("Rr", 1)])
                              tt("dve", o1[po:po + 64, 0:qn], PS[po:po + 64, accb[0], 0:qn], R[0][po:po + 64, 0:qn], ALU.mult,
                                 [("pb", accb[0]), ("Rr", 0)], ["o1"])
                              tt("dve", o2[po:po + 64, 0:qn], PS[po:po + 64, accb[1], 0:qn], R[1][po:po + 64, 0:qn], ALU.mult,
                                 [("pb", accb[1]), ("Rr", 1)], ["o2"])
                              stt(OP[po:po + 64, 0:qn], o2[po:po + 64, 0:qn], NEGLAM[po:po + 64, :], o1[po:po + 64, 0:qn], ALU.mult, ALU.add,
                                  ["o1", "o2", "lamt"], [("OP", hl)])

                      nst = len(steps)
                      for i in range(nst + 2):
                          if i < nst:
                              stage1(i)
                          if i >= 2:
                              stage2(i - 2)
                  dbg_tap("yB_%d" % li, ym[:, :, :], [128, 2, NT], BF16, [("ymB", k) for k in tbs_o])
                  wout_partial(sc, ym, 1, lambda tb: [("ymB", tb)])

              checkpoint(4)
              scCD = Scope()
              scCD.__enter__()
              ymC_t = T(scCD, "ymC", [128, 2, NT], BF16)
              with Scope() as sc:
                  ym = ymC_t
                  wna = T(sc, "wna", [128, 8, 768], BF16)
                  dma("pool", wna[:, :, 0:512], winv[:, :, 1024:1536], [], ["wna"], "wna")
                  dma("pool", wna[:, :, 512:768], winv[:, :, 1536:1792], [], ["wna"], "wna")
                  mbv = mb_d[li].rearrange("p (h i q) -> p h i q", h=4, i=21)
                  for ci in range(2):
                      with Scope() as sc2:
                          qn_t = T(sc2, "qnC", [128, NT], BF16)
                          kn_t = T(sc2, "knC", [128, NT], BF16)
                          VN = T(sc2, "VN", [128, 18, 2, 128], BF16)
                          MB = T(sc2, "MB", [128, 2, 21, 128], BF16)
                          sqb = [T(sc2, "sqC%d" % i, [128, 512], BF16) for i in range(2)]
                          rs = [T(sc2, "rsC%d" % i, [128, 512], F32) for i in range(2)]
                          Es = [T(sc2, "EsC%d" % i, [128, 640], F32) for i in range(3)]
                          Eb = [T(sc2, "EbC%d" % i, [128, 896], BF16) for i in range(3)]
                          Rn = [T(sc2, "RnC%d" % i, [128, 128], F32) for i in range(3)]
                          S.op("pool", lambda e, VN=VN: e.memset(VN[:, :, :, :].rearrange("p a h d -> p (a h d)"), 1.0), [], ["VNones"])
                          for hl in range(2):
                              dma("pool", MB[:, hl, :, :], mbv[:, 2 * ci + hl, :, :], [], ["MB"], "MB")
                          it = 0
                          for isq in (True, False):
                              wcol = (0 if isq else 256) + ci * 128
                              dst = qn_t if isq else kn_t
                              gcol = O_NQN if isq else O_NKN
                              for tb in (tbs_o if isq else tbs_all):
                                  c0, n = cols(tb)
                                  i2 = it % 2
                                  it += 1
                                  bk = bank()
                                  mm(PS[:, bk, 0:n], [(wna[:, c, wcol:wcol + 128], hT[:, c, c0:c0 + n]) for c in range(8)],
                                     ["wna"] + hkeys(tb), [("pb", bk)])
                                  act(sqb[i2][:, 0:n], PS[:, bk, 0:n], AF.Square, [("pb", bk)], [("sqC", i2)])
                                  bk2 = bank()
                                  mm(PS[:, bk2, 0:n], [(B64, sqb[i2][:, 0:n])], [("sqC", i2), "cb"], [("pb", bk2)])
                                  act(rs[i2][:, 0:n], PS[:, bk2, 0:n], AF.Ln, [("pb", bk2)], [("rsC", i2)], bias=EPS)
                                  act(rs[i2][:, 0:n], rs[i2][:, 0:n], AF.Exp, [("rsC", i2)], [("rsC", i2)], scale=-0.5)
                                  stt(dst[:, c0:c0 + n], PS[:, bk, 0:n], spc(gcol), rs[i2][:, 0:n], ALU.mult, ALU.mult,
                                      [("pb", bk), ("rsC", i2)] + SPR, [("qnC" if isq else "knC", tb)])
                          for t in range(18):
                              bk = bank()
                              mm(PS[:, bk, 0:128], [(hT[:, c, t * 128:(t + 1) * 128], wna[:, c, 512 + ci * 128:512 + (ci + 1) * 128]) for c in range(8)],
                                 ["wna"] + tt_hkeys(t), [("pb", bk)])
                              copy("dve", VN[:, t, 0, 0:64], PS[:, bk, 0:64], [("pb", bk), "VNones"], [("VN", t, 0)])
                              copy("dve", VN[:, t, 1, 64:128], PS[:, bk, 64:128], [("pb", bk), "VNones"], [("VN", t, 1)])
                          nblk = 18 if ctx_out else 16
                          nsteps = []
                          for m in range(nblk):
                              for hl in range(2):
                                  nsteps.append((m, hl))

                          def na_info(m):
                              if m < 16:
                                  a0, nl, id0 = na_local(m)
                                  return nl, id0, list(range(a0, a0 + nl)) + [16, 17], m // 4
                              return 0, 0, [16, 17], 4

                          def na_stage1(i):
                              m, hl = nsteps[i]
                              nl, id0, chunks, qtb = na_info(m)
                              nch = len(chunks)
                              pb = hl * 64
                              s2 = i % 3
                              sbk = (0, 2, 4)[s2]
                              psS = PSF[:, sbk * 512:sbk * 512 + 1024]
                              groups = []
                              rk = [("qnC", qtb)]
                              for j, a in enumerate(chunks):
                                  groups.append((psS[:, j * 128:(j + 1) * 128],
                                                 [(kn_t[pb:pb + 64, a * 128:(a + 1) * 128], qn_t[pb:pb + 64, m * 128:(m + 1) * 128])],
                                                 (pb, 0)))
                                  rk.append(("knC", a // 4 if a < 16 else 4))
                              if NA_WARM:
                                  groups = [(PS[:, sbk, 0:512], [(VN[:, 0, 0, :], kn_t[:, 0:512])], None) for _ in range(NA_WARM)] + groups
                              mm_multi(groups, rk + ["VNones", ("VN", 0, 0)], [("pb", sbk), ("pb", sbk + 1)])
                              if nl:
                                  stt(Es[s2][:, 0:nl * 128], psS[:, 0:nl * 128], 0.125,
                                      MB[:, hl, id0:id0 + nl, :].rearrange("p i q -> p (i q)"), ALU.mult, ALU.add,
                                      [("pb", sbk), ("pb", sbk + 1), "MB"], [("EsC", s2)])
                                  act(Eb[s2][:, 0:nl * 128], Es[s2][:, 0:nl * 128], AF.Exp, [("EsC", s2)], [("EbC", s2)])
                              act(Eb[s2][:, nl * 128:nch * 128], psS[:, nl * 128:nch * 128], AF.Exp, [("pb", sbk), ("pb", sbk + 1)], [("EbC", s2)], scale=0.125)

                          def na_stage2(i):
                              m, hl = nsteps[i]
                              nl, id0, chunks, qtb = na_info(m)
                              po, pz = (0, 64) if hl == 0 else (64, 0)
                              s2 = i % 3
                              abk = (6, 7)[i % 2]
                              pairs = [(VN[:, a, hl, :], Eb[s2][:, j * 128:(j + 1) * 128]) for j, a in enumerate(chunks)]
                              mm(PS[:, abk, 0:128], pairs, [("EbC", s2), "VNones"] + [("VN", a, hl) for a in chunks], [("pb", abk)])
                              if NA_ACT_RECIP:
                                  act(Rn[s2][pz:pz + 64, :], PS[pz:pz + 64, abk, 0:128], AF.Ln, [("pb", abk)], [("RnC", s2)])
                                  act(Rn[s2][pz:pz + 64, :], Rn[s2][pz:pz + 64, :], AF.Exp, [("RnC", s2)], [("RnC", s2)], scale=-1.0)
                                  return
                              else:
                                  recip(Rn[s2][po:po + 64, :], PS[pz:pz + 64, abk, 0:128], [("pb", abk)], [("RnC", s2)])
                                  tt("dve", ym[po:po + 64, ci, m * 128:(m + 1) * 128], PS[po:po + 64, abk, 0:128], Rn[s2][po:po + 64, :], ALU.mult,
                                     [("pb", abk), ("RnC", s2)], [("ymC", qtb, ci, hl)])

                          def na_stage2b(i):
                              if not NA_ACT_RECIP:
                                  return
                              m, hl = nsteps[i]
                              nl, id0, chunks, qtb = na_info(m)
                              po, pz = (0, 64) if hl == 0 else (64, 0)
                              s2 = i % 3
                              abk = (6, 7)[i % 2]
                              tt("dve", ym[po:po + 64, ci, m * 128:(m + 1) * 128], PS[po:po + 64, abk, 0:128], Rn[s2][pz:pz + 64, :], ALU.mult,
                                 [("pb", abk), ("RnC", s2)], [("ymC", qtb, ci, hl)])

                          for i in range(len(nsteps) + 2):
                              if i >= 2:
                                  na_stage2(i - 2)
                              if i < len(nsteps):
                                  na_stage1(i)
                              if i >= 2:
                                  na_stage2b(i - 2)
                  dbg_tap("yC_%d" % li, ym[:, :, :], [128, 2, NT], BF16, [("ymC", k, ci, hl) for k in tbs_o for ci in range(2) for hl in range(2)])

              checkpoint(5)
              with Scope() as sc:
                  ym = T(sc, "ymD", [128, 2, NT], BF16)
                  uT = T(sc, "uT", [128, 2, NT], BF16)
                  wg = T(sc, "wg", [128, 8, 512], BF16)
                  wsb = T(sc, "wsb", [128, 4, 128], BF16)
                  vg = [T(sc, "vg%d" % i, [128, 256], F32) for i in range(3)]
                  vsq = [T(sc, "vsq%d" % i, [128, 256], F32) for i in range(3)]
                  vss = [T(sc, "vss%d" % i, [128, 4], F32) for i in range(3)]
                  vn = [T(sc, "vn%d" % i, [128, 256], BF16) for i in range(3)]
                  gt = [[T(sc, "gt%d%d" % (i, j), [128, 128], F32) for j in range(2)] for i in range(3)]
                  dma("pool", wg[:, :, :], winv[:, :, 1792:2304], [], ["wg"], "wg")
                  dma("pool", wsb[:, :, :].rearrange("p g q -> p (g q)"), ws_d[li, :, :], [], ["wsb"], "wsb")
                  for oc in range(2):
                      def ev(bk, tb, c0, n, oc=oc):
                          act(uT[:, oc, c0:c0 + n], PS[:, bk, 0:n], AF.Gelu_apprx_tanh, [("pb", bk)], [("uT", oc, tb)])
                      proj_fm(wg, "wg", oc * 128, tbs_o, ev)
                  def gm_a(t):
                      i2 = t % 3
                      bk = bank((0, 1, 2, 3))
                      mm(PS[:, bk, 0:256], [(hT[:, c, t * 128:(t + 1) * 128], wg[:, c, 256:512]) for c in range(8)],
                         ["wg"] + tt_hkeys(t), [("pb", bk)])
                      act(vg[i2][:, :], PS[:, bk, 0:256], AF.Gelu_apprx_tanh, [("pb", bk)], [("vg", i2)])
                      tt("pool", vsq[i2][:, :], vg[i2][:, :], vg[i2][:, :], ALU.mult, [("vg", i2)], [("vsq", i2)])
                      S.op("dve", lambda e, i2=i2: e.tensor_reduce(out=vss[i2][:, :], in_=vsq[i2][:, :].rearrange("p (g d) -> p g d", d=64), axis=AX.X, op=ALU.add),
                           [("vsq", i2)], [("vss", i2)])

                  def gm_a2(t):
                      i2 = t % 3
                      act(vss[i2][:, :], vss[i2][:, :], AF.Ln, [("vss", i2)], [("vss", i2)], scale=1.0 / 64.0, bias=EPS)
                      act(vss[i2][:, :], vss[i2][:, :], AF.Exp, [("vss", i2)], [("vss", i2)], scale=-0.5)
                      for g in range(4):
                          stt(vn[i2][:, g * 64:(g + 1) * 64], vg[i2][:, g * 64:(g + 1) * 64], vss[i2][:, g:g + 1], spc(O_GN + g * 64, 64),
                              ALU.mult, ALU.mult, [("vg", i2), ("vss", i2)] + SPR, [("vn", i2)])

                  def gm_b(t):
                      i2 = t % 3
                      tb = t // 4 if t < 16 else 4
                      bk2 = bank((4, 5, 6, 7))
                      groups = []
                      for g in range(4):
                          pp = (g % 2) * 64
                          groups.append((PS[pp:pp + 64, bk2, (g // 2) * 128:(g // 2) * 128 + 128],
                                         [(vn[i2][:, g * 64:(g + 1) * 64], wsb[:, g, :])], (0, pp)))
                      mm_multi(groups, [("vn", i2), "wsb"], [("pb", bk2)])
                      for cj in range(2):
                          gk = ("gt", cj, i2)
                          tt("dve", gt[i2][cj][:, :], PS[:, bk2, cj * 128:(cj + 1) * 128], sp[:, O_BST + cj * 128:O_BST + (cj + 1) * 128], ALU.add,
                             [("pb", bk2)] + SPR, [gk])
                          tt("dve", ym[:, cj, t * 128:(t + 1) * 128], gt[i2][cj][:, :], uT[:, cj, t * 128:(t + 1) * 128], ALU.mult,
                             [gk, ("uT", cj, tb)], [("ymD", tb, cj)])

                  for k in range(ntt_o + 2):
                      if k >= 2:
                          gm_b(k - 2)
                      if 1 <= k <= ntt_o:
                          gm_a2(k - 1)
                      if k < ntt_o:
                          gm_a(k)
                  dbg_tap("yD_%d" % li, ym[:, :, :], [128, 2, NT], BF16, [("ymD", k, cj) for k in tbs_o for cj in range(2)])
                  wout_multi(sc, [(ymC_t, lambda tb: [("ymC", tb, ci, hl) for ci in range(2) for hl in range(2)]),
                                  (ym, lambda tb: [("ymD", tb, 0), ("ymD", tb, 1)])], 2)
              scCD.__exit__(None, None, None)
              dbg_tap("xmid_%d" % li, xT[:, :, :], [128, 8, NT], F32, [k for tb in range(5) for k in xkeys(tb)])

              checkpoint(6)
              norm_phase(1, tbs_o)
              wupv = wup_d[li].rearrange("(c p) n -> p c n", p=128)
              wdnv = wdn_d[li].rearrange("(j p) n -> p j n", p=128)
              segs = [(0, 2048)] + ([(2048, 256)] if ctx_out else [])
              with Scope() as sc:
                  hid = T(sc, "hid", [128, 4, NT], BF16)
                  wdn = T(sc, "wdn", [128, 4, 1024], BF16)
                  wup = [[T(sc, "wup%d%d" % (i, hv), [128, 8, 512], BF16) for hv in range(2)] for i in range(2)]
                  accg = T(sc, "accg", [128, 2048], F32)
                  accv = T(sc, "accv", [128, 2048], F32)
                  sg = T(sc, "sg", [128, 2048], BF16)
                  cacc = T(sc, "cacc", [128, 2, 4, 256], F32)
                  csg = T(sc, "csg", [128, 4, 256], BF16)
                  evi = [0]
                  for pi, (j0, nj) in enumerate(FFN_PASSES):
                      wu = wup[pi % 2]
                      for hv in range(2):
                          cs = hv * 2816 + j0 * 128
                          dma("pool", wu[hv][:, :, 0:nj * 128], wupv[:, :, cs:cs + nj * 128], [], [("wup", pi % 2, hv)], "wup%d%d" % (pi % 2, hv))
                      for hlf in range(2):
                          dma("pool", wdn[:, 0:nj, hlf * 512:(hlf + 1) * 512], wdnv[:, j0:j0 + nj, hlf * 512:(hlf + 1) * 512], [], ["wdn"], "wdn")
                      for jj in range(nj):
                          j = j0 + jj
                          for hv in range(2):
                              jc = hv * 22 + j
                              acc = accg if hv == 0 else accv
                              rb = hv * 4
                              for (s0, sn) in [(0, 2048)]:
                                  an = "accg" if hv == 0 else "accv"
                                  row = PSF[:, rb * 512:rb * 512 + sn]
                                  pieces = [(0, 1024), (1024, 2048)] if sn == 2048 else [(0, sn)]
                                  cw = lambda tap, jc=jc: sp[:, O_CW + jc * 3 + tap:O_CW + jc * 3 + tap + 1]
                                  bkey = lambda col: ("pb", rb + col // 512)
                                  for pidx, (p0_, p1_) in enumerate(pieces):
                                      groups = []
                                      rk = [("wup", pi % 2, hv)]
                                      wk_ = []
                                      for c0_ in range(p0_, p1_, 512):
                                          w = min(512, p1_ - c0_)
                                          groups.append((PS[:, rb + c0_ // 512, 0:w],
                                                         [(wu[hv][:, c, jj * 128:(jj + 1) * 128], hT[:, c, s0 + c0_:s0 + c0_ + w]) for c in range(8)], None))
                                          rk += hkeys((s0 + c0_) // 512 if s0 < 2048 else 4)
                                          wk_.append(bkey(c0_))
                                      mm_multi(groups, rk, wk_)
                                  for pidx, (p0_, p1_) in enumerate(pieces):
                                      ak = (an, s0, pidx)
                                      pbk = [bkey(c0_) for c0_ in range(p0_, p1_, 512)]
                                      act(acc[:, s0 + p0_:s0 + p1_], row[:, p0_:p1_], AF.Identity, pbk + SPR, [ak], scale=cw(1), bias=sp[:, O_CB + jc:O_CB + jc + 1])
                                      lo = max(p0_, 1)
                                      stt(acc[:, s0 + lo:s0 + p1_], row[:, lo - 1:p1_ - 1], cw(0), acc[:, s0 + lo:s0 + p1_], ALU.mult, ALU.add,
                                          pbk + [bkey(lo - 1), ak] + SPR, [ak])
                                  for pidx, (p0_, p1_) in enumerate(pieces):
                                      ak = (an, s0, pidx)
                                      pbk = [bkey(c0_) for c0_ in range(p0_, p1_, 512)]
                                      hi = min(p1_, sn - 1)
                                      stt(acc[:, s0 + p0_:s0 + hi], row[:, p0_ + 1:hi + 1], cw(2), acc[:, s0 + p0_:s0 + hi], ALU.mult, ALU.add,
                                          pbk + [bkey(hi), ak] + SPR, [ak])
                          for (s0, sn) in [(0, 2048)]:
                              pieces = [(0, 1024), (1024, 2048)] if sn == 2048 else [(0, sn)]
                              for pidx, (p0_, p1_) in enumerate(pieces):
                                  act(sg[:, s0 + p0_:s0 + p1_], accg[:, s0 + p0_:s0 + p1_], AF.Silu, [("accg", s0, pidx)], [("sg", s0, pidx)])
                                  tt("dve", hid[:, jj, s0 + p0_:s0 + p1_], sg[:, s0 + p0_:s0 + p1_], accv[:, s0 + p0_:s0 + p1_], ALU.mult,
                                     [("sg", s0, pidx), ("accv", s0, pidx)], [("hid", jj, s0)])
                      if ctx_out:
                          for jj in range(nj):
                              j = j0 + jj
                              for hv in range(2):
                                  jc = hv * 22 + j
                                  bk = bank()
                                  cac = cacc[:, hv, jj, :]
                                  ck = ("cacc", hv, jj)
                                  cwc = lambda tap, jc=jc: sp[:, O_CW + jc * 3 + tap:O_CW + jc * 3 + tap + 1]
                                  mm(PS[:, bk, 0:256], [(wu[hv][:, c, jj * 128:(jj + 1) * 128], hT[:, c, 2048:2304]) for c in range(8)],
                                     [("wup", pi % 2, hv)] + hkeys(4), [("pb", bk)])
                                  act(cac, PS[:, bk, 0:256], AF.Identity, [("pb", bk)] + SPR, [ck], scale=cwc(1), bias=sp[:, O_CB + jc:O_CB + jc + 1])
                                  stt(cac[:, 1:256], PS[:, bk, 0:255], cwc(0), cac[:, 1:256], ALU.mult, ALU.add, [("pb", bk), ck] + SPR, [ck])
                                  stt(cac[:, 0:255], PS[:, bk, 1:256], cwc(2), cac[:, 0:255], ALU.mult, ALU.add, [("pb", bk), ck] + SPR, [ck])
                          for jj in range(nj):
                              act(csg[:, jj, :], cacc[:, 0, jj, :], AF.Silu, [("cacc", 0, jj)], [("csg", jj)])
                              tt("dve", hid[:, jj, 2048:2304], csg[:, jj, :], cacc[:, 1, jj, :], ALU.mult, [("csg", jj), ("cacc", 1, jj)], [("hid", jj, 2048)])
                      for tb in tbs_o:
                          c0, n = cols(tb)
                          v = 0 if tb < 4 else 1
                          for oc in range(8):
                              bk = bank()
                              mm(PS[:, bk, 0:n], [(wdn[:, jj, oc * 128:(oc + 1) * 128], hid[:, jj, c0:c0 + n]) for jj in range(nj)],
                                 ["wdn"] + [("hid", jj, 0 if tb < 4 else 2048) for jj in range(nj)], [("pb", bk)])
                              evi[0] += 1
                              if False:
                                  ei = (evi[0] // EVAC_POOL) % 3
                                  act(etmp[ei][:, 0:n], PS[:, bk, 0:n], AF.Identity, [("pb", bk), MODK], [("etmp", ei)], scale=gate_ap(1, oc, v))
                                  tt("pool", xT[:, oc, c0:c0 + n], xT[:, oc, c0:c0 + n], etmp[ei][:, 0:n], ALU.add, [("etmp", ei), ("x", oc, tb)], [("x", oc, tb)])
                              else:
                                  stt(xT[:, oc, c0:c0 + n], PS[:, bk, 0:n], gate_ap(1, oc, v), xT[:, oc, c0:c0 + n], ALU.mult, ALU.add,
                                      [("pb", bk), MODK, ("x", oc, tb)], [("x", oc, tb)])
                              if li == L - 1 and pi == len(FFN_PASSES) - 1 and tb < 4 and not S.stopped:
                                  dma("sp", ov_early[:, oc, c0:c0 + n], xT[:, oc, c0:c0 + n], [("x", oc, tb)], [], "xo_%d_%d" % (oc, tb))
                                  stored_early.add((oc, tb))
              dbg_tap("xout_%d" % li, xT[:, :, :], [128, 8, NT], F32, [k for tb in range(5) for k in xkeys(tb)])

          except _Stop:
            pass
        S.stopped = False
        ov = out_d.rearrange("(c p) n -> p c n", p=128)
        for c in range(8):
            for tb in range(4):
                if (c, tb) not in stored_early:
                    dma("sp", ov[:, c, tb * 512:(tb + 1) * 512], xT[:, c, tb * 512:(tb + 1) * 512], [("x", c, tb)], [], "xout%d_%d" % (c, tb))
        if ctx_last:
            cov = cout_d.rearrange("(c p) n -> p c n", p=128)
            dma("sp", cov[:, :, :], xT[:, :, 2048:2304], xkeys(4), [], "cout")
        S.finish()
        with nc.Block() as block:
            S.emit(block)
    return nc, dbg_outs


def _bf(a):
    return np.ascontiguousarray(a.astype(np.float32)).astype(ml_dtypes.bfloat16)


def host_constants():
    p = np.arange(128)
    cb = np.zeros((128, NCB), np.float32)
    cb[:, C_ONESD:C_ONESD + 128] = 1.0 / 1024.0
    cb[:, C_B32:C_B32 + 128] = (p[:, None] // 32 == p[None, :] // 32) / 32.0
    cb[:, C_B64:C_B64 + 128] = (p[:, None] // 64 == p[None, :] // 64) / 64.0
    prot = np.zeros((128, 128), np.float32)
    for m in range(128):
        d = m % 32
        if d < 16:
            prot[m + 16, m] = -1.0
        else:
            prot[m - 16, m] = 1.0
    cb[:, C_PROT:C_PROT + 128] = prot
    for c in range(2):
        ch = c * 128 + p
        grp, cc = ch // 64, ch % 64
        l = np.arange(64)
        ang = 2 * np.pi * np.outer(cc, l) / 64.0
        blkc = np.zeros((128, 256), np.float32)
        blks = np.zeros((128, 256), np.float32)
        for i in range(128):
            blkc[i, grp[i] * 64:(grp[i] + 1) * 64] = np.cos(ang[i]) / 8.0
            blks[i, grp[i] * 64:(grp[i] + 1) * 64] = -np.sin(ang[i]) / 8.0
        cb[:, C_CS64 + c * 512:C_CS64 + c * 512 + 256] = blkc
        cb[:, C_CS64 + c * 512 + 256:C_CS64 + (c + 1) * 512] = blks
    t = np.arange(256)
    a256 = 2 * np.pi * ((np.outer(t, t)) % 256) / 256.0
    c256, s256 = np.cos(a256) / 16.0, np.sin(a256) / 16.0
    for tt_ in range(2):
        cb[:, C_C256 + tt_ * 256:C_C256 + (tt_ + 1) * 256] = c256[tt_ * 128:(tt_ + 1) * 128]
        cb[:, C_S256 + tt_ * 256:C_S256 + (tt_ + 1) * 256] = s256[tt_ * 128:(tt_ + 1) * 128]
    n = np.arange(2048, dtype=np.int64)
    aN = 2 * np.pi * ((np.outer(n, n)) % 2048).astype(np.float64) / 2048.0
    sc = 1.0 / math.sqrt(2048.0)
    CN = _bf(np.cos(aN) * sc)
    SN = _bf(np.sin(aN) * sc)
    freqs = (10000.0 ** (-np.arange(8, dtype=np.float32) / 8)).astype(np.float32)
    tok = np.arange(2048)
    row = (tok // 64).astype(np.float32)
    col = (tok % 64).astype(np.float32)
    ang = np.concatenate([row[:, None] * freqs, col[:, None] * freqs], axis=-1).astype(np.float32)
    slot = (p % 32) % 16
    rope = np.stack([np.cos(ang)[:, slot].T, np.sin(ang)[:, slot].T], axis=1).astype(np.float32)
    return _bf(cb), np.ascontiguousarray(rope), CN, SN


def na_mask_index():
    def one(m, a):
        k = np.arange(128)
        kr, kc = 2 * a + k // 64, k % 64
        qr, qc = 2 * m + k // 64, k % 64
        r0 = np.clip(qr - 4, 0, 24)
        c0 = np.clip(qc - 8, 0, 48)
        okr = (kr[:, None] >= r0[None, :]) & (kr[:, None] <= r0[None, :] + 7)
        okc = (kc[:, None] >= c0[None, :]) & (kc[:, None] <= c0[None, :] + 15)
        ro = kr[:, None] - qr[None, :] + 7
        co = kc[:, None] - qc[None, :] + 15
        return np.where(okr & okc, ro * 31 + co, 15 * 31)
    ids = [one(5, 5 + d) for d in range(-2, 3)]
    for m, a0 in ((0, 0), (1, 0), (14, 12), (15, 12)):
        ids += [one(m, a) for a in range(a0, a0 + 4)]
    return np.stack(ids)


def layer_pack(li, inp, midx):
    p = np.arange(128)
    sp = np.zeros((128, NS), np.float32)
    sp[:, O_BADA:O_BADA + 48] = inp["b_ada"][li].reshape(48, 128).T
    sp[:, O_GMIX:O_GMIX + 8] = inp["g_mix"][li].reshape(8, 128).T
    sp[:, O_GFFN:O_GFFN + 8] = inp["g_ffn"][li].reshape(8, 128).T
    sp[:, O_DQN] = inp["diff_qn"][li][p % 32]
    sp[:, O_DKN] = inp["diff_kn"][li][p % 32]
    sp[:, O_SUBLN] = inp["diff_subln"][li][p % 64]
    sp[:, O_NQN] = inp["na_qn"][li][p % 64]
    sp[:, O_NKN] = inp["na_kn"][li][p % 64]
    lam_init = 0.8 - 0.6 * math.exp(-0.3 * li)
    sp[:, O_LAMI] = lam_init
    sp[:, O_OMLI] = 1.0 - lam_init
    sp[:, O_LAM:O_LAM + 128] = inp["diff_lam"][li].reshape(1, 128)
    sp[:, O_GN:O_GN + 256] = inp["gmlp_norm"][li].reshape(1, 256)
    gb = inp["gmlp_b"][li]
    for cj in range(2):
        sp[:, O_BST + cj * 128:O_BST + (cj + 1) * 128] = gb[cj * 2 + p // 64, :]
    sp[:, O_CW:O_CW + 132] = inp["ffn_conv"][li].reshape(3, 44, 128).transpose(2, 1, 0).reshape(128, 132)
    sp[:, O_CB:O_CB + 44] = inp["ffn_conv_b"][li].reshape(44, 128).T
    wsT = np.ascontiguousarray(inp["gmlp_ws"][li].transpose(2, 0, 1)).reshape(128, 512)
    rp = np.concatenate([inp["na_rpb"][li].reshape(4, 465), np.full((4, 1), -30000.0, np.float32)], axis=1)
    mb = rp[:, midx]
    mb = np.ascontiguousarray(mb.transpose(2, 0, 1, 3)).reshape(128, 4 * 21 * 128)
    return sp, wsT, mb


_CACHE = {}


def _get_prog(n_layers, ctx_last):
    key = (n_layers, ctx_last)
    if key not in _CACHE:
        _CACHE[key] = build(n_layers, ctx_last)[0]
    return _CACHE[key]


def _consts():
    if "c" not in _CACHE:
        _CACHE["c"] = host_constants() + (na_mask_index(),)
    return _CACHE["c"]


FUSED = True


def kernel(**inp):
    inp = {k: np.asarray(v, dtype=np.float32) for k, v in inp.items()}
    cbf, rope, CN, SN, midx = _consts()
    B = inp["x"].shape[0]
    packs = [layer_pack(li, inp, midx) for li in range(DEPTH)]
    xT = [np.ascontiguousarray(inp["x"][b].T) for b in range(B)]
    cT = [np.ascontiguousarray(inp["ctx"][b].T) for b in range(B)]
    cpk = []
    for b in range(B):
        a = np.stack([inp["c"][b].reshape(8, 128).T, inp["c_ctx"].reshape(8, 128).T], axis=-1)
        cpk.append(np.ascontiguousarray(a.reshape(128, 16)))

    def layer_inputs(lis):
        return {
            "spk": np.stack([packs[li][0] for li in lis]), "wsT": np.stack([packs[li][1] for li in lis]),
            "mb": np.stack([packs[li][2] for li in lis]),
            "w_ada": inp["w_ada"][lis], "w_in": inp["w_in"][lis], "w_out": inp["w_out"][lis],
            "w_up": inp["ffn_up"][lis], "w_dn": inp["ffn_down"][lis],
        }
    common = {"cbf": cbf, "rope": rope, "CN": CN, "SN": SN}
    if FUSED:
        nc = _get_prog(DEPTH, False)
        lw = layer_inputs(list(range(DEPTH)))
        in_maps = [dict(common, xT=xT[b], cxT=cT[b], cpk=cpk[b], **lw) for b in range(B)]
        res = run_bass_kernel_spmd(nc, in_maps, core_ids=list(range(B)))
        outs = [r["outT"] for r in res.results]
    else:
        nc = _get_prog(1, True)
        for li in range(DEPTH):
            lw = layer_inputs([li])
            in_maps = [dict(common, xT=xT[b], cxT=cT[b], cpk=cpk[b], **lw) for b in range(B)]
            res = run_bass_kernel_spmd(nc, in_maps, core_ids=list(range(B)))
            xT = [np.ascontiguousarray(r["outT"]) for r in res.results]
            cT = [np.ascontiguousarray(r["coutT"]) for r in res.results]
        outs = xT
    return np.stack([np.ascontiguousarray(o.T) for o in outs]).astype(np.float32)
```

```python
import math
from contextlib import ExitStack
import numpy as np
import ml_dtypes
import concourse.bass as bass
import concourse.mybir as mybir
from concourse.bass_utils import run_bass_kernel_spmd

F32 = mybir.dt.float32
BF16 = mybir.dt.bfloat16
AF = mybir.ActivationFunctionType
ALU = mybir.AluOpType
AX = mybir.AxisListType

DEPTH = 4
FFN_VARIANT = 0
SKIPW = 0
NA_WARM = 0
PRECISE_FENCE = False
PE_WARM = 1
NA_ACT_RECIP = True
EVAC_POOL = 3
SES_ENGINES = ("act", "dve", "pool")
VCOPIES = 4
NT = 2304
EPS = 1e-6
NS = 888
O_BADA, O_GMIX, O_GFFN = 0, 48, 56
O_DQN, O_DKN, O_SUBLN, O_NQN, O_NKN, O_LAMI, O_OMLI = 64, 65, 66, 67, 68, 69, 70
O_LAM, O_GN, O_BST, O_CW, O_CB = 72, 200, 456, 712, 844
C_ONESD, C_B32, C_B64, C_PROT, C_CS64, C_C256, C_S256, NCB = 0, 128, 256, 384, 512, 1536, 2048, 2560
FFN_PASSES = [(0, 4), (4, 4), (8, 4), (12, 4), (16, 4), (20, 2)]


class Sched:
    ENGS = ("pe", "act", "dve", "pool", "sp")

    def __init__(self, nc, stack):
        self.nc, self.stack = nc, stack
        self.ops = {e: [] for e in self.ENGS}
        self.cnt = {}
        self.known = {e: {} for e in self.ENGS}
        self.last_w = {}
        self.readers = {}
        self.sems = {}
        self.fence_vals = {}
        self.stopped = False
        self.owner = {}
        self.scopes = []

    def open_scope(self):
        self.scopes.append({"keys": set(), "max": {}})

    def close_scope(self):
        sc = self.scopes.pop()
        for sk, v in sc["max"].items():
            if self.fence_vals.get(sk, 0) < v:
                self.fence_vals[sk] = v
        for k in sc["keys"]:
            self.owner.pop(k, None)
            self.last_w.pop(k, None)
            self.readers.pop(k, None)
        if self.scopes:
            up = self.scopes[-1]["max"]
            for sk, v in sc["max"].items():
                if up.get(sk, 0) < v:
                    up[sk] = v

    def sem(self, key):
        if key not in self.sems:
            self.sems[key] = self.stack.enter_context(self.nc.semaphore("s%d" % len(self.sems)))
        return self.sems[key]

    def op(self, eng, fn, reads=(), writes=(), dma=None):
        if self.stopped:
            return
        waits = {}

        def need(sk, val):
            if sk == ("eng", eng) and eng not in SES_ENGINES:
                return
            if self.known[eng].get(sk, 0) >= val:
                return
            if waits.get(sk, 0) < val:
                waits[sk] = val

        local = []
        for k in list(reads) + list(writes):
            if k not in self.owner:
                self.owner[k] = self.scopes[-1] if self.scopes else None
                if self.scopes:
                    self.scopes[-1]["keys"].add(k)
            if self.owner[k] is not None:
                local.append(self.owner[k])
        if local or not PRECISE_FENCE:
            for sk, v in self.fence_vals.items():
                need(sk, v)
        for k in reads:
            if k in self.last_w:
                need(*self.last_w[k])
        for k in writes:
            if k in self.last_w:
                need(*self.last_w[k])
            for sk, v in self.readers.get(k, {}).items():
                need(sk, v)
        for sk, v in waits.items():
            self.known[eng][sk] = v
        sk = ("dma", dma) if dma is not None else ("eng", eng)
        self.sem(sk)
        inc = 16 if dma is not None else 1
        self.cnt[sk] = self.cnt.get(sk, 0) + inc
        val = self.cnt[sk]
        for k in reads:
            d = self.readers.setdefault(k, {})
            d[sk] = max(d.get(sk, 0), val)
        for k in writes:
            self.last_w[k] = (sk, val)
            self.readers[k] = {}
        if PRECISE_FENCE:
            for scd in local:
                scd["max"][sk] = val
        else:
            for scd in self.scopes:
                scd["max"][sk] = val
        self.ops[eng].append((sorted(waits.items()), fn, sk, inc))

    def finish(self):
        self.ops["sp"].append((sorted(self.cnt.items()), None, None, 0))

    def emit(self, block):
        engs = {"pe": "tensor", "act": "scalar", "dve": "vector", "pool": "gpsimd", "sp": "sync"}

        def mk(ename):
            def body(e):
                for waits, fn, sk, inc in self.ops[ename]:
                    for wk, v in waits:
                        e.wait_ge(self.sems[wk], v)
                    if fn is not None:
                        fn(e).then_inc(self.sems[sk], inc)
            return body

        for ename, attr in engs.items():
            if self.ops[ename]:
                getattr(block, attr)(mk(ename))


def cols(tb):
    return (tb * 512, 512) if tb < 4 else (2048, 256)


def na_local(m):
    if 2 <= m <= 13:
        return m - 2, 5, 0
    e = {0: 0, 1: 1, 14: 2, 15: 3}[m]
    return (0 if m < 2 else 12), 4, 5 + 4 * e


class _Stop(Exception):
    pass


def build(n_layers, ctx_last=False, dbg=None, stop=None):
    L = n_layers
    nc = bass.Bass("TRN2", target_bir_lowering=False)
    D = lambda name, shape, dt, kind="ExternalInput": nc.dram_tensor(name, shape, dt, kind=kind).ap()
    xT_d = D("xT", [1024, 2048], F32)
    cxT_d = D("cxT", [1024, 256], F32)
    cpk_d = D("cpk", [128, 16], F32)
    cb_d = D("cbf", [128, NCB], BF16)
    rope_d = D("rope", [128, 2, 2048], F32)
    CN_d = D("CN", [2048, 2048], BF16)
    SN_d = D("SN", [2048, 2048], BF16)
    sp_d = D("spk", [L, 128, NS], F32)
    ws_d = D("wsT", [L, 128, 512], F32)
    mb_d = D("mb", [L, 128, 4 * 21 * 128], F32)
    wada_d = D("w_ada", [L, 1024, 6144], F32)
    win_d = D("w_in", [L, 1024, 2304], F32)
    wout_d = D("w_out", [L, 1024, 1024], F32)
    wup_d = D("w_up", [L, 1024, 5632], F32)
    wdn_d = D("w_dn", [L, 2816, 1024], F32)
    out_d = D("outT", [1024, 2048], F32, "ExternalOutput")
    cout_d = D("coutT", [1024, 256], F32, "ExternalOutput") if ctx_last else None
    dbg_outs = {}

    with ExitStack() as st:
        S = Sched(nc, st)

        tcnt = [0]

        class Scope(ExitStack):
            def __enter__(self):
                S.open_scope()
                return super().__enter__()

            def __exit__(self, *a):
                r = super().__exit__(*a)
                S.close_scope()
                return r

        def T(stack, name, shape, dt):
            tcnt[0] += 1
            return stack.enter_context(nc.sbuf_tensor("sb%d_%s" % (tcnt[0], name), shape, dt))

        PS = st.enter_context(nc.psum_tensor("PS", [128, 8, 512], F32))
        PSF = PS[:, :, :].rearrange("p b n -> p (b n)")
        xT = T(st, "xT", [128, 8, NT], F32)
        hT = T(st, "hT", [128, 8, NT], BF16)
        CB = T(st, "CB", [128, NCB], BF16)
        SPK = T(st, "SPK", [128, 1, NS], F32)
        MODS = T(st, "MOD", [128, 2, 48, 2], F32)
        AMS = T(st, "AM", [128, 2, 2, 8, 2], F32)
        ADAP = T(st, "ADAP", [128, 2, 64], F32)
        cpk = T(st, "cpk", [128, 8, 2], F32)
        sT = T(st, "sT", [128, 8, 2], BF16)
        lamt = T(st, "lamt", [128, 8], F32)

        bank_rr = [0]
        ov_early = out_d.rearrange("(c p) n -> p c n", p=128)
        stored_early = set()

        def bank(allowed=(0, 1, 2, 3, 4, 5, 6, 7)):
            bank_rr[0] += 1
            return allowed[bank_rr[0] % len(allowed)]

        def mm(out_ap, pairs, reads, writes, tile_position=None):
            def fn(e, pairs=pairs, out_ap=out_ap):
                n = len(pairs)
                for i, (l, r) in enumerate(pairs):
                    kw = {} if tile_position is None else {"tile_position": tile_position}
                    inst = e.matmul(out_ap, lhsT=l, rhs=r, start=(i == 0), stop=(i == n - 1), **kw)
                return inst
            S.op("pe", fn, reads, writes)

        def mm_multi(groups, reads, writes):
            def fn(e, groups=groups):
                for out_ap, pairs, tp in groups:
                    n = len(pairs)
                    for i, (l, r) in enumerate(pairs):
                        kw = {} if tp is None else {"tile_position": tp}
                        inst = e.matmul(out_ap, lhsT=l, rhs=r, start=(i == 0), stop=(i == n - 1), **kw)
                return inst
            S.op("pe", fn, reads, writes)

        def act(out, in_, func, reads, writes, **kw):
            S.op("act", lambda e: e.activation(out=out, in_=in_, func=func, **kw), reads, writes)

        def dma(eng, out, in_, reads, writes, sem):
            S.op(eng, lambda e: e.dma_start(out=out, in_=in_), reads, writes, dma=sem)

        def tt(eng, out, in0, in1, op, reads, writes):
            S.op(eng, lambda e: e.tensor_tensor(out=out, in0=in0, in1=in1, op=op), reads, writes)

        def stt(out, in0, scalar, in1, op0, op1, reads, writes):
            S.op("dve", lambda e: e.scalar_tensor_tensor(out=out, in0=in0, scalar=scalar, in1=in1, op0=op0, op1=op1),
                 reads, writes)

        def ts(eng, out, in0, s1, s2, op0, op1, reads, writes):
            if op1 is None:
                S.op(eng, lambda e: e.tensor_scalar(out=out, in0=in0, scalar1=s1, scalar2=None, op0=op0), reads, writes)
            else:
                S.op(eng, lambda e: e.tensor_scalar(out=out, in0=in0, scalar1=s1, scalar2=s2, op0=op0, op1=op1), reads, writes)

        def copy(eng, out, in_, reads, writes):
            if eng == "act":
                S.op("act", lambda e: e.activation(out=out, in_=in_, func=AF.Identity), reads, writes)
            else:
                S.op(eng, lambda e: e.tensor_copy(out=out, in_=in_), reads, writes)

        def recip(out, in_, reads, writes):
            S.op("dve", lambda e: e.reciprocal(out=out, in_=in_), reads, writes)

        def dbg_tap(name, ap, shape, dt, reads):
            if dbg is None or name not in dbg or S.stopped:
                return
            d = D("dbg_" + name, list(shape), dt, "ExternalOutput")
            dbg_outs[name] = d
            dma("sp", d, ap, reads, [], "dbg_" + name)

        ev_rr = [0]

        def ev_eng():
            ev_rr[0] += 1
            return "act" if ev_rr[0] % 2 else "dve"

        def xkeys(tb):
            return [("x", c, tb) for c in range(8)]

        def hkeys(tb):
            return [("h", c, tb) for c in range(8)]

        def tt_hkeys(t):
            return hkeys(t // 4 if t < 16 else 4)

        dma("sp", CB[:, :], cb_d[:, :], [], ["cb"], "cb")
        dma("sp", cpk[:, :, :].rearrange("p c v -> p (c v)"), cpk_d[:, :], [], ["cpk"], "cpk")
        xv = xT_d.rearrange("(c p) n -> p c n", p=128)
        cv = cxT_d.rearrange("(c p) n -> p c n", p=128)
        for c in range(8):
            dma("sp", xT[:, c, 0:2048], xv[:, c, :], [], [("x", c, tb) for tb in range(4)], "xin%d" % c)
        dma("sp", xT[:, :, 2048:2304], cv[:, :, :], [], xkeys(4), "cin")
        act(sT[:, :, :], cpk[:, :, :], AF.Silu, ["cpk"], ["sT"])
        ONESD = CB[:, C_ONESD:C_ONESD + 128]
        B32 = CB[:, C_B32:C_B32 + 128]
        B64 = CB[:, C_B64:C_B64 + 128]
        PROT = CB[:, C_PROT:C_PROT + 128]

        def checkpoint(k):
            if stop is not None and stop == k:
                S.stopped = True

        for li in range(L):
          try:
              ctx_out = (li < L - 1) or ctx_last
              tbs_all = [0, 1, 2, 3, 4]
              tbs_o = tbs_all if ctx_out else [0, 1, 2, 3]
              ntt_o = 18 if ctx_out else 16
              sp = SPK[:, 0, :]

              def spc(o, n=1, sp=sp):
                  return sp[:, o:o + n]
              dma("sp", SPK[:, 0, :], sp_d[li, :, :], [], [("sp", 0)], "sp0")
              SPR = [("sp", 0)]

              MOD = MODS[:, li % 2]
              AM = AMS[:, li % 2]
              MODK = ("mod", li % 2)
              AMK = ("am", li % 2)

              def ada_load(lt, jg, wa):
                  wav = wada_d[lt].rearrange("(c p) n -> p c n", p=128)
                  dma("pool", wa[jg % len(wa)][:, :, :], wav[:, :, jg * 512:(jg + 1) * 512], [], [("wa", jg % len(wa))], "wa%d" % (jg % len(wa)))

              def ada_params(lt):
                  dma("sp", ADAP[:, lt % 2, :], sp_d[lt, :, 0:64], [], [("adap", lt % 2)], "adap%d" % (lt % 2))

              def ada_piece(lt, jg, wa, bk):
                  wb = wa[jg % len(wa)]
                  groups = []
                  for j4 in range(4):
                      groups.append((PS[:, bk, j4 * 2:j4 * 2 + 2],
                                     [(wb[:, c, j4 * 128:(j4 + 1) * 128], sT[:, c, :]) for c in range(8)], None))
                  mm_multi(groups, [("wa", jg % len(wa)), "sT"], [("pb", bk)])
                  psm = PS[:, bk, 0:8].rearrange("p (j v) -> p j v", v=2)
                  for v in range(2):
                      tt("dve", MODS[:, lt % 2, jg * 4:(jg + 1) * 4, v], psm[:, :, v], ADAP[:, lt % 2, O_BADA + jg * 4:O_BADA + (jg + 1) * 4], ALU.add,
                         [("pb", bk), ("adap", lt % 2)], [("mod", lt % 2)])

              def ada_finish(lt, whichs=(0, 1)):
                  for which, (so, go) in enumerate(((8, O_GMIX), (32, O_GFFN))):
                      if which not in whichs:
                          continue
                      for v in range(2):
                          stt(AMS[:, lt % 2, which, :, v], MODS[:, lt % 2, so:so + 8, v], 1.0, ADAP[:, lt % 2, go:go + 8], ALU.add, ALU.mult,
                              [("mod", lt % 2), ("adap", lt % 2)], [("am", lt % 2)])

              if li == 0:
                  scP = Scope()
                  scP.__enter__()
                  wa0 = [T(scP, "wa%d" % i, [128, 8, 512], BF16) for i in range(6)]
                  ada_params(0)
                  for jg in range(4):
                      ada_load(0, jg, wa0)
                  for jg in range(4):
                      ada_piece(0, jg, wa0, 7)
                  ada_finish(0, (0,))
                  for jg in range(4, 10):
                      ada_load(0, jg, wa0)
              checkpoint(1)
              dbg_tap("mod%d" % li, MOD[:, :, :], [128, 48, 2], F32, [MODK])

              def shift_ap(which, c, v):
                  s = (0, 24)[which] + c
                  return MOD[:, s, v:v + 1]

              def gate_ap(which, c, v):
                  s = (16, 40)[which] + c
                  return MOD[:, s, v:v + 1]

              def norm_phase(which, tbs):
                  with Scope() as sc:
                      sq = [T(sc, "sq%d" % i, [128, 8, 512], BF16) for i in range(2)]
                      rstd = [T(sc, "rstd%d" % i, [128, 512], F32) for i in range(2)]
                      tmp = [T(sc, "ntmp%d" % i, [128, 512], F32) for i in range(3)]

                      def n_a(k):
                          tb = tbs[k]
                          c0, n = cols(tb)
                          q2 = k % 2
                          act(sq[q2][:, :, 0:n], xT[:, :, c0:c0 + n], AF.Square, xkeys(tb), [("sq", q2)])
                          bk = bank()
                          mm(PS[:, bk, 0:n], [(ONESD, sq[q2][:, c, 0:n]) for c in range(8)], [("sq", q2), "cb"], [("pb", bk)])
                          act(rstd[q2][:, 0:n], PS[:, bk, 0:n], AF.Ln, [("pb", bk)], [("rstd", q2)], bias=EPS)
                          act(rstd[q2][:, 0:n], rstd[q2][:, 0:n], AF.Exp, [("rstd", q2)], [("rstd", q2)], scale=-0.5)

                      def n_b(k):
                          tb = tbs[k]
                          c0, n = cols(tb)
                          v = 0 if tb < 4 else 1
                          q2 = k % 2
                          for c in range(8):
                              t = tmp[c % 3]
                              tk = ("ntmp", c % 3)
                              tt("dve", t[:, 0:n], xT[:, c, c0:c0 + n], rstd[q2][:, 0:n], ALU.mult, [("x", c, tb), ("rstd", q2)], [tk])
                              act(hT[:, c, c0:c0 + n], t[:, 0:n], AF.Identity, [tk, AMK, MODK], [("h", c, tb)],
                                  scale=AM[:, which, c, v:v + 1], bias=shift_ap(which, c, v))

                      n_a(0)
                      for k in range(len(tbs)):
                          if k + 1 < len(tbs):
                              n_a(k + 1)
                          n_b(k)

              norm_phase(0, tbs_all)
              if li == 0:
                  for jg in range(4, 12):
                      ada_piece(0, jg, wa0, 7)
                      if jg + 6 < 12:
                          ada_load(0, jg + 6, wa0)
                  ada_finish(0, (1,))
                  scP.__exit__(None, None, None)
              checkpoint(2)
              dbg_tap("h1_%d" % li, hT[:, :, :], [128, 8, NT], BF16, [k for tb in range(5) for k in hkeys(tb)])

              winv = win_d[li].rearrange("(c p) n -> p c n", p=128)
              woutv = wout_d[li].rearrange("(c p) n -> p c n", p=128)

              def wout_partial(sc, ym, mix, ykeys_fn):
                  wout_multi(sc, [(ym, ykeys_fn)], mix)

              def wout_multi(sc, yms, mix):
                  nch = 2 * len(yms)
                  wo = T(sc, "wo", [128, nch, 1024], BF16)
                  wtmp = [T(sc, "wtmp%d" % i, [128, 512], F32) for i in range(3)]
                  wev = [0]
                  for hlf in range(2):
                      dma("pool", wo[:, :, hlf * 512:(hlf + 1) * 512], woutv[:, 2 * mix:2 * mix + nch, hlf * 512:(hlf + 1) * 512],
                          [], ["wo"], "wo")
                  for tb in tbs_o:
                      c0, n = cols(tb)
                      v = 0 if tb < 4 else 1
                      for oc in range(8):
                          bk = bank()
                          pairs = []
                          rkeys = ["wo"]
                          for yi, (ym_, kf) in enumerate(yms):
                              for c in range(2):
                                  pairs.append((wo[:, 2 * yi + c, oc * 128:(oc + 1) * 128], ym_[:, c, c0:c0 + n]))
                              rkeys += kf(tb)
                          mm(PS[:, bk, 0:n], pairs, rkeys, [("pb", bk)])
                          wev[0] += 1
                          if EVAC_POOL and wev[0] % EVAC_POOL == 0:
                              ei = (wev[0] // EVAC_POOL) % 3
                              act(wtmp[ei][:, 0:n], PS[:, bk, 0:n], AF.Identity, [("pb", bk), MODK], [("wtmp", ei)], scale=gate_ap(0, oc, v))
                              tt("pool", xT[:, oc, c0:c0 + n], xT[:, oc, c0:c0 + n], wtmp[ei][:, 0:n], ALU.add, [("wtmp", ei), ("x", oc, tb)], [("x", oc, tb)])
                          else:
                              stt(xT[:, oc, c0:c0 + n], PS[:, bk, 0:n], gate_ap(0, oc, v), xT[:, oc, c0:c0 + n], ALU.mult, ALU.add,
                                  [("pb", bk), MODK, ("x", oc, tb)], [("x", oc, tb)])

              def proj_fm(wt, wkey, wcol, tbs, evac):
                  for tb in tbs:
                      c0, n = cols(tb)
                      bk = bank()
                      mm(PS[:, bk, 0:n], [(wt[:, c, wcol:wcol + 128], hT[:, c, c0:c0 + n]) for c in range(8)],
                         [wkey] + hkeys(tb), [("pb", bk)])
                      evac(bk, tb, c0, n)

              with Scope() as sc:
                  ym = T(sc, "ymA", [128, 2, NT], BF16)
                  pf = T(sc, "pf", [128, 2, NT], BF16)
                  AB = T(sc, "AB", [128, 18, 512], BF16)
                  wf = T(sc, "wf", [128, 8, 256], BF16)
                  CNb = [T(sc, "CNb%d" % i, [128, 16, 256], BF16) for i in range(2)]
                  SNb = [T(sc, "SNb%d" % i, [128, 16, 256], BF16) for i in range(2)]
                  dma("pool", wf[:, :, :], winv[:, :, 0:256], [], ["wf"], "wf")
                  for oc in range(2):
                      def ev(bk, tb, c0, n, oc=oc):
                          copy(ev_eng(), pf[:, oc, c0:c0 + n], PS[:, bk, 0:n], [("pb", bk)], [("pf", oc, tb)])
                      proj_fm(wf, "wf", oc * 128, tbs_o, ev)
                  for t in range(ntt_o):
                      tb = t // 4 if t < 16 else 4
                      bk = bank()
                      mm(PS[:, bk, :], [(pf[:, c, t * 128:(t + 1) * 128], CB[:, C_CS64 + c * 512:C_CS64 + (c + 1) * 512]) for c in range(2)],
                         [("pf", 0, tb), ("pf", 1, tb), "cb"], [("pb", bk)])
                      copy(ev_eng(), AB[:, t, :], PS[:, bk, :], [("pb", bk)], [("AB", t)])
                  CNv = CN_d.rearrange("(t p) k -> p t k", p=128)
                  SNv = SN_d.rearrange("(t p) k -> p t k", p=128)
                  def ft_load(kb):
                      dma("sp", CNb[kb % 2][:, :, :], CNv[:, :, kb * 256:(kb + 1) * 256], [], [("CNb", kb % 2)], "CNb%d" % (kb % 2))
                      dma("sp", SNb[kb % 2][:, :, :], SNv[:, :, kb * 256:(kb + 1) * 256], [], [("SNb", kb % 2)], "SNb%d" % (kb % 2))
                  for kb in range(8):
                      if kb == 0:
                          ft_load(0)
                          ft_load(1)
                      for lc in range(2):
                          bk = bank()
                          pairs = []
                          for t in range(16):
                              pairs.append((AB[:, t, lc * 128:(lc + 1) * 128], CNb[kb % 2][:, t, :]))
                              pairs.append((AB[:, t, 256 + lc * 128:256 + (lc + 1) * 128], SNb[kb % 2][:, t, :]))
                          mm(PS[:, bk, 0:256], pairs, [("AB", t) for t in range(16)] + [("CNb", kb % 2), ("SNb", kb % 2)], [("pb", bk)])
                          copy(ev_eng(), ym[:, lc, kb * 256:(kb + 1) * 256], PS[:, bk, 0:256], [("pb", bk)], [("ymA", kb // 2)])
                      if kb + 2 < 8:
                          ft_load(kb + 2)
                  if ctx_out:
                      for lc in range(2):
                          bk = bank()
                          pairs = []
                          for t in range(2):
                              pairs.append((AB[:, 16 + t, lc * 128:(lc + 1) * 128], CB[:, C_C256 + t * 256:C_C256 + (t + 1) * 256]))
                              pairs.append((AB[:, 16 + t, 256 + lc * 128:256 + (lc + 1) * 128], CB[:, C_S256 + t * 256:C_S256 + (t + 1) * 256]))
                          mm(PS[:, bk, 0:256], pairs, [("AB", 16), ("AB", 17), "cb"], [("pb", bk)])
                          copy(ev_eng(), ym[:, lc, 2048:2304], PS[:, bk, 0:256], [("pb", bk)], [("ymA", 4)])
                  dbg_tap("yA_%d" % li, ym[:, :, :], [128, 2, NT], BF16, [("ymA", k) for k in range(5)])
                  wout_partial(sc, ym, 0, lambda tb: [("ymA", tb)])

              checkpoint(3)
              with Scope() as sc:
                  ym = T(sc, "ymB", [128, 2, NT], BF16)
                  qk = T(sc, "qk", [128, 4, NT], BF16)
                  VA = T(sc, "VA", [128, 18, 4, 128], BF16)
                  S.op("pool", lambda e: e.memset(VA[:, :, :, :].rearrange("p a h d -> p (a h d)"), 1.0), [], ["VAones"])
                  with Scope() as sc2:
                      wqk = T(sc2, "wqk", [128, 8, 512], BF16)
                      wv = T(sc2, "wvB", [128, 8, 256], BF16)
                      ROPE = [T(sc2, "ROPE%d" % i, [128, 2, 512], F32) for i in range(2)]
                      sqb = [T(sc2, "sqb%d" % i, [128, 512], BF16) for i in range(2)]
                      ub = [T(sc2, "ub%d" % i, [128, 512], BF16) for i in range(2)]
                      rs = [T(sc2, "rsB%d" % i, [128, 512], F32) for i in range(2)]
                      t1 = [T(sc2, "t1B%d" % i, [128, 512], F32) for i in range(2)]
                      t2 = [T(sc2, "t2B%d" % i, [128, 512], F32) for i in range(2)]
                      dma("pool", wqk[:, :, :], winv[:, :, 256:768], [], ["wqk"], "wqk")
                      dma("pool", wv[:, :, :], winv[:, :, 768:1024], [], ["wvB"], "wvB")
                      it = 0
                      blocks = []
                      for tb in tbs_all:
                          for oi in range(4):
                              if oi < 2 and tb == 4 and not ctx_out:
                                  continue
                              blocks.append((tb, oi))
                      rope_loaded = set()
                      pbank = {}

                      def qk_a(bi):
                          tb, oi = blocks[bi]
                          c0, n = cols(tb)
                          i2 = bi % 2
                          rp = ROPE[tb % 2]
                          rkk = ("rope", tb % 2)
                          if tb < 4 and tb not in rope_loaded:
                              rope_loaded.add(tb)
                              dma("sp", rp[:, :, :], rope_d[:, :, c0:c0 + n], [], [rkk], "rope%d" % (tb % 2))
                          gcol = O_DQN if oi < 2 else O_DKN
                          bk = bank()
                          mm(PS[:, bk, 0:n], [(wqk[:, c, oi * 128:(oi + 1) * 128], hT[:, c, c0:c0 + n]) for c in range(8)],
                             ["wqk"] + hkeys(tb), [("pb", bk)])
                          act(sqb[i2][:, 0:n], PS[:, bk, 0:n], AF.Square, [("pb", bk)], [("sqb", i2)])
                          act(ub[i2][:, 0:n], PS[:, bk, 0:n], AF.Identity, [("pb", bk)] + SPR, [("ub", i2)], scale=spc(gcol))
                          bk2 = bank()
                          mm(PS[:, bk2, 0:n], [(B32, sqb[i2][:, 0:n])], [("sqb", i2), "cb"], [("pb", bk2)])
                          if tb < 4:
                              bk3 = bank()
                              pbank[bi] = bk3
                              mm(PS[:, bk3, 0:n], [(PROT, ub[i2][:, 0:n])], [("ub", i2), "cb"], [("pb", bk3)])
                          act(rs[i2][:, 0:n], PS[:, bk2, 0:n], AF.Ln, [("pb", bk2)], [("rsB", i2)], bias=EPS)
                          act(rs[i2][:, 0:n], rs[i2][:, 0:n], AF.Exp, [("rsB", i2)], [("rsB", i2)], scale=-0.5)

                      def qk_b(bi):
                          tb, oi = blocks[bi]
                          c0, n = cols(tb)
                          i2 = bi % 2
                          rp = ROPE[tb % 2]
                          rkk = ("rope", tb % 2)
                          if tb < 4:
                              bk3 = pbank[bi]
                              tt("dve", t1[i2][:, 0:n], ub[i2][:, 0:n], rp[:, 0, 0:n], ALU.mult, [("ub", i2), rkk], [("t1B", i2)])
                              tt("dve", t2[i2][:, 0:n], PS[:, bk3, 0:n], rp[:, 1, 0:n], ALU.mult, [("pb", bk3), rkk], [("t2B", i2)])
                              tt("pool", t1[i2][:, 0:n], t1[i2][:, 0:n], t2[i2][:, 0:n], ALU.add, [("t1B", i2), ("t2B", i2)], [("t1B", i2)])
                              tt("dve", qk[:, oi, c0:c0 + n], t1[i2][:, 0:n], rs[i2][:, 0:n], ALU.mult, [("t1B", i2), ("rsB", i2)], [("qk", oi, tb)])
                          else:
                              tt("dve", qk[:, oi, c0:c0 + n], ub[i2][:, 0:n], rs[i2][:, 0:n], ALU.mult, [("ub", i2), ("rsB", i2)], [("qk", oi, tb)])

                      qk_a(0)
                      for bi in range(len(blocks)):
                          if bi + 1 < len(blocks):
                              qk_a(bi + 1)
                          qk_b(bi)
                      checkpoint(3.1)
                      for t in range(18):
                          bk = bank()
                          mm(PS[:, bk, 0:256], [(hT[:, c, t * 128:(t + 1) * 128], wv[:, c, :]) for c in range(8)],
                             ["wvB"] + tt_hkeys(t), [("pb", bk)])
                          for hh in range(VCOPIES):
                              off = 0 if hh % 2 == 0 else 64
                              copy("dve", VA[:, t, hh, off:off + 64], PS[:, bk, hh * 64:(hh + 1) * 64],
                                   [("pb", bk), "VAones"], [("VA", t, 0)])
                  dbg_tap("qkB_%d" % li, qk[:, :, :], [128, 4, NT], BF16, [("qk", oi, tb) for oi in range(4) for tb in range(5) if not (oi < 2 and tb == 4 and not ctx_out)])
                  checkpoint(3.2)
                  with Scope() as sc2:
                      lp = T(sc2, "lp", [128, 64], F32)
                      tt("dve", lp[:, 0:32], spc(O_LAM, 32), spc(O_LAM + 32, 32), ALU.mult, SPR, ["lp"])
                      tt("dve", lp[:, 32:64], spc(O_LAM + 64, 32), spc(O_LAM + 96, 32), ALU.mult, SPR, ["lp"])
                      S.op("dve", lambda e: e.tensor_reduce(out=lamt[:, 0:2], in_=lp[:, :].rearrange("p (a b) -> p a b", b=32), axis=AX.X, op=ALU.add),
                           ["lp"], ["lamt"])
                      act(lamt[:, 2:4], lamt[:, 0:2], AF.Exp, ["lamt"], ["lamt"])
                      tt("dve", lamt[:, 4:5], lamt[:, 2:3], lamt[:, 3:4], ALU.subtract, ["lamt"], ["lamt"])
                      tt("dve", lamt[:, 5:6], lamt[:, 4:5], spc(O_LAMI), ALU.add, ["lamt"] + SPR, ["lamt"])
                      ts("dve", lamt[:, 6:7], lamt[:, 5:6], -1.0, None, ALU.mult, None, ["lamt"], ["lamt"])
                  NEGLAM = lamt[:, 6:7]
                  checkpoint(3.3)
                  with Scope() as sc2:
                      Eb = [T(sc2, "Eb%d" % i, [128, 2, 512], BF16) for i in range(3)]
                      R = [T(sc2, "Rr%d" % i, [128, 512], F32) for i in range(2)]
                      o1 = T(sc2, "o1", [128, 512], F32)
                      o2 = T(sc2, "o2", [128, 512], F32)
                      OP = T(sc2, "OP", [128, 512], F32)
                      sqo = T(sc2, "sqo", [128, 512], BF16)
                      rso = T(sc2, "rso", [128, 512], F32)
                      SCALE = 32.0 ** -0.5
                      steps = []
                      pairs_l = [(qb, hp, 18 if qb < 4 else 2) for qb in tbs_o for hp in range(2)]
                      for pj, (qb, hp, nk_) in enumerate(pairs_l):
                          kcs = list(range(18)) if qb < 4 else [16, 17]
                          for hl in range(2):
                              for ki, kc in enumerate(kcs):
                                  steps.append(("kv", qb, hp, hl, ki, kc, len(kcs)))
                          nxt = pairs_l[pj + 1][2] if pj + 1 < len(pairs_l) else 0
                          steps.append(("subln", qb, hp, 10 if nxt == 18 else 0))
                      order = []
                      pending = []
                      for st_ in steps:
                          pending = [(d - 1, x) for d, x in pending]
                          if st_[0] == "subln":
                              pending.append((st_[3], st_))
                          else:
                              order.append(st_)
                          for d, x in list(pending):
                              if d <= 0:
                                  order.append(x)
                                  pending.remove((d, x))
                      order += [x for _, x in pending]
                      steps = order
                      wa_n = None
                      if li + 1 < L:
                          wa_n = [T(sc2, "wan%d" % i, [128, 8, 512], BF16) for i in range(2)]
                          ada_params(li + 1)
                          ada_load(li + 1, 0, wa_n)
                          ada_load(li + 1, 1, wa_n)
                          order = []
                          nkv = 0
                          jg_n = 0
                          for st_ in steps:
                              order.append(st_)
                              if st_[0] == "kv":
                                  nkv += 1
                                  if nkv % 22 == 0 and jg_n < 12:
                                      order.append(("ada", jg_n))
                                      jg_n += 1
                          while jg_n < 12:
                              order.append(("ada", jg_n))
                              jg_n += 1
                          steps = order

                      skip_warm = set()
                      for i_, st_ in enumerate(steps):
                          if st_[0] == "ada":
                              skip_warm.update(range(i_ + 1, i_ + 1 + SKIPW))

                      def stage1(i):
                          stp = steps[i]
                          sb = (0, 2)[i % 2]
                          if stp[0] == "kv":
                              _, qb, hp, hl, ki, kc, nk = stp
                              q0, qn = cols(qb)
                              pb = hl * 64
                              eb = Eb[i % 3]
                              ek = ("Eb", i % 3)
                              ktb = kc // 4 if kc < 16 else 4
                              groups = []
                              for c2 in range(2):
                                  p0 = pb + 32 * c2
                                  groups.append((PS[:, sb + c2, 0:qn],
                                                 [(qk[p0:p0 + 32, 2 + hp, kc * 128:(kc + 1) * 128], qk[p0:p0 + 32, hp, q0:q0 + qn])],
                                                 (p0, 0)))
                              if PE_WARM and i not in skip_warm:
                                  groups = [(PS[:, sb + c2, 0:512], [(VA[:, 0, 0, :], qk[:, 0, 0:512])], None) for c2 in range(PE_WARM)] + groups
                              mm_multi(groups, [("qk", 2 + hp, ktb), ("qk", hp, qb)], [("pb", sb), ("pb", sb + 1)])
                              act(eb[:, :, 0:qn], PS[:, sb:sb + 2, 0:qn], AF.Exp, [("pb", sb), ("pb", sb + 1)], [ek], scale=SCALE)
                      def stage2(i):
                          stp = steps[i]
                          sb = (0, 2)[i % 2]
                          if stp[0] == "ada":
                              jg = stp[1]
                              ada_piece(li + 1, jg, wa_n, sb)
                              if jg + 2 < 12:
                                  ada_load(li + 1, jg + 2, wa_n)
                              if jg == 11:
                                  ada_finish(li + 1)
                              return
                          if stp[0] != "kv":
                              _, qb, hp, _d = stp
                              q0, qn = cols(qb)
                              act(sqo[:, 0:qn], OP[:, 0:qn], AF.Square, [("OP", 0), ("OP", 1)], ["sqo"])
                              mm(PS[:, sb, 0:qn], [(B64, sqo[:, 0:qn])], ["sqo", "cb"], [("pb", sb)])
                              act(rso[:, 0:qn], PS[:, sb, 0:qn], AF.Ln, [("pb", sb)], ["rso"], bias=EPS)
                              act(rso[:, 0:qn], rso[:, 0:qn], AF.Exp, ["rso"], ["rso"], scale=-0.5)
                              stt(OP[:, 0:qn], OP[:, 0:qn], spc(O_OMLI), rso[:, 0:qn], ALU.mult, ALU.mult,
                                  [("OP", 0), ("OP", 1), "rso"] + SPR, [("OP", 0), ("OP", 1)])
                              act(ym[:, hp, q0:q0 + qn], OP[:, 0:qn], AF.Identity, [("OP", 0), ("OP", 1)] + SPR, [("ymB", qb)], scale=spc(O_SUBLN))
                              return
                          _, qb, hp, hl, ki, kc, nk = stp
                          q0, qn = cols(qb)
                          h = 2 * hp + hl
                          po, pz = (0, 64) if hl == 0 else (64, 0)
                          accb = (4, 5) if hl == 0 else (6, 7)
                          eb = Eb[i % 3]
                          ek = ("Eb", i % 3)
                          groups = [(PS[:, accb[c2], 0:qn], VA[:, kc, h, :], eb[:, c2, 0:qn]) for c2 in range(2)]

                          def fn(e, groups=groups, first=(ki == 0), last=(ki == nk - 1)):
                              for out_ap, l, r in groups:
                                  inst = e.matmul(out_ap, lhsT=l, rhs=r, start=first, stop=last)
                              return inst
                          S.op("pe", fn, [ek, ("VA", kc, 0), "VAones"], [("pb", accb[0]), ("pb", accb[1])])
                          if ki == nk - 1:
                              recip(R[0][po:po + 64, 0:qn], PS[pz:pz + 64, accb[0], 0:qn], [("pb", accb[0])], [("Rr", 0)])
                              recip(R[1][po:po + 64, 0:qn], PS[pz:pz + 64, accb[1], 0:qn], [("pb", accb[1])], [("Rr", 1)])
                              tt("dve", o1[po:po + 64, 0:qn], PS[po:po + 64, accb[0], 0:qn], R[0][po:po + 64, 0:qn], ALU.mult,
                                 [("pb", accb[0]), ("Rr", 0)], ["o1"])
                              tt("dve", o2[po:po + 64, 0:qn], PS[po:po + 64, accb[1], 0:qn], R[1][po:po + 64, 0:qn], ALU.mult,
                                 [("pb", accb[1]), ("Rr", 1)], ["o2"])
                              stt(OP[po:po + 64, 0:qn], o2[po:po + 64, 0:qn], NEGLAM[po:po + 64, :], o1[po:po + 64, 0:qn], ALU.mult, ALU.add,
                                  ["o1", "o2", "lamt"], [("OP", hl)])

                      nst = len(steps)
                      for i in range(nst + 2):
                          if i < nst:
                              stage1(i)
                          if i >= 2:
                              stage2(i - 2)
                  dbg_tap("yB_%d" % li, ym[:, :, :], [128, 2, NT], BF16, [("ymB", k) for k in tbs_o])
                  wout_partial(sc, ym, 1, lambda tb: [("ymB", tb)])

              checkpoint(4)
              scCD = Scope()
              scCD.__enter__()
              ymC_t = T(scCD, "ymC", [128, 2, NT], BF16)
              with Scope() as sc:
                  ym = ymC_t
                  wna = T(sc, "wna", [128, 8, 768], BF16)
                  dma("pool", wna[:, :, 0:512], winv[:, :, 1024:1536], [], ["wna"], "wna")
                  dma("pool", wna[:, :, 512:768], winv[:, :, 1536:1792], [], ["wna"], "wna")
                  mbv = mb_d[li].rearrange("p (h i q) -> p h i q", h=4, i=21)
                  for ci in range(2):
                      with Scope() as sc2:
                          qn_t = T(sc2, "qnC", [128, NT], BF16)
                          kn_t = T(sc2, "knC", [128, NT], BF16)
                          VN = T(sc2, "VN", [128, 18, 2, 128], BF16)
                          MB = T(sc2, "MB", [128, 2, 21, 128], BF16)
                          sqb = [T(sc2, "sqC%d" % i, [128, 512], BF16) for i in range(2)]
                          rs = [T(sc2, "rsC%d" % i, [128, 512], F32) for i in range(2)]
                          Es = [T(sc2, "EsC%d" % i, [128, 640], F32) for i in range(3)]
                          Eb = [T(sc2, "EbC%d" % i, [128, 896], BF16) for i in range(3)]
                          Rn = [T(sc2, "RnC%d" % i, [128, 128], F32) for i in range(3)]
                          S.op("pool", lambda e, VN=VN: e.memset(VN[:, :, :, :].rearrange("p a h d -> p (a h d)"), 1.0), [], ["VNones"])
                          for hl in range(2):
                              dma("pool", MB[:, hl, :, :], mbv[:, 2 * ci + hl, :, :], [], ["MB"], "MB")
                          it = 0
                          for isq in (True, False):
                              wcol = (0 if isq else 256) + ci * 128
                              dst = qn_t if isq else kn_t
                              gcol = O_NQN if isq else O_NKN
                              for tb in (tbs_o if isq else tbs_all):
                                  c0, n = cols(tb)
                                  i2 = it % 2
                                  it += 1
                                  bk = bank()
                                  mm(PS[:, bk, 0:n], [(wna[:, c, wcol:wcol + 128], hT[:, c, c0:c0 + n]) for c in range(8)],
                                     ["wna"] + hkeys(tb), [("pb", bk)])
                                  act(sqb[i2][:, 0:n], PS[:, bk, 0:n], AF.Square, [("pb", bk)], [("sqC", i2)])
                                  bk2 = bank()
                                  mm(PS[:, bk2, 0:n], [(B64, sqb[i2][:, 0:n])], [("sqC", i2), "cb"], [("pb", bk2)])
                                  act(rs[i2][:, 0:n], PS[:, bk2, 0:n], AF.Ln, [("pb", bk2)], [("rsC", i2)], bias=EPS)
                                  act(rs[i2][:, 0:n], rs[i2][:, 0:n], AF.Exp, [("rsC", i2)], [("rsC", i2)], scale=-0.5)
                                  stt(dst[:, c0:c0 + n], PS[:, bk, 0:n], spc(gcol), rs[i2][:, 0:n], ALU.mult, ALU.mult,
                                      [("pb", bk), ("rsC", i2)] + SPR, [("qnC" if isq else "knC", tb)])
                          for t in range(18):
                              bk = bank()
                              mm(PS[:, bk, 0:128], [(hT[:, c, t * 128:(t + 1) * 128], wna[:, c, 512 + ci * 128:512 + (ci + 1) * 128]) for c in range(8)],
                                 ["wna"] + tt_hkeys(t), [("pb", bk)])
                              copy("dve", VN[:, t, 0, 0:64], PS[:, bk, 0:64], [("pb", bk), "VNones"], [("VN", t, 0)])
                              copy("dve", VN[:, t, 1, 64:128], PS[:, bk, 64:128], [("pb", bk), "VNones"], [("VN", t, 1)])
                          nblk = 18 if ctx_out else 16
                          nsteps = []
                          for m in range(nblk):
                              for hl in range(2):
                                  nsteps.append((m, hl))

                          def na_info(m):
                              if m < 16:
                                  a0, nl, id0 = na_local(m)
                                  return nl, id0, list(range(a0, a0 + nl)) + [16, 17], m // 4
                              return 0, 0, [16, 17], 4

                          def na_stage1(i):
                              m, hl = nsteps[i]
                              nl, id0, chunks, qtb = na_info(m)
                              nch = len(chunks)
                              pb = hl * 64
                              s2 = i % 3
                              sbk = (0, 2, 4)[s2]
                              psS = PSF[:, sbk * 512:sbk * 512 + 1024]
                              groups = []
                              rk = [("qnC", qtb)]
                              for j, a in enumerate(chunks):
                                  groups.append((psS[:, j * 128:(j + 1) * 128],
                                                 [(kn_t[pb:pb + 64, a * 128:(a + 1) * 128], qn_t[pb:pb + 64, m * 128:(m + 1) * 128])],
                                                 (pb, 0)))
                                  rk.append(("knC", a // 4 if a < 16 else 4))
                              if NA_WARM:
                                  groups = [(PS[:, sbk, 0:512], [(VN[:, 0, 0, :], kn_t[:, 0:512])], None) for _ in range(NA_WARM)] + groups
                              mm_multi(groups, rk + ["VNones", ("VN", 0, 0)], [("pb", sbk), ("pb", sbk + 1)])
                              if nl:
                                  stt(Es[s2][:, 0:nl * 128], psS[:, 0:nl * 128], 0.125,
                                      MB[:, hl, id0:id0 + nl, :].rearrange("p i q -> p (i q)"), ALU.mult, ALU.add,
                                      [("pb", sbk), ("pb", sbk + 1), "MB"], [("EsC", s2)])
                                  act(Eb[s2][:, 0:nl * 128], Es[s2][:, 0:nl * 128], AF.Exp, [("EsC", s2)], [("EbC", s2)])
                              act(Eb[s2][:, nl * 128:nch * 128], psS[:, nl * 128:nch * 128], AF.Exp, [("pb", sbk), ("pb", sbk + 1)], [("EbC", s2)], scale=0.125)

                          def na_stage2(i):
                              m, hl = nsteps[i]
                              nl, id0, chunks, qtb = na_info(m)
                              po, pz = (0, 64) if hl == 0 else (64, 0)
                              s2 = i % 3
                              abk = (6, 7)[i % 2]
                              pairs = [(VN[:, a, hl, :], Eb[s2][:, j * 128:(j + 1) * 128]) for j, a in enumerate(chunks)]
                              mm(PS[:, abk, 0:128], pairs, [("EbC", s2), "VNones"] + [("VN", a, hl) for a in chunks], [("pb", abk)])
                              if NA_ACT_RECIP:
                                  act(Rn[s2][pz:pz + 64, :], PS[pz:pz + 64, abk, 0:128], AF.Ln, [("pb", abk)], [("RnC", s2)])
                                  act(Rn[s2][pz:pz + 64, :], Rn[s2][pz:pz + 64, :], AF.Exp, [("RnC", s2)], [("RnC", s2)], scale=-1.0)
                                  return
                              else:
                                  recip(Rn[s2][po:po + 64, :], PS[pz:pz + 64, abk, 0:128], [("pb", abk)], [("RnC", s2)])
                                  tt("dve", ym[po:po + 64, ci, m * 128:(m + 1) * 128], PS[po:po + 64, abk, 0:128], Rn[s2][po:po + 64, :], ALU.mult,
                                     [("pb", abk), ("RnC", s2)], [("ymC", qtb, ci, hl)])

                          def na_stage2b(i):
                              if not NA_ACT_RECIP:
                                  return
                              m, hl = nsteps[i]
                              nl, id0, chunks, qtb = na_info(m)
                              po, pz = (0, 64) if hl == 0 else (64, 0)
                              s2 = i % 3
                              abk = (6, 7)[i % 2]
                              tt("dve", ym[po:po + 64, ci, m * 128:(m + 1) * 128], PS[po:po + 64, abk, 0:128], Rn[s2][pz:pz + 64, :], ALU.mult,
                                 [("pb", abk), ("RnC", s2)], [("ymC", qtb, ci, hl)])

                          for i in range(len(nsteps) + 2):
                              if i >= 2:
                                  na_stage2(i - 2)
                              if i < len(nsteps):
                                  na_stage1(i)
                              if i >= 2:
                                  na_stage2b(i - 2)
                  dbg_tap("yC_%d" % li, ym[:, :, :], [128, 2, NT], BF16, [("ymC", k, ci, hl) for k in tbs_o for ci in range(2) for hl in range(2)])

              checkpoint(5)
              with Scope() as sc:
                  ym = T(sc, "ymD", [128, 2, NT], BF16)
                  uT = T(sc, "uT", [128, 2, NT], BF16)
                  wg = T(sc, "wg", [128, 8, 512], BF16)
                  wsb = T(sc, "wsb", [128, 4, 128], BF16)
                  vg = [T(sc, "vg%d" % i, [128, 256], F32) for i in range(3)]
                  vsq = [T(sc, "vsq%d" % i, [128, 256], F32) for i in range(3)]
                  vss = [T(sc, "vss%d" % i, [128, 4], F32) for i in range(3)]
                  vn = [T(sc, "vn%d" % i, [128, 256], BF16) for i in range(3)]
                  gt = [[T(sc, "gt%d%d" % (i, j), [128, 128], F32) for j in range(2)] for i in range(3)]
                  dma("pool", wg[:, :, :], winv[:, :, 1792:2304], [], ["wg"], "wg")
                  dma("pool", wsb[:, :, :].rearrange("p g q -> p (g q)"), ws_d[li, :, :], [], ["wsb"], "wsb")
                  for oc in range(2):
                      def ev(bk, tb, c0, n, oc=oc):
                          act(uT[:, oc, c0:c0 + n], PS[:, bk, 0:n], AF.Gelu_apprx_tanh, [("pb", bk)], [("uT", oc, tb)])
                      proj_fm(wg, "wg", oc * 128, tbs_o, ev)
                  def gm_a(t):
                      i2 = t % 3
                      bk = bank((0, 1, 2, 3))
                      mm(PS[:, bk, 0:256], [(hT[:, c, t * 128:(t + 1) * 128], wg[:, c, 256:512]) for c in range(8)],
                         ["wg"] + tt_hkeys(t), [("pb", bk)])
                      act(vg[i2][:, :], PS[:, bk, 0:256], AF.Gelu_apprx_tanh, [("pb", bk)], [("vg", i2)])
                      tt("pool", vsq[i2][:, :], vg[i2][:, :], vg[i2][:, :], ALU.mult, [("vg", i2)], [("vsq", i2)])
                      S.op("dve", lambda e, i2=i2: e.tensor_reduce(out=vss[i2][:, :], in_=vsq[i2][:, :].rearrange("p (g d) -> p g d", d=64), axis=AX.X, op=ALU.add),
                           [("vsq", i2)], [("vss", i2)])

                  def gm_a2(t):
                      i2 = t % 3
                      act(vss[i2][:, :], vss[i2][:, :], AF.Ln, [("vss", i2)], [("vss", i2)], scale=1.0 / 64.0, bias=EPS)
                      act(vss[i2][:, :], vss[i2][:, :], AF.Exp, [("vss", i2)], [("vss", i2)], scale=-0.5)
                      for g in range(4):
                          stt(vn[i2][:, g * 64:(g + 1) * 64], vg[i2][:, g * 64:(g + 1) * 64], vss[i2][:, g:g + 1], spc(O_GN + g * 64, 64),
                              ALU.mult, ALU.mult, [("vg", i2), ("vss", i2)] + SPR, [("vn", i2)])

                  def gm_b(t):
                      i2 = t % 3
                      tb = t // 4 if t < 16 else 4
                      bk2 = bank((4, 5, 6, 7))
                      groups = []
                      for g in range(4):
                          pp = (g % 2) * 64
                          groups.append((PS[pp:pp + 64, bk2, (g // 2) * 128:(g // 2) * 128 + 128],
                                         [(vn[i2][:, g * 64:(g + 1) * 64], wsb[:, g, :])], (0, pp)))
                      mm_multi(groups, [("vn", i2), "wsb"], [("pb", bk2)])
                      for cj in range(2):
                          gk = ("gt", cj, i2)
                          tt("dve", gt[i2][cj][:, :], PS[:, bk2, cj * 128:(cj + 1) * 128], sp[:, O_BST + cj * 128:O_BST + (cj + 1) * 128], ALU.add,
                             [("pb", bk2)] + SPR, [gk])
                          tt("dve", ym[:, cj, t * 128:(t + 1) * 128], gt[i2][cj][:, :], uT[:, cj, t * 128:(t + 1) * 128], ALU.mult,
                             [gk, ("uT", cj, tb)], [("ymD", tb, cj)])

                  for k in range(ntt_o + 2):
                      if k >= 2:
                          gm_b(k - 2)
                      if 1 <= k <= ntt_o:
                          gm_a2(k - 1)
                      if k < ntt_o:
                          gm_a(k)
                  dbg_tap("yD_%d" % li, ym[:, :, :], [128, 2, NT], BF16, [("ymD", k, cj) for k in tbs_o for cj in range(2)])
                  wout_multi(sc, [(ymC_t, lambda tb: [("ymC", tb, ci, hl) for ci in range(2) for hl in range(2)]),
                                  (ym, lambda tb: [("ymD", tb, 0), ("ymD", tb, 1)])], 2)
              scCD.__exit__(None, None, None)
              dbg_tap("xmid_%d" % li, xT[:, :, :], [128, 8, NT], F32, [k for tb in range(5) for k in xkeys(tb)])

              checkpoint(6)
              wupv = wup_d[li].rearrange("(c p) n -> p c n", p=128)
              wdnv = wdn_d[li].rearrange("(j p) n -> p j n", p=128)
              segs = [(0, 2048)] + ([(2048, 256)] if ctx_out else [])
              with Scope() as sc:
                  wup = [[T(sc, "wup%d%d" % (i, hv), [128, 8, 512], BF16) for hv in range(2)] for i in range(2)]
                  wdn = T(sc, "wdn", [128, 4, 1024], BF16)
                  j0_, nj_ = FFN_PASSES[0]
                  for hv in range(2):
                      cs = hv * 2816 + j0_ * 128
                      dma("pool", wup[0][hv][:, :, 0:nj_ * 128], wupv[:, :, cs:cs + nj_ * 128], [], [("wup", 0, hv)], "wup0%d" % hv)
                  for hlf in range(2):
                      dma("pool", wdn[:, 0:nj_, hlf * 512:(hlf + 1) * 512], wdnv[:, j0_:j0_ + nj_, hlf * 512:(hlf + 1) * 512], [], ["wdn"], "wdn")
                  norm_phase(1, tbs_o)
                  hid = T(sc, "hid", [128, 4, NT], BF16)
                  accg = T(sc, "accg", [128, 2048], F32)
                  accv = T(sc, "accv", [128, 2048], F32)
                  sg = T(sc, "sg", [128, 2048], BF16)
                  cacc = T(sc, "cacc", [128, 2, 4, 256], F32)
                  csg = T(sc, "csg", [128, 4, 256], BF16)
                  evi = [0]
                  for pi, (j0, nj) in enumerate(FFN_PASSES):
                      wu = wup[pi % 2]
                      for hv in range(2):
                          cs = hv * 2816 + j0 * 128
                          if pi > 0:
                              dma("pool", wu[hv][:, :, 0:nj * 128], wupv[:, :, cs:cs + nj * 128], [], [("wup", pi % 2, hv)], "wup%d%d" % (pi % 2, hv))
                      for hlf in range(2):
                          if pi > 0:
                              dma("pool", wdn[:, 0:nj, hlf * 512:(hlf + 1) * 512], wdnv[:, j0:j0 + nj, hlf * 512:(hlf + 1) * 512], [], ["wdn"], "wdn")
                      for jj in range(nj):
                          j = j0 + jj
                          for hv in range(2):
                              jc = hv * 22 + j
                              acc = accg if hv == 0 else accv
                              rb = hv * 4
                              for (s0, sn) in [(0, 2048)]:
                                  an = "accg" if hv == 0 else "accv"
                                  row = PSF[:, rb * 512:rb * 512 + sn]
                                  pieces = [(0, 1024), (1024, 2048)] if sn == 2048 else [(0, sn)]
                                  cw = lambda tap, jc=jc: sp[:, O_CW + jc * 3 + tap:O_CW + jc * 3 + tap + 1]
                                  bkey = lambda col: ("pb", rb + col // 512)
                                  for pidx, (p0_, p1_) in enumerate(pieces):
                                      groups = []
                                      rk = [("wup", pi % 2, hv)]
                                      wk_ = []
                                      for c0_ in range(p0_, p1_, 512):
                                          w = min(512, p1_ - c0_)
                                          groups.append((PS[:, rb + c0_ // 512, 0:w],
                                                         [(wu[hv][:, c, jj * 128:(jj + 1) * 128], hT[:, c, s0 + c0_:s0 + c0_ + w]) for c in range(8)], None))
                                          rk += hkeys((s0 + c0_) // 512 if s0 < 2048 else 4)
                                          wk_.append(bkey(c0_))
                                      mm_multi(groups, rk, wk_)
                                  for pidx, (p0_, p1_) in enumerate(pieces):
                                      ak = (an, s0, pidx)
                                      pbk = [bkey(c0_) for c0_ in range(p0_, p1_, 512)]
                                      act(acc[:, s0 + p0_:s0 + p1_], row[:, p0_:p1_], AF.Identity, pbk + SPR, [ak], scale=cw(1), bias=sp[:, O_CB + jc:O_CB + jc + 1])
                                      lo = max(p0_, 1)
                                      stt(acc[:, s0 + lo:s0 + p1_], row[:, lo - 1:p1_ - 1], cw(0), acc[:, s0 + lo:s0 + p1_], ALU.mult, ALU.add,
                                          pbk + [bkey(lo - 1), ak] + SPR, [ak])
                                  for pidx, (p0_, p1_) in enumerate(pieces):
                                      ak = (an, s0, pidx)
                                      pbk = [bkey(c0_) for c0_ in range(p0_, p1_, 512)]
                                      hi = min(p1_, sn - 1)
                                      stt(acc[:, s0 + p0_:s0 + hi], row[:, p0_ + 1:hi + 1], cw(2), acc[:, s0 + p0_:s0 + hi], ALU.mult, ALU.add,
                                          pbk + [bkey(hi), ak] + SPR, [ak])
                          for (s0, sn) in [(0, 2048)]:
                              pieces = [(0, 1024), (1024, 2048)] if sn == 2048 else [(0, sn)]
                              for pidx, (p0_, p1_) in enumerate(pieces):
                                  act(sg[:, s0 + p0_:s0 + p1_], accg[:, s0 + p0_:s0 + p1_], AF.Silu, [("accg", s0, pidx)], [("sg", s0, pidx)])
                                  tt("dve", hid[:, jj, s0 + p0_:s0 + p1_], sg[:, s0 + p0_:s0 + p1_], accv[:, s0 + p0_:s0 + p1_], ALU.mult,
                                     [("sg", s0, pidx), ("accv", s0, pidx)], [("hid", jj, s0)])
                      if ctx_out:
                          for jj in range(nj):
                              j = j0 + jj
                              for hv in range(2):
                                  jc = hv * 22 + j
                                  bk = bank()
                                  cac = cacc[:, hv, jj, :]
                                  ck = ("cacc", hv, jj)
                                  cwc = lambda tap, jc=jc: sp[:, O_CW + jc * 3 + tap:O_CW + jc * 3 + tap + 1]
                                  mm(PS[:, bk, 0:256], [(wu[hv][:, c, jj * 128:(jj + 1) * 128], hT[:, c, 2048:2304]) for c in range(8)],
                                     [("wup", pi % 2, hv)] + hkeys(4), [("pb", bk)])
                                  act(cac, PS[:, bk, 0:256], AF.Identity, [("pb", bk)] + SPR, [ck], scale=cwc(1), bias=sp[:, O_CB + jc:O_CB + jc + 1])
                                  stt(cac[:, 1:256], PS[:, bk, 0:255], cwc(0), cac[:, 1:256], ALU.mult, ALU.add, [("pb", bk), ck] + SPR, [ck])
                                  stt(cac[:, 0:255], PS[:, bk, 1:256], cwc(2), cac[:, 0:255], ALU.mult, ALU.add, [("pb", bk), ck] + SPR, [ck])
                          for jj in range(nj):
                              act(csg[:, jj, :], cacc[:, 0, jj, :], AF.Silu, [("cacc", 0, jj)], [("csg", jj)])
                              tt("dve", hid[:, jj, 2048:2304], csg[:, jj, :], cacc[:, 1, jj, :], ALU.mult, [("csg", jj), ("cacc", 1, jj)], [("hid", jj, 2048)])
                      for tb in tbs_o:
                          c0, n = cols(tb)
                          v = 0 if tb < 4 else 1
                          for oc in range(8):
                              bk = bank()
                              mm(PS[:, bk, 0:n], [(wdn[:, jj, oc * 128:(oc + 1) * 128], hid[:, jj, c0:c0 + n]) for jj in range(nj)],
                                 ["wdn"] + [("hid", jj, 0 if tb < 4 else 2048) for jj in range(nj)], [("pb", bk)])
                              evi[0] += 1
                              if False:
                                  ei = (evi[0] // EVAC_POOL) % 3
                                  act(etmp[ei][:, 0:n], PS[:, bk, 0:n], AF.Identity, [("pb", bk), MODK], [("etmp", ei)], scale=gate_ap(1, oc, v))
                                  tt("pool", xT[:, oc, c0:c0 + n], xT[:, oc, c0:c0 + n], etmp[ei][:, 0:n], ALU.add, [("etmp", ei), ("x", oc, tb)], [("x", oc, tb)])
                              else:
                                  stt(xT[:, oc, c0:c0 + n], PS[:, bk, 0:n], gate_ap(1, oc, v), xT[:, oc, c0:c0 + n], ALU.mult, ALU.add,
                                      [("pb", bk), MODK, ("x", oc, tb)], [("x", oc, tb)])
                              if li == L - 1 and pi == len(FFN_PASSES) - 1 and tb < 4 and not S.stopped:
                                  dma("sp", ov_early[:, oc, c0:c0 + n], xT[:, oc, c0:c0 + n], [("x", oc, tb)], [], "xo_%d_%d" % (oc, tb))
                                  stored_early.add((oc, tb))
              dbg_tap("xout_%d" % li, xT[:, :, :], [128, 8, NT], F32, [k for tb in range(5) for k in xkeys(tb)])

          except _Stop:
            pass
        S.stopped = False
        ov = out_d.rearrange("(c p) n -> p c n", p=128)
        for c in range(8):
            for tb in range(4):
                if (c, tb) not in stored_early:
                    dma("sp", ov[:, c, tb * 512:(tb + 1) * 512], xT[:, c, tb * 512:(tb + 1) * 512], [("x", c, tb)], [], "xout%d_%d" % (c, tb))
        if ctx_last:
            cov = cout_d.rearrange("(c p) n -> p c n", p=128)
            dma("sp", cov[:, :, :], xT[:, :, 2048:2304], xkeys(4), [], "cout")
        S.finish()
        with nc.Block() as block:
            S.emit(block)
    return nc, dbg_outs


def _bf(a):
    return np.ascontiguousarray(a.astype(np.float32)).astype(ml_dtypes.bfloat16)


def host_constants():
    p = np.arange(128)
    cb = np.zeros((128, NCB), np.float32)
    cb[:, C_ONESD:C_ONESD + 128] = 1.0 / 1024.0
    cb[:, C_B32:C_B32 + 128] = (p[:, None] // 32 == p[None, :] // 32) / 32.0
    cb[:, C_B64:C_B64 + 128] = (p[:, None] // 64 == p[None, :] // 64) / 64.0
    prot = np.zeros((128, 128), np.float32)
    for m in range(128):
        d = m % 32
        if d < 16:
            prot[m + 16, m] = -1.0
        else:
            prot[m - 16, m] = 1.0
    cb[:, C_PROT:C_PROT + 128] = prot
    for c in range(2):
        ch = c * 128 + p
        grp, cc = ch // 64, ch % 64
        l = np.arange(64)
        ang = 2 * np.pi * np.outer(cc, l) / 64.0
        blkc = np.zeros((128, 256), np.float32)
        blks = np.zeros((128, 256), np.float32)
        for i in range(128):
            blkc[i, grp[i] * 64:(grp[i] + 1) * 64] = np.cos(ang[i]) / 8.0
            blks[i, grp[i] * 64:(grp[i] + 1) * 64] = -np.sin(ang[i]) / 8.0
        cb[:, C_CS64 + c * 512:C_CS64 + c * 512 + 256] = blkc
        cb[:, C_CS64 + c * 512 + 256:C_CS64 + (c + 1) * 512] = blks
    t = np.arange(256)
    a256 = 2 * np.pi * ((np.outer(t, t)) % 256) / 256.0
    c256, s256 = np.cos(a256) / 16.0, np.sin(a256) / 16.0
    for tt_ in range(2):
        cb[:, C_C256 + tt_ * 256:C_C256 + (tt_ + 1) * 256] = c256[tt_ * 128:(tt_ + 1) * 128]
        cb[:, C_S256 + tt_ * 256:C_S256 + (tt_ + 1) * 256] = s256[tt_ * 128:(tt_ + 1) * 128]
    n = np.arange(2048, dtype=np.int64)
    aN = 2 * np.pi * ((np.outer(n, n)) % 2048).astype(np.float64) / 2048.0
    sc = 1.0 / math.sqrt(2048.0)
    CN = _bf(np.cos(aN) * sc)
    SN = _bf(np.sin(aN) * sc)
    freqs = (10000.0 ** (-np.arange(8, dtype=np.float32) / 8)).astype(np.float32)
    tok = np.arange(2048)
    row = (tok // 64).astype(np.float32)
    col = (tok % 64).astype(np.float32)
    ang = np.concatenate([row[:, None] * freqs, col[:, None] * freqs], axis=-1).astype(np.float32)
    slot = (p % 32) % 16
    rope = np.stack([np.cos(ang)[:, slot].T, np.sin(ang)[:, slot].T], axis=1).astype(np.float32)
    return _bf(cb), np.ascontiguousarray(rope), CN, SN


def na_mask_index():
    def one(m, a):
        k = np.arange(128)
        kr, kc = 2 * a + k // 64, k % 64
        qr, qc = 2 * m + k // 64, k % 64
        r0 = np.clip(qr - 4, 0, 24)
        c0 = np.clip(qc - 8, 0, 48)
        okr = (kr[:, None] >= r0[None, :]) & (kr[:, None] <= r0[None, :] + 7)
        okc = (kc[:, None] >= c0[None, :]) & (kc[:, None] <= c0[None, :] + 15)
        ro = kr[:, None] - qr[None, :] + 7
        co = kc[:, None] - qc[None, :] + 15
        return np.where(okr & okc, ro * 31 + co, 15 * 31)
    ids = [one(5, 5 + d) for d in range(-2, 3)]
    for m, a0 in ((0, 0), (1, 0), (14, 12), (15, 12)):
        ids += [one(m, a) for a in range(a0, a0 + 4)]
    return np.stack(ids)


def layer_pack(li, inp, midx):
    p = np.arange(128)
    sp = np.zeros((128, NS), np.float32)
    sp[:, O_BADA:O_BADA + 48] = inp["b_ada"][li].reshape(48, 128).T
    sp[:, O_GMIX:O_GMIX + 8] = inp["g_mix"][li].reshape(8, 128).T
    sp[:, O_GFFN:O_GFFN + 8] = inp["g_ffn"][li].reshape(8, 128).T
    sp[:, O_DQN] = inp["diff_qn"][li][p % 32]
    sp[:, O_DKN] = inp["diff_kn"][li][p % 32]
    sp[:, O_SUBLN] = inp["diff_subln"][li][p % 64]
    sp[:, O_NQN] = inp["na_qn"][li][p % 64]
    sp[:, O_NKN] = inp["na_kn"][li][p % 64]
    lam_init = 0.8 - 0.6 * math.exp(-0.3 * li)
    sp[:, O_LAMI] = lam_init
    sp[:, O_OMLI] = 1.0 - lam_init
    sp[:, O_LAM:O_LAM + 128] = inp["diff_lam"][li].reshape(1, 128)
    sp[:, O_GN:O_GN + 256] = inp["gmlp_norm"][li].reshape(1, 256)
    gb = inp["gmlp_b"][li]
    for cj in range(2):
        sp[:, O_BST + cj * 128:O_BST + (cj + 1) * 128] = gb[cj * 2 + p // 64, :]
    sp[:, O_CW:O_CW + 132] = inp["ffn_conv"][li].reshape(3, 44, 128).transpose(2, 1, 0).reshape(128, 132)
    sp[:, O_CB:O_CB + 44] = inp["ffn_conv_b"][li].reshape(44, 128).T
    wsT = np.ascontiguousarray(inp["gmlp_ws"][li].transpose(2, 0, 1)).reshape(128, 512)
    rp = np.concatenate([inp["na_rpb"][li].reshape(4, 465), np.full((4, 1), -30000.0, np.float32)], axis=1)
    mb = rp[:, midx]
    mb = np.ascontiguousarray(mb.transpose(2, 0, 1, 3)).reshape(128, 4 * 21 * 128)
    return sp, wsT, mb


_CACHE = {}


def _get_prog(n_layers, ctx_last):
    key = (n_layers, ctx_last)
    if key not in _CACHE:
        _CACHE[key] = build(n_layers, ctx_last)[0]
    return _CACHE[key]


def _consts():
    if "c" not in _CACHE:
        _CACHE["c"] = host_constants() + (na_mask_index(),)
    return _CACHE["c"]


FUSED = True


def kernel(**inp):
    inp = {k: np.asarray(v, dtype=np.float32) for k, v in inp.items()}
    cbf, rope, CN, SN, midx = _consts()
    B = inp["x"].shape[0]
    packs = [layer_pack(li, inp, midx) for li in range(DEPTH)]
    xT = [np.ascontiguousarray(inp["x"][b].T) for b in range(B)]
    cT = [np.ascontiguousarray(inp["ctx"][b].T) for b in range(B)]
    cpk = []
    for b in range(B):
        a = np.stack([inp["c"][b].reshape(8, 128).T, inp["c_ctx"].reshape(8, 128).T], axis=-1)
        cpk.append(np.ascontiguousarray(a.reshape(128, 16)))

    def layer_inputs(lis):
        return {
            "spk": np.stack([packs[li][0] for li in lis]), "wsT": np.stack([packs[li][1] for li in lis]),
            "mb": np.stack([packs[li][2] for li in lis]),
            "w_ada": inp["w_ada"][lis], "w_in": inp["w_in"][lis], "w_out": inp["w_out"][lis],
            "w_up": inp["ffn_up"][lis], "w_dn": inp["ffn_down"][lis],
        }
    common = {"cbf": cbf, "rope": rope, "CN": CN, "SN": SN}
    if FUSED:
        nc = _get_prog(DEPTH, False)
        lw = layer_inputs(list(range(DEPTH)))
        in_maps = [dict(common, xT=xT[b], cxT=cT[b], cpk=cpk[b], **lw) for b in range(B)]
        res = run_bass_kernel_spmd(nc, in_maps, core_ids=list(range(B)))
        outs = [r["outT"] for r in res.results]
    else:
        nc = _get_prog(1, True)
        for li in range(DEPTH):
            lw = layer_inputs([li])
            in_maps = [dict(common, xT=xT[b], cxT=cT[b], cpk=cpk[b], **lw) for b in range(B)]
            res = run_bass_kernel_spmd(nc, in_maps, core_ids=list(range(B)))
            xT = [np.ascontiguousarray(r["outT"]) for r in res.results]
            cT = [np.ascontiguousarray(r["coutT"]) for r in res.results]
        outs = xT
    return np.stack([np.ascontiguousarray(o.T) for o in outs]).astype(np.float32)
```

```python
import math
from contextlib import ExitStack
import numpy as np
import ml_dtypes
import concourse.bass as bass
import concourse.mybir as mybir
from concourse.bass_utils import run_bass_kernel_spmd

F32 = mybir.dt.float32
BF16 = mybir.dt.bfloat16
AF = mybir.ActivationFunctionType
ALU = mybir.AluOpType
AX = mybir.AxisListType

DEPTH = 4
FFN_VARIANT = 0
SKIPW = 0
NA_WARM = 0
PRECISE_FENCE = False
PE_WARM = 1
NA_ACT_RECIP = True
EVAC_POOL = 3
SES_ENGINES = ("act", "dve", "pool")
VCOPIES = 4
NT = 2304
EPS = 1e-6
NS = 888
O_BADA, O_GMIX, O_GFFN = 0, 48, 56
O_DQN, O_DKN, O_SUBLN, O_NQN, O_NKN, O_LAMI, O_OMLI = 64, 65, 66, 67, 68, 69, 70
O_LAM, O_GN, O_BST, O_CW, O_CB = 72, 200, 456, 712, 844
C_ONESD, C_B32, C_B64, C_PROT, C_CS64, C_C256, C_S256, NCB = 0, 128, 256, 384, 512, 1536, 2048, 2560
FFN_PASSES = [(0, 4), (4, 4), (8, 4), (12, 4), (16, 4), (20, 2)]


class Sched:
    ENGS = ("pe", "act", "dve", "pool", "sp")

    def __init__(self, nc, stack):
        self.nc, self.stack = nc, stack
        self.ops = {e: [] for e in self.ENGS}
        self.cnt = {}
        self.known = {e: {} for e in self.ENGS}
        self.last_w = {}
        self.readers = {}
        self.sems = {}
        self.fence_vals = {}
        self.stopped = False
        self.owner = {}
        self.scopes = []

    def open_scope(self):
        self.scopes.append({"keys": set(), "max": {}})

    def close_scope(self):
        sc = self.scopes.pop()
        for sk, v in sc["max"].items():
            if self.fence_vals.get(sk, 0) < v:
                self.fence_vals[sk] = v
        for k in sc["keys"]:
            self.owner.pop(k, None)
            self.last_w.pop(k, None)
            self.readers.pop(k, None)
        if self.scopes:
            up = self.scopes[-1]["max"]
            for sk, v in sc["max"].items():
                if up.get(sk, 0) < v:
                    up[sk] = v

    def sem(self, key):
        if key not in self.sems:
            self.sems[key] = self.stack.enter_context(self.nc.semaphore("s%d" % len(self.sems)))
        return self.sems[key]

    def op(self, eng, fn, reads=(), writes=(), dma=None):
        if self.stopped:
            return
        waits = {}

        def need(sk, val):
            if sk == ("eng", eng) and eng not in SES_ENGINES:
                return
            if self.known[eng].get(sk, 0) >= val:
                return
            if waits.get(sk, 0) < val:
                waits[sk] = val

        local = []
        for k in list(reads) + list(writes):
            if k not in self.owner:
                self.owner[k] = self.scopes[-1] if self.scopes else None
                if self.scopes:
                    self.scopes[-1]["keys"].add(k)
            if self.owner[k] is not None:
                local.append(self.owner[k])
        if local or not PRECISE_FENCE:
            for sk, v in self.fence_vals.items():
                need(sk, v)
        for k in reads:
            if k in self.last_w:
                need(*self.last_w[k])
        for k in writes:
            if k in self.last_w:
                need(*self.last_w[k])
            for sk, v in self.readers.get(k, {}).items():
                need(sk, v)
        for sk, v in waits.items():
            self.known[eng][sk] = v
        sk = ("dma", dma) if dma is not None else ("eng", eng)
        self.sem(sk)
        inc = 16 if dma is not None else 1
        self.cnt[sk] = self.cnt.get(sk, 0) + inc
        val = self.cnt[sk]
        for k in reads:
            d = self.readers.setdefault(k, {})
            d[sk] = max(d.get(sk, 0), val)
        for k in writes:
            self.last_w[k] = (sk, val)
            self.readers[k] = {}
        if PRECISE_FENCE:
            for scd in local:
                scd["max"][sk] = val
        else:
            for scd in self.scopes:
                scd["max"][sk] = val
        self.ops[eng].append((sorted(waits.items()), fn, sk, inc))

    def finish(self):
        self.ops["sp"].append((sorted(self.cnt.items()), None, None, 0))

    def emit(self, block):
        engs = {"pe": "tensor", "act": "scalar", "dve": "vector", "pool": "gpsimd", "sp": "sync"}

        def mk(ename):
            def body(e):
                for waits, fn, sk, inc in self.ops[ename]:
                    for wk, v in waits:
                        e.wait_ge(self.sems[wk], v)
                    if fn is not None:
                        fn(e).then_inc(self.sems[sk], inc)
            return body

        for ename, attr in engs.items():
            if self.ops[ename]:
                getattr(block, attr)(mk(ename))


def cols(tb):
    return (tb * 512, 512) if tb < 4 else (2048, 256)


def na_local(m):
    if 2 <= m <= 13:
        return m - 2, 5, 0
    e = {0: 0, 1: 1, 14: 2, 15: 3}[m]
    return (0 if m < 2 else 12), 4, 5 + 4 * e


class _Stop(Exception):
    pass


def build(n_layers, ctx_last=False, dbg=None, stop=None):
    L = n_layers
    nc = bass.Bass("TRN2", target_bir_lowering=False)
    D = lambda name, shape, dt, kind="ExternalInput": nc.dram_tensor(name, shape, dt, kind=kind).ap()
    xT_d = D("xT", [1024, 2048], F32)
    cxT_d = D("cxT", [1024, 256], F32)
    cpk_d = D("cpk", [128, 16], F32)
    cb_d = D("cbf", [128, NCB], BF16)
    rope_d = D("rope", [128, 2, 2048], F32)
    CN_d = D("CN", [2048, 2048], BF16)
    SN_d = D("SN", [2048, 2048], BF16)
    sp_d = D("spk", [L, 128, NS], F32)
    ws_d = D("wsT", [L, 128, 512], F32)
    mb_d = D("mb", [L, 128, 4 * 21 * 128], F32)
    wada_d = D("w_ada", [L, 1024, 6144], F32)
    win_d = D("w_in", [L, 1024, 2304], F32)
    wout_d = D("w_out", [L, 1024, 1024], F32)
    wup_d = D("w_up", [L, 1024, 5632], F32)
    wdn_d = D("w_dn", [L, 2816, 1024], F32)
    out_d = D("outT", [1024, 2048], F32, "ExternalOutput")
    cout_d = D("coutT", [1024, 256], F32, "ExternalOutput") if ctx_last else None
    dbg_outs = {}

    with ExitStack() as st:
        S = Sched(nc, st)

        tcnt = [0]

        class Scope(ExitStack):
            def __enter__(self):
                S.open_scope()
                return super().__enter__()

            def __exit__(self, *a):
                r = super().__exit__(*a)
                S.close_scope()
                return r

        def T(stack, name, shape, dt):
            tcnt[0] += 1
            return stack.enter_context(nc.sbuf_tensor("sb%d_%s" % (tcnt[0], name), shape, dt))

        PS = st.enter_context(nc.psum_tensor("PS", [128, 8, 512], F32))
        PSF = PS[:, :, :].rearrange("p b n -> p (b n)")
        xT = T(st, "xT", [128, 8, NT], F32)
        hT = T(st, "hT", [128, 8, NT], BF16)
        CB = T(st, "CB", [128, NCB], BF16)
        SPK = T(st, "SPK", [128, 1, NS], F32)
        MODS = T(st, "MOD", [128, 2, 48, 2], F32)
        AMS = T(st, "AM", [128, 2, 2, 8, 2], F32)
        ADAP = T(st, "ADAP", [128, 2, 64], F32)
        cpk = T(st, "cpk", [128, 8, 2], F32)
        sT = T(st, "sT", [128, 8, 2], BF16)
        lamt = T(st, "lamt", [128, 8], F32)

        bank_rr = [0]
        ov_early = out_d.rearrange("(c p) n -> p c n", p=128)
        stored_early = set()

        def bank(allowed=(0, 1, 2, 3, 4, 5, 6, 7)):
            bank_rr[0] += 1
            return allowed[bank_rr[0] % len(allowed)]

        def mm(out_ap, pairs, reads, writes, tile_position=None):
            def fn(e, pairs=pairs, out_ap=out_ap):
                n = len(pairs)
                for i, (l, r) in enumerate(pairs):
                    kw = {} if tile_position is None else {"tile_position": tile_position}
                    inst = e.matmul(out_ap, lhsT=l, rhs=r, start=(i == 0), stop=(i == n - 1), **kw)
                return inst
            S.op("pe", fn, reads, writes)

        def mm_multi(groups, reads, writes):
            def fn(e, groups=groups):
                for out_ap, pairs, tp in groups:
                    n = len(pairs)
                    for i, (l, r) in enumerate(pairs):
                        kw = {} if tp is None else {"tile_position": tp}
                        inst = e.matmul(out_ap, lhsT=l, rhs=r, start=(i == 0), stop=(i == n - 1), **kw)
                return inst
            S.op("pe", fn, reads, writes)

        def act(out, in_, func, reads, writes, **kw):
            S.op("act", lambda e: e.activation(out=out, in_=in_, func=func, **kw), reads, writes)

        def dma(eng, out, in_, reads, writes, sem):
            S.op(eng, lambda e: e.dma_start(out=out, in_=in_), reads, writes, dma=sem)

        def tt(eng, out, in0, in1, op, reads, writes):
            S.op(eng, lambda e: e.tensor_tensor(out=out, in0=in0, in1=in1, op=op), reads, writes)

        def stt(out, in0, scalar, in1, op0, op1, reads, writes):
            S.op("dve", lambda e: e.scalar_tensor_tensor(out=out, in0=in0, scalar=scalar, in1=in1, op0=op0, op1=op1),
                 reads, writes)

        def ts(eng, out, in0, s1, s2, op0, op1, reads, writes):
            if op1 is None:
                S.op(eng, lambda e: e.tensor_scalar(out=out, in0=in0, scalar1=s1, scalar2=None, op0=op0), reads, writes)
            else:
                S.op(eng, lambda e: e.tensor_scalar(out=out, in0=in0, scalar1=s1, scalar2=s2, op0=op0, op1=op1), reads, writes)

        def copy(eng, out, in_, reads, writes):
            if eng == "act":
                S.op("act", lambda e: e.activation(out=out, in_=in_, func=AF.Identity), reads, writes)
            else:
                S.op(eng, lambda e: e.tensor_copy(out=out, in_=in_), reads, writes)

        def recip(out, in_, reads, writes):
            S.op("dve", lambda e: e.reciprocal(out=out, in_=in_), reads, writes)

        def dbg_tap(name, ap, shape, dt, reads):
            if dbg is None or name not in dbg or S.stopped:
                return
            d = D("dbg_" + name, list(shape), dt, "ExternalOutput")
            dbg_outs[name] = d
            dma("sp", d, ap, reads, [], "dbg_" + name)

        ev_rr = [0]

        def ev_eng():
            ev_rr[0] += 1
            return "act" if ev_rr[0] % 2 else "dve"

        def xkeys(tb):
            return [("x", c, tb) for c in range(8)]

        def hkeys(tb):
            return [("h", c, tb) for c in range(8)]

        def tt_hkeys(t):
            return hkeys(t // 4 if t < 16 else 4)

        dma("sp", CB[:, :], cb_d[:, :], [], ["cb"], "cb")
        dma("sp", cpk[:, :, :].rearrange("p c v -> p (c v)"), cpk_d[:, :], [], ["cpk"], "cpk")
        xv = xT_d.rearrange("(c p) n -> p c n", p=128)
        cv = cxT_d.rearrange("(c p) n -> p c n", p=128)
        for c in range(8):
            dma("sp", xT[:, c, 0:2048], xv[:, c, :], [], [("x", c, tb) for tb in range(4)], "xin%d" % c)
        dma("sp", xT[:, :, 2048:2304], cv[:, :, :], [], xkeys(4), "cin")
        act(sT[:, :, :], cpk[:, :, :], AF.Silu, ["cpk"], ["sT"])
        ONESD = CB[:, C_ONESD:C_ONESD + 128]
        B32 = CB[:, C_B32:C_B32 + 128]
        B64 = CB[:, C_B64:C_B64 + 128]
        PROT = CB[:, C_PROT:C_PROT + 128]

        def checkpoint(k):
            if stop is not None and stop == k:
                S.stopped = True

        for li in range(L):
          try:
              ctx_out = (li < L - 1) or ctx_last
              tbs_all = [0, 1, 2, 3, 4]
              tbs_o = tbs_all if ctx_out else [0, 1, 2, 3]
              ntt_o = 18 if ctx_out else 16
              sp = SPK[:, 0, :]

              def spc(o, n=1, sp=sp):
                  return sp[:, o:o + n]
              dma("sp", SPK[:, 0, :], sp_d[li, :, :], [], [("sp", 0)], "sp0")
              SPR = [("sp", 0)]

              MOD = MODS[:, li % 2]
              AM = AMS[:, li % 2]
              MODK = ("mod", li % 2)
              AMK = ("am", li % 2)

              def ada_load(lt, jg, wa):
                  wav = wada_d[lt].rearrange("(c p) n -> p c n", p=128)
                  dma("pool", wa[jg % len(wa)][:, :, :], wav[:, :, jg * 512:(jg + 1) * 512], [], [("wa", jg % len(wa))], "wa%d" % (jg % len(wa)))

              def ada_params(lt):
                  dma("sp", ADAP[:, lt % 2, :], sp_d[lt, :, 0:64], [], [("adap", lt % 2)], "adap%d" % (lt % 2))

              def ada_piece(lt, jg, wa, bk):
                  wb = wa[jg % len(wa)]
                  groups = []
                  for j4 in range(4):
                      groups.append((PS[:, bk, j4 * 2:j4 * 2 + 2],
                                     [(wb[:, c, j4 * 128:(j4 + 1) * 128], sT[:, c, :]) for c in range(8)], None))
                  mm_multi(groups, [("wa", jg % len(wa)), "sT"], [("pb", bk)])
                  psm = PS[:, bk, 0:8].rearrange("p (j v) -> p j v", v=2)
                  for v in range(2):
                      tt("dve", MODS[:, lt % 2, jg * 4:(jg + 1) * 4, v], psm[:, :, v], ADAP[:, lt % 2, O_BADA + jg * 4:O_BADA + (jg + 1) * 4], ALU.add,
                         [("pb", bk), ("adap", lt % 2)], [("mod", lt % 2)])

              def ada_finish(lt, whichs=(0, 1)):
                  for which, (so, go) in enumerate(((8, O_GMIX), (32, O_GFFN))):
                      if which not in whichs:
                          continue
                      for v in range(2):
                          stt(AMS[:, lt % 2, which, :, v], MODS[:, lt % 2, so:so + 8, v], 1.0, ADAP[:, lt % 2, go:go + 8], ALU.add, ALU.mult,
                              [("mod", lt % 2), ("adap", lt % 2)], [("am", lt % 2)])

              if li == 0:
                  scP = Scope()
                  scP.__enter__()
                  wa0 = [T(scP, "wa%d" % i, [128, 8, 512], BF16) for i in range(6)]
                  ada_params(0)
                  for jg in range(4):
                      ada_load(0, jg, wa0)
                  for jg in range(4):
                      ada_piece(0, jg, wa0, 7)
                  ada_finish(0, (0,))
                  for jg in range(4, 10):
                      ada_load(0, jg, wa0)
              checkpoint(1)
              dbg_tap("mod%d" % li, MOD[:, :, :], [128, 48, 2], F32, [MODK])

              def shift_ap(which, c, v):
                  s = (0, 24)[which] + c
                  return MOD[:, s, v:v + 1]

              def gate_ap(which, c, v):
                  s = (16, 40)[which] + c
                  return MOD[:, s, v:v + 1]

              def norm_phase(which, tbs):
                  with Scope() as sc:
                      sq = [T(sc, "sq%d" % i, [128, 8, 512], BF16) for i in range(2)]
                      rstd = [T(sc, "rstd%d" % i, [128, 512], F32) for i in range(2)]
                      tmp = [T(sc, "ntmp%d" % i, [128, 512], F32) for i in range(3)]

                      def n_a(k):
                          tb = tbs[k]
                          c0, n = cols(tb)
                          q2 = k % 2
                          act(sq[q2][:, :, 0:n], xT[:, :, c0:c0 + n], AF.Square, xkeys(tb), [("sq", q2)])
                          bk = bank()
                          mm(PS[:, bk, 0:n], [(ONESD, sq[q2][:, c, 0:n]) for c in range(8)], [("sq", q2), "cb"], [("pb", bk)])
                          act(rstd[q2][:, 0:n], PS[:, bk, 0:n], AF.Ln, [("pb", bk)], [("rstd", q2)], bias=EPS)
                          act(rstd[q2][:, 0:n], rstd[q2][:, 0:n], AF.Exp, [("rstd", q2)], [("rstd", q2)], scale=-0.5)

                      def n_b(k):
                          tb = tbs[k]
                          c0, n = cols(tb)
                          v = 0 if tb < 4 else 1
                          q2 = k % 2
                          for c in range(8):
                              t = tmp[c % 3]
                              tk = ("ntmp", c % 3)
                              tt("dve", t[:, 0:n], xT[:, c, c0:c0 + n], rstd[q2][:, 0:n], ALU.mult, [("x", c, tb), ("rstd", q2)], [tk])
                              act(hT[:, c, c0:c0 + n], t[:, 0:n], AF.Identity, [tk, AMK, MODK], [("h", c, tb)],
                                  scale=AM[:, which, c, v:v + 1], bias=shift_ap(which, c, v))

                      n_a(0)
                      for k in range(len(tbs)):
                          if k + 1 < len(tbs):
                              n_a(k + 1)
                          n_b(k)

              norm_phase(0, tbs_all)
              if li == 0:
                  for jg in range(4, 12):
                      ada_piece(0, jg, wa0, 7)
                      if jg + 6 < 12:
                          ada_load(0, jg + 6, wa0)
                  ada_finish(0, (1,))
                  scP.__exit__(None, None, None)
              checkpoint(2)
              dbg_tap("h1_%d" % li, hT[:, :, :], [128, 8, NT], BF16, [k for tb in range(5) for k in hkeys(tb)])

              winv = win_d[li].rearrange("(c p) n -> p c n", p=128)
              woutv = wout_d[li].rearrange("(c p) n -> p c n", p=128)

              def wout_partial(sc, ym, mix, ykeys_fn, wo=None):
                  wout_multi(sc, [(ym, ykeys_fn)], mix, wo)

              def wo_prefetch(sc, mix, nch):
                  wo = T(sc, "wo", [128, nch, 1024], BF16)
                  for hlf in range(2):
                      dma("pool", wo[:, :, hlf * 512:(hlf + 1) * 512], woutv[:, 2 * mix:2 * mix + nch, hlf * 512:(hlf + 1) * 512],
                          [], ["wo"], "wo")
                  return wo

              def wout_multi(sc, yms, mix, wo=None):
                  nch = 2 * len(yms)
                  if wo is None:
                      wo = wo_prefetch(sc, mix, nch)
                  wtmp = [T(sc, "wtmp%d" % i, [128, 512], F32) for i in range(3)]
                  wev = [0]
                  for tb in tbs_o:
                      c0, n = cols(tb)
                      v = 0 if tb < 4 else 1
                      for oc in range(8):
                          bk = bank()
                          pairs = []
                          rkeys = ["wo"]
                          for yi, (ym_, kf) in enumerate(yms):
                              for c in range(2):
                                  pairs.append((wo[:, 2 * yi + c, oc * 128:(oc + 1) * 128], ym_[:, c, c0:c0 + n]))
                              rkeys += kf(tb)
                          mm(PS[:, bk, 0:n], pairs, rkeys, [("pb", bk)])
                          wev[0] += 1
                          if EVAC_POOL and wev[0] % EVAC_POOL == 0:
                              ei = (wev[0] // EVAC_POOL) % 3
                              act(wtmp[ei][:, 0:n], PS[:, bk, 0:n], AF.Identity, [("pb", bk), MODK], [("wtmp", ei)], scale=gate_ap(0, oc, v))
                              tt("pool", xT[:, oc, c0:c0 + n], xT[:, oc, c0:c0 + n], wtmp[ei][:, 0:n], ALU.add, [("wtmp", ei), ("x", oc, tb)], [("x", oc, tb)])
                          else:
                              stt(xT[:, oc, c0:c0 + n], PS[:, bk, 0:n], gate_ap(0, oc, v), xT[:, oc, c0:c0 + n], ALU.mult, ALU.add,
                                  [("pb", bk), MODK, ("x", oc, tb)], [("x", oc, tb)])

              def proj_fm(wt, wkey, wcol, tbs, evac):
                  for tb in tbs:
                      c0, n = cols(tb)
                      bk = bank()
                      mm(PS[:, bk, 0:n], [(wt[:, c, wcol:wcol + 128], hT[:, c, c0:c0 + n]) for c in range(8)],
                         [wkey] + hkeys(tb), [("pb", bk)])
                      evac(bk, tb, c0, n)

              with Scope() as sc:
                  ym = T(sc, "ymA", [128, 2, NT], BF16)
                  pf = T(sc, "pf", [128, 2, NT], BF16)
                  AB = T(sc, "AB", [128, 18, 512], BF16)
                  wf = T(sc, "wf", [128, 8, 256], BF16)
                  CNb = [T(sc, "CNb%d" % i, [128, 16, 256], BF16) for i in range(2)]
                  SNb = [T(sc, "SNb%d" % i, [128, 16, 256], BF16) for i in range(2)]
                  dma("pool", wf[:, :, :], winv[:, :, 0:256], [], ["wf"], "wf")
                  woA = wo_prefetch(sc, 0, 2)
                  CNv = CN_d.rearrange("(t p) k -> p t k", p=128)
                  SNv = SN_d.rearrange("(t p) k -> p t k", p=128)

                  def ft_load(kb):
                      dma("sp", CNb[kb % 2][:, :, :], CNv[:, :, kb * 256:(kb + 1) * 256], [], [("CNb", kb % 2)], "CNb%d" % (kb % 2))
                      dma("sp", SNb[kb % 2][:, :, :], SNv[:, :, kb * 256:(kb + 1) * 256], [], [("SNb", kb % 2)], "SNb%d" % (kb % 2))
                  ft_load(0)
                  ft_load(1)
                  for oc in range(2):
                      def ev(bk, tb, c0, n, oc=oc):
                          copy(ev_eng(), pf[:, oc, c0:c0 + n], PS[:, bk, 0:n], [("pb", bk)], [("pf", oc, tb)])
                      proj_fm(wf, "wf", oc * 128, tbs_o, ev)
                  for t in range(ntt_o):
                      tb = t // 4 if t < 16 else 4
                      bk = bank()
                      mm(PS[:, bk, :], [(pf[:, c, t * 128:(t + 1) * 128], CB[:, C_CS64 + c * 512:C_CS64 + (c + 1) * 512]) for c in range(2)],
                         [("pf", 0, tb), ("pf", 1, tb), "cb"], [("pb", bk)])
                      copy(ev_eng(), AB[:, t, :], PS[:, bk, :], [("pb", bk)], [("AB", t)])
                  for kb in range(8):
                      for lc in range(2):
                          bk = bank()
                          pairs = []
                          for t in range(16):
                              pairs.append((AB[:, t, lc * 128:(lc + 1) * 128], CNb[kb % 2][:, t, :]))
                              pairs.append((AB[:, t, 256 + lc * 128:256 + (lc + 1) * 128], SNb[kb % 2][:, t, :]))
                          mm(PS[:, bk, 0:256], pairs, [("AB", t) for t in range(16)] + [("CNb", kb % 2), ("SNb", kb % 2)], [("pb", bk)])
                          copy(ev_eng(), ym[:, lc, kb * 256:(kb + 1) * 256], PS[:, bk, 0:256], [("pb", bk)], [("ymA", kb // 2)])
                      if kb + 2 < 8:
                          ft_load(kb + 2)
                  if ctx_out:
                      for lc in range(2):
                          bk = bank()
                          pairs = []
                          for t in range(2):
                              pairs.append((AB[:, 16 + t, lc * 128:(lc + 1) * 128], CB[:, C_C256 + t * 256:C_C256 + (t + 1) * 256]))
                              pairs.append((AB[:, 16 + t, 256 + lc * 128:256 + (lc + 1) * 128], CB[:, C_S256 + t * 256:C_S256 + (t + 1) * 256]))
                          mm(PS[:, bk, 0:256], pairs, [("AB", 16), ("AB", 17), "cb"], [("pb", bk)])
                          copy(ev_eng(), ym[:, lc, 2048:2304], PS[:, bk, 0:256], [("pb", bk)], [("ymA", 4)])
                  dbg_tap("yA_%d" % li, ym[:, :, :], [128, 2, NT], BF16, [("ymA", k) for k in range(5)])
                  wout_partial(sc, ym, 0, lambda tb: [("ymA", tb)], woA)

              checkpoint(3)
              with Scope() as sc:
                  ym = T(sc, "ymB", [128, 2, NT], BF16)
                  qk = T(sc, "qk", [128, 4, NT], BF16)
                  VA = T(sc, "VA", [128, 18, 4, 128], BF16)
                  S.op("pool", lambda e: e.memset(VA[:, :, :, :].rearrange("p a h d -> p (a h d)"), 1.0), [], ["VAones"])
                  with Scope() as sc2:
                      wqk = T(sc2, "wqk", [128, 8, 512], BF16)
                      wv = T(sc2, "wvB", [128, 8, 256], BF16)
                      ROPE = [T(sc2, "ROPE%d" % i, [128, 2, 512], F32) for i in range(2)]
                      sqb = [T(sc2, "sqb%d" % i, [128, 512], BF16) for i in range(2)]
                      ub = [T(sc2, "ub%d" % i, [128, 512], BF16) for i in range(2)]
                      rs = [T(sc2, "rsB%d" % i, [128, 512], F32) for i in range(2)]
                      t1 = [T(sc2, "t1B%d" % i, [128, 512], F32) for i in range(2)]
                      t2 = [T(sc2, "t2B%d" % i, [128, 512], F32) for i in range(2)]
                      dma("pool", wqk[:, :, :], winv[:, :, 256:768], [], ["wqk"], "wqk")
                      dma("pool", wv[:, :, :], winv[:, :, 768:1024], [], ["wvB"], "wvB")
                      it = 0
                      blocks = []
                      for tb in tbs_all:
                          for oi in range(4):
                              if oi < 2 and tb == 4 and not ctx_out:
                                  continue
                              blocks.append((tb, oi))
                      rope_loaded = set()
                      pbank = {}

                      def qk_a(bi):
                          tb, oi = blocks[bi]
                          c0, n = cols(tb)
                          i2 = bi % 2
                          rp = ROPE[tb % 2]
                          rkk = ("rope", tb % 2)
                          if tb < 4 and tb not in rope_loaded:
                              rope_loaded.add(tb)
                              dma("sp", rp[:, :, :], rope_d[:, :, c0:c0 + n], [], [rkk], "rope%d" % (tb % 2))
                          gcol = O_DQN if oi < 2 else O_DKN
                          bk = bank()
                          mm(PS[:, bk, 0:n], [(wqk[:, c, oi * 128:(oi + 1) * 128], hT[:, c, c0:c0 + n]) for c in range(8)],
                             ["wqk"] + hkeys(tb), [("pb", bk)])
                          act(sqb[i2][:, 0:n], PS[:, bk, 0:n], AF.Square, [("pb", bk)], [("sqb", i2)])
                          act(ub[i2][:, 0:n], PS[:, bk, 0:n], AF.Identity, [("pb", bk)] + SPR, [("ub", i2)], scale=spc(gcol))
                          bk2 = bank()
                          mm(PS[:, bk2, 0:n], [(B32, sqb[i2][:, 0:n])], [("sqb", i2), "cb"], [("pb", bk2)])
                          if tb < 4:
                              bk3 = bank()
                              pbank[bi] = bk3
                              mm(PS[:, bk3, 0:n], [(PROT, ub[i2][:, 0:n])], [("ub", i2), "cb"], [("pb", bk3)])
                          act(rs[i2][:, 0:n], PS[:, bk2, 0:n], AF.Ln, [("pb", bk2)], [("rsB", i2)], bias=EPS)
                          act(rs[i2][:, 0:n], rs[i2][:, 0:n], AF.Exp, [("rsB", i2)], [("rsB", i2)], scale=-0.5)

                      def qk_b(bi):
                          tb, oi = blocks[bi]
                          c0, n = cols(tb)
                          i2 = bi % 2
                          rp = ROPE[tb % 2]
                          rkk = ("rope", tb % 2)
                          if tb < 4:
                              bk3 = pbank[bi]
                              tt("dve", t1[i2][:, 0:n], ub[i2][:, 0:n], rp[:, 0, 0:n], ALU.mult, [("ub", i2), rkk], [("t1B", i2)])
                              tt("dve", t2[i2][:, 0:n], PS[:, bk3, 0:n], rp[:, 1, 0:n], ALU.mult, [("pb", bk3), rkk], [("t2B", i2)])
                              tt("pool", t1[i2][:, 0:n], t1[i2][:, 0:n], t2[i2][:, 0:n], ALU.add, [("t1B", i2), ("t2B", i2)], [("t1B", i2)])
                              tt("dve", qk[:, oi, c0:c0 + n], t1[i2][:, 0:n], rs[i2][:, 0:n], ALU.mult, [("t1B", i2), ("rsB", i2)], [("qk", oi, tb)])
                          else:
                              tt("dve", qk[:, oi, c0:c0 + n], ub[i2][:, 0:n], rs[i2][:, 0:n], ALU.mult, [("ub", i2), ("rsB", i2)], [("qk", oi, tb)])

                      qk_a(0)
                      for bi in range(len(blocks)):
                          if bi + 1 < len(blocks):
                              qk_a(bi + 1)
                          qk_b(bi)
                      checkpoint(3.1)
                      for t in range(18):
                          bk = bank()
                          mm(PS[:, bk, 0:256], [(hT[:, c, t * 128:(t + 1) * 128], wv[:, c, :]) for c in range(8)],
                             ["wvB"] + tt_hkeys(t), [("pb", bk)])
                          for hh in range(VCOPIES):
                              off = 0 if hh % 2 == 0 else 64
                              copy("dve", VA[:, t, hh, off:off + 64], PS[:, bk, hh * 64:(hh + 1) * 64],
                                   [("pb", bk), "VAones"], [("VA", t, 0)])
                  dbg_tap("qkB_%d" % li, qk[:, :, :], [128, 4, NT], BF16, [("qk", oi, tb) for oi in range(4) for tb in range(5) if not (oi < 2 and tb == 4 and not ctx_out)])
                  checkpoint(3.2)
                  with Scope() as sc2:
                      lp = T(sc2, "lp", [128, 64], F32)
                      tt("dve", lp[:, 0:32], spc(O_LAM, 32), spc(O_LAM + 32, 32), ALU.mult, SPR, ["lp"])
                      tt("dve", lp[:, 32:64], spc(O_LAM + 64, 32), spc(O_LAM + 96, 32), ALU.mult, SPR, ["lp"])
                      S.op("dve", lambda e: e.tensor_reduce(out=lamt[:, 0:2], in_=lp[:, :].rearrange("p (a b) -> p a b", b=32), axis=AX.X, op=ALU.add),
                           ["lp"], ["lamt"])
                      act(lamt[:, 2:4], lamt[:, 0:2], AF.Exp, ["lamt"], ["lamt"])
                      tt("dve", lamt[:, 4:5], lamt[:, 2:3], lamt[:, 3:4], ALU.subtract, ["lamt"], ["lamt"])
                      tt("dve", lamt[:, 5:6], lamt[:, 4:5], spc(O_LAMI), ALU.add, ["lamt"] + SPR, ["lamt"])
                      ts("dve", lamt[:, 6:7], lamt[:, 5:6], -1.0, None, ALU.mult, None, ["lamt"], ["lamt"])
                  NEGLAM = lamt[:, 6:7]
                  checkpoint(3.3)
                  with Scope() as sc2:
                      Eb = [T(sc2, "Eb%d" % i, [128, 2, 512], BF16) for i in range(3)]
                      R = [T(sc2, "Rr%d" % i, [128, 512], F32) for i in range(2)]
                      o1 = T(sc2, "o1", [128, 512], F32)
                      o2 = T(sc2, "o2", [128, 512], F32)
                      OP = T(sc2, "OP", [128, 512], F32)
                      sqo = T(sc2, "sqo", [128, 512], BF16)
                      rso = T(sc2, "rso", [128, 512], F32)
                      SCALE = 32.0 ** -0.5
                      steps = []
                      pairs_l = [(qb, hp, 18 if qb < 4 else 2) for qb in tbs_o for hp in range(2)]
                      for pj, (qb, hp, nk_) in enumerate(pairs_l):
                          kcs = list(range(18)) if qb < 4 else [16, 17]
                          for hl in range(2):
                              for ki, kc in enumerate(kcs):
                                  steps.append(("kv", qb, hp, hl, ki, kc, len(kcs)))
                          nxt = pairs_l[pj + 1][2] if pj + 1 < len(pairs_l) else 0
                          steps.append(("subln", qb, hp, 10 if nxt == 18 else 0))
                      order = []
                      pending = []
                      for st_ in steps:
                          pending = [(d - 1, x) for d, x in pending]
                          if st_[0] == "subln":
                              pending.append((st_[3], st_))
                          else:
                              order.append(st_)
                          for d, x in list(pending):
                              if d <= 0:
                                  order.append(x)
                                  pending.remove((d, x))
                      order += [x for _, x in pending]
                      steps = order
                      wa_n = None
                      if li + 1 < L:
                          wa_n = [T(sc2, "wan%d" % i, [128, 8, 512], BF16) for i in range(2)]
                          ada_params(li + 1)
                          ada_load(li + 1, 0, wa_n)
                          ada_load(li + 1, 1, wa_n)
                          order = []
                          nkv = 0
                          jg_n = 0
                          for st_ in steps:
                              order.append(st_)
                              if st_[0] == "kv":
                                  nkv += 1
                                  if nkv % 22 == 0 and jg_n < 12:
                                      order.append(("ada", jg_n))
                                      jg_n += 1
                          while jg_n < 12:
                              order.append(("ada", jg_n))
                              jg_n += 1
                          steps = order

                      skip_warm = set()
                      for i_, st_ in enumerate(steps):
                          if st_[0] == "ada":
                              skip_warm.update(range(i_ + 1, i_ + 1 + SKIPW))

                      def stage1(i):
                          stp = steps[i]
                          sb = (0, 2)[i % 2]
                          if stp[0] == "kv":
                              _, qb, hp, hl, ki, kc, nk = stp
                              q0, qn = cols(qb)
                              pb = hl * 64
                              eb = Eb[i % 3]
                              ek = ("Eb", i % 3)
                              ktb = kc // 4 if kc < 16 else 4
                              groups = []
                              for c2 in range(2):
                                  p0 = pb + 32 * c2
                                  groups.append((PS[:, sb + c2, 0:qn],
                                                 [(qk[p0:p0 + 32, 2 + hp, kc * 128:(kc + 1) * 128], qk[p0:p0 + 32, hp, q0:q0 + qn])],
                                                 (p0, 0)))
                              if PE_WARM and i not in skip_warm:
                                  groups = [(PS[:, sb + c2, 0:512], [(VA[:, 0, 0, :], qk[:, 0, 0:512])], None) for c2 in range(PE_WARM)] + groups
                              mm_multi(groups, [("qk", 2 + hp, ktb), ("qk", hp, qb)], [("pb", sb), ("pb", sb + 1)])
                              act(eb[:, :, 0:qn], PS[:, sb:sb + 2, 0:qn], AF.Exp, [("pb", sb), ("pb", sb + 1)], [ek], scale=SCALE)
                      def stage2(i):
                          stp = steps[i]
                          sb = (0, 2)[i % 2]
                          if stp[0] == "ada":
                              jg = stp[1]
                              ada_piece(li + 1, jg, wa_n, sb)
                              if jg + 2 < 12:
                                  ada_load(li + 1, jg + 2, wa_n)
                              if jg == 11:
                                  ada_finish(li + 1)
                              return
                          if stp[0] != "kv":
                              _, qb, hp, _d = stp
                              q0, qn = cols(qb)
                              act(sqo[:, 0:qn], OP[:, 0:qn], AF.Square, [("OP", 0), ("OP", 1)], ["sqo"])
                              mm(PS[:, sb, 0:qn], [(B64, sqo[:, 0:qn])], ["sqo", "cb"], [("pb", sb)])
                              act(rso[:, 0:qn], PS[:, sb, 0:qn], AF.Ln, [("pb", sb)], ["rso"], bias=EPS)
                              act(rso[:, 0:qn], rso[:, 0:qn], AF.Exp, ["rso"], ["rso"], scale=-0.5)
                              stt(OP[:, 0:qn], OP[:, 0:qn], spc(O_OMLI), rso[:, 0:qn], ALU.mult, ALU.mult,
                                  [("OP", 0), ("OP", 1), "rso"] + SPR, [("OP", 0), ("OP", 1)])
                              act(ym[:, hp, q0:q0 + qn], OP[:, 0:qn], AF.Identity, [("OP", 0), ("OP", 1)] + SPR, [("ymB", qb)], scale=spc(O_SUBLN))
                              return
                          _, qb, hp, hl, ki, kc, nk = stp
                          q0, qn = cols(qb)
                          h = 2 * hp + hl
                          po, pz = (0, 64) if hl == 0 else (64, 0)
                          accb = (4, 5) if hl == 0 else (6, 7)
                          eb = Eb[i % 3]
                          ek = ("Eb", i % 3)
                          groups = [(PS[:, accb[c2], 0:qn], VA[:, kc, h, :], eb[:, c2, 0:qn]) for c2 in range(2)]

                          def fn(e, groups=groups, first=(ki == 0), last=(ki == nk - 1)):
                              for out_ap, l, r in groups:
                                  inst = e.matmul(out_ap, lhsT=l, rhs=r, start=first, stop=last)
                              return inst
                          S.op("pe", fn, [ek, ("VA", kc, 0), "VAones"], [("pb", accb[0]), ("pb", accb[1])])
                          if ki == nk - 1:
                              recip(R[0][po:po + 64, 0:qn], PS[pz:pz + 64, accb[0], 0:qn], [("pb", accb[0])], [("Rr", 0)])
                              recip(R[1][po:po + 64, 0:qn], PS[pz:pz + 64, accb[1], 0:qn], [("pb", accb[1])], [("Rr", 1)])
                              tt("dve", o1[po:po + 64, 0:qn], PS[po:po + 64, accb[0], 0:qn], R[0][po:po + 64, 0:qn], ALU.mult,
                                 [("pb", accb[0]), ("Rr", 0)], ["o1"])
                              tt("dve", o2[po:po + 64, 0:qn], PS[po:po + 64, accb[1], 0:qn], R[1][po:po + 64, 0:qn], ALU.mult,
                                 [("pb", accb[1]), ("Rr", 1)], ["o2"])
                              stt(OP[po:po + 64, 0:qn], o2[po:po + 64, 0:qn], NEGLAM[po:po + 64, :], o1[po:po + 64, 0:qn], ALU.mult, ALU.add,
                                  ["o1", "o2", "lamt"], [("OP", hl)])

                      nst = len(steps)
                      for i in range(nst + 2):
                          if i < nst:
                              stage1(i)
                          if i >= 2:
                              stage2(i - 2)
                  dbg_tap("yB_%d" % li, ym[:, :, :], [128, 2, NT], BF16, [("ymB", k) for k in tbs_o])
                  wout_partial(sc, ym, 1, lambda tb: [("ymB", tb)])

              checkpoint(4)
              scCD = Scope()
              scCD.__enter__()
              ymC_t = T(scCD, "ymC", [128, 2, NT], BF16)
              with Scope() as sc:
                  ym = ymC_t
                  wna = T(sc, "wna", [128, 8, 768], BF16)
                  dma("pool", wna[:, :, 0:512], winv[:, :, 1024:1536], [], ["wna"], "wna")
                  dma("pool", wna[:, :, 512:768], winv[:, :, 1536:1792], [], ["wna"], "wna")
                  mbv = mb_d[li].rearrange("p (h i q) -> p h i q", h=4, i=21)
                  for ci in range(2):
                      with Scope() as sc2:
                          qn_t = T(sc2, "qnC", [128, NT], BF16)
                          kn_t = T(sc2, "knC", [128, NT], BF16)
                          VN = T(sc2, "VN", [128, 18, 2, 128], BF16)
                          MB = T(sc2, "MB", [128, 2, 21, 128], BF16)
                          sqb = [T(sc2, "sqC%d" % i, [128, 512], BF16) for i in range(2)]
                          rs = [T(sc2, "rsC%d" % i, [128, 512], F32) for i in range(2)]
                          Es = [T(sc2, "EsC%d" % i, [128, 640], F32) for i in range(3)]
                          Eb = [T(sc2, "EbC%d" % i, [128, 896], BF16) for i in range(3)]
                          Rn = [T(sc2, "RnC%d" % i, [128, 128], F32) for i in range(3)]
                          S.op("pool", lambda e, VN=VN: e.memset(VN[:, :, :, :].rearrange("p a h d -> p (a h d)"), 1.0), [], ["VNones"])
                          for hl in range(2):
                              dma("pool", MB[:, hl, :, :], mbv[:, 2 * ci + hl, :, :], [], ["MB"], "MB")
                          it = 0
                          for isq in (True, False):
                              wcol = (0 if isq else 256) + ci * 128
                              dst = qn_t if isq else kn_t
                              gcol = O_NQN if isq else O_NKN
                              for tb in (tbs_o if isq else tbs_all):
                                  c0, n = cols(tb)
                                  i2 = it % 2
                                  it += 1
                                  bk = bank()
                                  mm(PS[:, bk, 0:n], [(wna[:, c, wcol:wcol + 128], hT[:, c, c0:c0 + n]) for c in range(8)],
                                     ["wna"] + hkeys(tb), [("pb", bk)])
                                  act(sqb[i2][:, 0:n], PS[:, bk, 0:n], AF.Square, [("pb", bk)], [("sqC", i2)])
                                  bk2 = bank()
                                  mm(PS[:, bk2, 0:n], [(B64, sqb[i2][:, 0:n])], [("sqC", i2), "cb"], [("pb", bk2)])
                                  act(rs[i2][:, 0:n], PS[:, bk2, 0:n], AF.Ln, [("pb", bk2)], [("rsC", i2)], bias=EPS)
                                  act(rs[i2][:, 0:n], rs[i2][:, 0:n], AF.Exp, [("rsC", i2)], [("rsC", i2)], scale=-0.5)
                                  stt(dst[:, c0:c0 + n], PS[:, bk, 0:n], spc(gcol), rs[i2][:, 0:n], ALU.mult, ALU.mult,
                                      [("pb", bk), ("rsC", i2)] + SPR, [("qnC" if isq else "knC", tb)])
                          for t in range(18):
                              bk = bank()
                              mm(PS[:, bk, 0:128], [(hT[:, c, t * 128:(t + 1) * 128], wna[:, c, 512 + ci * 128:512 + (ci + 1) * 128]) for c in range(8)],
                                 ["wna"] + tt_hkeys(t), [("pb", bk)])
                              copy("dve", VN[:, t, 0, 0:64], PS[:, bk, 0:64], [("pb", bk), "VNones"], [("VN", t, 0)])
                              copy("dve", VN[:, t, 1, 64:128], PS[:, bk, 64:128], [("pb", bk), "VNones"], [("VN", t, 1)])
                          nblk = 18 if ctx_out else 16
                          nsteps = []
                          for m in range(nblk):
                              for hl in range(2):
                                  nsteps.append((m, hl))

                          def na_info(m):
                              if m < 16:
                                  a0, nl, id0 = na_local(m)
                                  return nl, id0, list(range(a0, a0 + nl)) + [16, 17], m // 4
                              return 0, 0, [16, 17], 4

                          def na_stage1(i):
                              m, hl = nsteps[i]
                              nl, id0, chunks, qtb = na_info(m)
                              nch = len(chunks)
                              pb = hl * 64
                              s2 = i % 3
                              sbk = (0, 2, 4)[s2]
                              psS = PSF[:, sbk * 512:sbk * 512 + 1024]
                              groups = []
                              rk = [("qnC", qtb)]
                              for j, a in enumerate(chunks):
                                  groups.append((psS[:, j * 128:(j + 1) * 128],
                                                 [(kn_t[pb:pb + 64, a * 128:(a + 1) * 128], qn_t[pb:pb + 64, m * 128:(m + 1) * 128])],
                                                 (pb, 0)))
                                  rk.append(("knC", a // 4 if a < 16 else 4))
                              if NA_WARM:
                                  groups = [(PS[:, sbk, 0:512], [(VN[:, 0, 0, :], kn_t[:, 0:512])], None) for _ in range(NA_WARM)] + groups
                              mm_multi(groups, rk + ["VNones", ("VN", 0, 0)], [("pb", sbk), ("pb", sbk + 1)])
                              if nl:
                                  stt(Es[s2][:, 0:nl * 128], psS[:, 0:nl * 128], 0.125,
                                      MB[:, hl, id0:id0 + nl, :].rearrange("p i q -> p (i q)"), ALU.mult, ALU.add,
                                      [("pb", sbk), ("pb", sbk + 1), "MB"], [("EsC", s2)])
                                  act(Eb[s2][:, 0:nl * 128], Es[s2][:, 0:nl * 128], AF.Exp, [("EsC", s2)], [("EbC", s2)])
                              act(Eb[s2][:, nl * 128:nch * 128], psS[:, nl * 128:nch * 128], AF.Exp, [("pb", sbk), ("pb", sbk + 1)], [("EbC", s2)], scale=0.125)

                          def na_stage2(i):
                              m, hl = nsteps[i]
                              nl, id0, chunks, qtb = na_info(m)
                              po, pz = (0, 64) if hl == 0 else (64, 0)
                              s2 = i % 3
                              abk = (6, 7)[i % 2]
                              pairs = [(VN[:, a, hl, :], Eb[s2][:, j * 128:(j + 1) * 128]) for j, a in enumerate(chunks)]
                              mm(PS[:, abk, 0:128], pairs, [("EbC", s2), "VNones"] + [("VN", a, hl) for a in chunks], [("pb", abk)])
                              if NA_ACT_RECIP:
                                  act(Rn[s2][pz:pz + 64, :], PS[pz:pz + 64, abk, 0:128], AF.Ln, [("pb", abk)], [("RnC", s2)])
                                  act(Rn[s2][pz:pz + 64, :], Rn[s2][pz:pz + 64, :], AF.Exp, [("RnC", s2)], [("RnC", s2)], scale=-1.0)
                                  return
                              else:
                                  recip(Rn[s2][po:po + 64, :], PS[pz:pz + 64, abk, 0:128], [("pb", abk)], [("RnC", s2)])
                                  tt("dve", ym[po:po + 64, ci, m * 128:(m + 1) * 128], PS[po:po + 64, abk, 0:128], Rn[s2][po:po + 64, :], ALU.mult,
                                     [("pb", abk), ("RnC", s2)], [("ymC", qtb, ci, hl)])

                          def na_stage2b(i):
                              if not NA_ACT_RECIP:
                                  return
                              m, hl = nsteps[i]
                              nl, id0, chunks, qtb = na_info(m)
                              po, pz = (0, 64) if hl == 0 else (64, 0)
                              s2 = i % 3
                              abk = (6, 7)[i % 2]
                              tt("dve", ym[po:po + 64, ci, m * 128:(m + 1) * 128], PS[po:po + 64, abk, 0:128], Rn[s2][pz:pz + 64, :], ALU.mult,
                                 [("pb", abk), ("RnC", s2)], [("ymC", qtb, ci, hl)])

                          for i in range(len(nsteps) + 2):
                              if i >= 2:
                                  na_stage2(i - 2)
                              if i < len(nsteps):
                                  na_stage1(i)
                              if i >= 2:
                                  na_stage2b(i - 2)
                  dbg_tap("yC_%d" % li, ym[:, :, :], [128, 2, NT], BF16, [("ymC", k, ci, hl) for k in tbs_o for ci in range(2) for hl in range(2)])

              checkpoint(5)
              with Scope() as sc:
                  ym = T(sc, "ymD", [128, 2, NT], BF16)
                  uT = T(sc, "uT", [128, 2, NT], BF16)
                  wg = T(sc, "wg", [128, 8, 512], BF16)
                  wsb = T(sc, "wsb", [128, 4, 128], BF16)
                  vg = [T(sc, "vg%d" % i, [128, 256], F32) for i in range(3)]
                  vsq = [T(sc, "vsq%d" % i, [128, 256], F32) for i in range(3)]
                  vss = [T(sc, "vss%d" % i, [128, 4], F32) for i in range(3)]
                  vn = [T(sc, "vn%d" % i, [128, 256], BF16) for i in range(3)]
                  gt = [[T(sc, "gt%d%d" % (i, j), [128, 128], F32) for j in range(2)] for i in range(3)]
                  dma("pool", wg[:, :, :], winv[:, :, 1792:2304], [], ["wg"], "wg")
                  woCD = wo_prefetch(sc, 2, 4)
                  dma("pool", wsb[:, :, :].rearrange("p g q -> p (g q)"), ws_d[li, :, :], [], ["wsb"], "wsb")
                  for oc in range(2):
                      def ev(bk, tb, c0, n, oc=oc):
                          act(uT[:, oc, c0:c0 + n], PS[:, bk, 0:n], AF.Gelu_apprx_tanh, [("pb", bk)], [("uT", oc, tb)])
                      proj_fm(wg, "wg", oc * 128, tbs_o, ev)
                  def gm_a(t):
                      i2 = t % 3
                      bk = bank((0, 1, 2, 3))
                      mm(PS[:, bk, 0:256], [(hT[:, c, t * 128:(t + 1) * 128], wg[:, c, 256:512]) for c in range(8)],
                         ["wg"] + tt_hkeys(t), [("pb", bk)])
                      act(vg[i2][:, :], PS[:, bk, 0:256], AF.Gelu_apprx_tanh, [("pb", bk)], [("vg", i2)])
                      tt("pool", vsq[i2][:, :], vg[i2][:, :], vg[i2][:, :], ALU.mult, [("vg", i2)], [("vsq", i2)])
                      S.op("dve", lambda e, i2=i2: e.tensor_reduce(out=vss[i2][:, :], in_=vsq[i2][:, :].rearrange("p (g d) -> p g d", d=64), axis=AX.X, op=ALU.add),
                           [("vsq", i2)], [("vss", i2)])

                  def gm_a2(t):
                      i2 = t % 3
                      act(vss[i2][:, :], vss[i2][:, :], AF.Ln, [("vss", i2)], [("vss", i2)], scale=1.0 / 64.0, bias=EPS)
                      act(vss[i2][:, :], vss[i2][:, :], AF.Exp, [("vss", i2)], [("vss", i2)], scale=-0.5)
                      for g in range(4):
                          stt(vn[i2][:, g * 64:(g + 1) * 64], vg[i2][:, g * 64:(g + 1) * 64], vss[i2][:, g:g + 1], spc(O_GN + g * 64, 64),
                              ALU.mult, ALU.mult, [("vg", i2), ("vss", i2)] + SPR, [("vn", i2)])

                  def gm_b(t):
                      i2 = t % 3
                      tb = t // 4 if t < 16 else 4
                      bk2 = bank((4, 5, 6, 7))
                      groups = []
                      for g in range(4):
                          pp = (g % 2) * 64
                          groups.append((PS[pp:pp + 64, bk2, (g // 2) * 128:(g // 2) * 128 + 128],
                                         [(vn[i2][:, g * 64:(g + 1) * 64], wsb[:, g, :])], (0, pp)))
                      mm_multi(groups, [("vn", i2), "wsb"], [("pb", bk2)])
                      for cj in range(2):
                          gk = ("gt", cj, i2)
                          tt("dve", gt[i2][cj][:, :], PS[:, bk2, cj * 128:(cj + 1) * 128], sp[:, O_BST + cj * 128:O_BST + (cj + 1) * 128], ALU.add,
                             [("pb", bk2)] + SPR, [gk])
                          tt("dve", ym[:, cj, t * 128:(t + 1) * 128], gt[i2][cj][:, :], uT[:, cj, t * 128:(t + 1) * 128], ALU.mult,
                             [gk, ("uT", cj, tb)], [("ymD", tb, cj)])

                  for k in range(ntt_o + 2):
                      if k >= 2:
                          gm_b(k - 2)
                      if 1 <= k <= ntt_o:
                          gm_a2(k - 1)
                      if k < ntt_o:
                          gm_a(k)
                  dbg_tap("yD_%d" % li, ym[:, :, :], [128, 2, NT], BF16, [("ymD", k, cj) for k in tbs_o for cj in range(2)])
                  wout_multi(sc, [(ymC_t, lambda tb: [("ymC", tb, ci, hl) for ci in range(2) for hl in range(2)]),
                                  (ym, lambda tb: [("ymD", tb, 0), ("ymD", tb, 1)])], 2, woCD)
              scCD.__exit__(None, None, None)
              dbg_tap("xmid_%d" % li, xT[:, :, :], [128, 8, NT], F32, [k for tb in range(5) for k in xkeys(tb)])

              checkpoint(6)
              wupv = wup_d[li].rearrange("(c p) n -> p c n", p=128)
              wdnv = wdn_d[li].rearrange("(j p) n -> p j n", p=128)
              segs = [(0, 2048)] + ([(2048, 256)] if ctx_out else [])
              with Scope() as sc:
                  wup = [[T(sc, "wup%d%d" % (i, hv), [128, 8, 512], BF16) for hv in range(2)] for i in range(2)]
                  wdn = T(sc, "wdn", [128, 4, 1024], BF16)
                  j0_, nj_ = FFN_PASSES[0]
                  for hv in range(2):
                      cs = hv * 2816 + j0_ * 128
                      dma("pool", wup[0][hv][:, :, 0:nj_ * 128], wupv[:, :, cs:cs + nj_ * 128], [], [("wup", 0, hv)], "wup0%d" % hv)
                  for hlf in range(2):
                      dma("pool", wdn[:, 0:nj_, hlf * 512:(hlf + 1) * 512], wdnv[:, j0_:j0_ + nj_, hlf * 512:(hlf + 1) * 512], [], ["wdn"], "wdn")
                  norm_phase(1, tbs_o)
                  hid = T(sc, "hid", [128, 4, NT], BF16)
                  accg = T(sc, "accg", [128, 2048], F32)
                  accv = T(sc, "accv", [128, 2048], F32)
                  sg = T(sc, "sg", [128, 2048], BF16)
                  cacc = T(sc, "cacc", [128, 2, 4, 256], F32)
                  csg = T(sc, "csg", [128, 4, 256], BF16)
                  evi = [0]
                  for pi, (j0, nj) in enumerate(FFN_PASSES):
                      wu = wup[pi % 2]
                      for hv in range(2):
                          cs = hv * 2816 + j0 * 128
                          if pi > 0:
                              dma("pool", wu[hv][:, :, 0:nj * 128], wupv[:, :, cs:cs + nj * 128], [], [("wup", pi % 2, hv)], "wup%d%d" % (pi % 2, hv))
                      for hlf in range(2):
                          if pi > 0:
                              dma("pool", wdn[:, 0:nj, hlf * 512:(hlf + 1) * 512], wdnv[:, j0:j0 + nj, hlf * 512:(hlf + 1) * 512], [], ["wdn"], "wdn")
                      for jj in range(nj):
                          j = j0 + jj
                          for hv in range(2):
                              jc = hv * 22 + j
                              acc = accg if hv == 0 else accv
                              rb = hv * 4
                              for (s0, sn) in [(0, 2048)]:
                                  an = "accg" if hv == 0 else "accv"
                                  row = PSF[:, rb * 512:rb * 512 + sn]
                                  pieces = [(0, 1024), (1024, 2048)] if sn == 2048 else [(0, sn)]
                                  cw = lambda tap, jc=jc: sp[:, O_CW + jc * 3 + tap:O_CW + jc * 3 + tap + 1]
                                  bkey = lambda col: ("pb", rb + col // 512)
                                  for pidx, (p0_, p1_) in enumerate(pieces):
                                      groups = []
                                      rk = [("wup", pi % 2, hv)]
                                      wk_ = []
                                      for c0_ in range(p0_, p1_, 512):
                                          w = min(512, p1_ - c0_)
                                          groups.append((PS[:, rb + c0_ // 512, 0:w],
                                                         [(wu[hv][:, c, jj * 128:(jj + 1) * 128], hT[:, c, s0 + c0_:s0 + c0_ + w]) for c in range(8)], None))
                                          rk += hkeys((s0 + c0_) // 512 if s0 < 2048 else 4)
                                          wk_.append(bkey(c0_))
                                      mm_multi(groups, rk, wk_)
                                  for pidx, (p0_, p1_) in enumerate(pieces):
                                      ak = (an, s0, pidx)
                                      pbk = [bkey(c0_) for c0_ in range(p0_, p1_, 512)]
                                      act(acc[:, s0 + p0_:s0 + p1_], row[:, p0_:p1_], AF.Identity, pbk + SPR, [ak], scale=cw(1), bias=sp[:, O_CB + jc:O_CB + jc + 1])
                                      lo = max(p0_, 1)
                                      stt(acc[:, s0 + lo:s0 + p1_], row[:, lo - 1:p1_ - 1], cw(0), acc[:, s0 + lo:s0 + p1_], ALU.mult, ALU.add,
                                          pbk + [bkey(lo - 1), ak] + SPR, [ak])
                                  for pidx, (p0_, p1_) in enumerate(pieces):
                                      ak = (an, s0, pidx)
                                      pbk = [bkey(c0_) for c0_ in range(p0_, p1_, 512)]
                                      hi = min(p1_, sn - 1)
                                      stt(acc[:, s0 + p0_:s0 + hi], row[:, p0_ + 1:hi + 1], cw(2), acc[:, s0 + p0_:s0 + hi], ALU.mult, ALU.add,
                                          pbk + [bkey(hi), ak] + SPR, [ak])
                          for (s0, sn) in [(0, 2048)]:
                              pieces = [(0, 1024), (1024, 2048)] if sn == 2048 else [(0, sn)]
                              for pidx, (p0_, p1_) in enumerate(pieces):
                                  act(sg[:, s0 + p0_:s0 + p1_], accg[:, s0 + p0_:s0 + p1_], AF.Silu, [("accg", s0, pidx)], [("sg", s0, pidx)])
                                  tt("dve", hid[:, jj, s0 + p0_:s0 + p1_], sg[:, s0 + p0_:s0 + p1_], accv[:, s0 + p0_:s0 + p1_], ALU.mult,
                                     [("sg", s0, pidx), ("accv", s0, pidx)], [("hid", jj, s0)])
                      if ctx_out:
                          for jj in range(nj):
                              j = j0 + jj
                              for hv in range(2):
                                  jc = hv * 22 + j
                                  bk = bank()
                                  cac = cacc[:, hv, jj, :]
                                  ck = ("cacc", hv, jj)
                                  cwc = lambda tap, jc=jc: sp[:, O_CW + jc * 3 + tap:O_CW + jc * 3 + tap + 1]
                                  mm(PS[:, bk, 0:256], [(wu[hv][:, c, jj * 128:(jj + 1) * 128], hT[:, c, 2048:2304]) for c in range(8)],
                                     [("wup", pi % 2, hv)] + hkeys(4), [("pb", bk)])
                                  act(cac, PS[:, bk, 0:256], AF.Identity, [("pb", bk)] + SPR, [ck], scale=cwc(1), bias=sp[:, O_CB + jc:O_CB + jc + 1])
                                  stt(cac[:, 1:256], PS[:, bk, 0:255], cwc(0), cac[:, 1:256], ALU.mult, ALU.add, [("pb", bk), ck] + SPR, [ck])
                                  stt(cac[:, 0:255], PS[:, bk, 1:256], cwc(2), cac[:, 0:255], ALU.mult, ALU.add, [("pb", bk), ck] + SPR, [ck])
                          for jj in range(nj):
                              act(csg[:, jj, :], cacc[:, 0, jj, :], AF.Silu, [("cacc", 0, jj)], [("csg", jj)])
                              tt("dve", hid[:, jj, 2048:2304], csg[:, jj, :], cacc[:, 1, jj, :], ALU.mult, [("csg", jj), ("cacc", 1, jj)], [("hid", jj, 2048)])
                      for tb in tbs_o:
                          c0, n = cols(tb)
                          v = 0 if tb < 4 else 1
                          for oc in range(8):
                              bk = bank()
                              mm(PS[:, bk, 0:n], [(wdn[:, jj, oc * 128:(oc + 1) * 128], hid[:, jj, c0:c0 + n]) for jj in range(nj)],
                                 ["wdn"] + [("hid", jj, 0 if tb < 4 else 2048) for jj in range(nj)], [("pb", bk)])
                              evi[0] += 1
                              if False:
                                  ei = (evi[0] // EVAC_POOL) % 3
                                  act(etmp[ei][:, 0:n], PS[:, bk, 0:n], AF.Identity, [("pb", bk), MODK], [("etmp", ei)], scale=gate_ap(1, oc, v))
                                  tt("pool", xT[:, oc, c0:c0 + n], xT[:, oc, c0:c0 + n], etmp[ei][:, 0:n], ALU.add, [("etmp", ei), ("x", oc, tb)], [("x", oc, tb)])
                              else:
                                  stt(xT[:, oc, c0:c0 + n], PS[:, bk, 0:n], gate_ap(1, oc, v), xT[:, oc, c0:c0 + n], ALU.mult, ALU.add,
                                      [("pb", bk), MODK, ("x", oc, tb)], [("x", oc, tb)])
                              if li == L - 1 and pi == len(FFN_PASSES) - 1 and tb < 4 and not S.stopped:
                                  dma("sp", ov_early[:, oc, c0:c0 + n], xT[:, oc, c0:c0 + n], [("x", oc, tb)], [], "xo_%d_%d" % (oc, tb))
                                  stored_early.add((oc, tb))
              dbg_tap("xout_%d" % li, xT[:, :, :], [128, 8, NT], F32, [k for tb in range(5) for k in xkeys(tb)])

          except _Stop:
            pass
        S.stopped = False
        ov = out_d.rearrange("(c p) n -> p c n", p=128)
        for c in range(8):
            for tb in range(4):
                if (c, tb) not in stored_early:
                    dma("sp", ov[:, c, tb * 512:(tb + 1) * 512], xT[:, c, tb * 512:(tb + 1) * 512], [("x", c, tb)], [], "xout%d_%d" % (c, tb))
        if ctx_last:
            cov = cout_d.rearrange("(c p) n -> p c n", p=128)
            dma("sp", cov[:, :, :], xT[:, :, 2048:2304], xkeys(4), [], "cout")
        S.finish()
        with nc.Block() as block:
            S.emit(block)
    return nc, dbg_outs


def _bf(a):
    return np.ascontiguousarray(a.astype(np.float32)).astype(ml_dtypes.bfloat16)


def host_constants():
    p = np.arange(128)
    cb = np.zeros((128, NCB), np.float32)
    cb[:, C_ONESD:C_ONESD + 128] = 1.0 / 1024.0
    cb[:, C_B32:C_B32 + 128] = (p[:, None] // 32 == p[None, :] // 32) / 32.0
    cb[:, C_B64:C_B64 + 128] = (p[:, None] // 64 == p[None, :] // 64) / 64.0
    prot = np.zeros((128, 128), np.float32)
    for m in range(128):
        d = m % 32
        if d < 16:
            prot[m + 16, m] = -1.0
        else:
            prot[m - 16, m] = 1.0
    cb[:, C_PROT:C_PROT + 128] = prot
    for c in range(2):
        ch = c * 128 + p
        grp, cc = ch // 64, ch % 64
        l = np.arange(64)
        ang = 2 * np.pi * np.outer(cc, l) / 64.0
        blkc = np.zeros((128, 256), np.float32)
        blks = np.zeros((128, 256), np.float32)
        for i in range(128):
            blkc[i, grp[i] * 64:(grp[i] + 1) * 64] = np.cos(ang[i]) / 8.0
            blks[i, grp[i] * 64:(grp[i] + 1) * 64] = -np.sin(ang[i]) / 8.0
        cb[:, C_CS64 + c * 512:C_CS64 + c * 512 + 256] = blkc
        cb[:, C_CS64 + c * 512 + 256:C_CS64 + (c + 1) * 512] = blks
    t = np.arange(256)
    a256 = 2 * np.pi * ((np.outer(t, t)) % 256) / 256.0
    c256, s256 = np.cos(a256) / 16.0, np.sin(a256) / 16.0
    for tt_ in range(2):
        cb[:, C_C256 + tt_ * 256:C_C256 + (tt_ + 1) * 256] = c256[tt_ * 128:(tt_ + 1) * 128]
        cb[:, C_S256 + tt_ * 256:C_S256 + (tt_ + 1) * 256] = s256[tt_ * 128:(tt_ + 1) * 128]
    n = np.arange(2048, dtype=np.int64)
    aN = 2 * np.pi * ((np.outer(n, n)) % 2048).astype(np.float64) / 2048.0
    sc = 1.0 / math.sqrt(2048.0)
    CN = _bf(np.cos(aN) * sc)
    SN = _bf(np.sin(aN) * sc)
    freqs = (10000.0 ** (-np.arange(8, dtype=np.float32) / 8)).astype(np.float32)
    tok = np.arange(2048)
    row = (tok // 64).astype(np.float32)
    col = (tok % 64).astype(np.float32)
    ang = np.concatenate([row[:, None] * freqs, col[:, None] * freqs], axis=-1).astype(np.float32)
    slot = (p % 32) % 16
    rope = np.stack([np.cos(ang)[:, slot].T, np.sin(ang)[:, slot].T], axis=1).astype(np.float32)
    return _bf(cb), np.ascontiguousarray(rope), CN, SN


def na_mask_index():
    def one(m, a):
        k = np.arange(128)
        kr, kc = 2 * a + k // 64, k % 64
        qr, qc = 2 * m + k // 64, k % 64
        r0 = np.clip(qr - 4, 0, 24)
        c0 = np.clip(qc - 8, 0, 48)
        okr = (kr[:, None] >= r0[None, :]) & (kr[:, None] <= r0[None, :] + 7)
        okc = (kc[:, None] >= c0[None, :]) & (kc[:, None] <= c0[None, :] + 15)
        ro = kr[:, None] - qr[None, :] + 7
        co = kc[:, None] - qc[None, :] + 15
        return np.where(okr & okc, ro * 31 + co, 15 * 31)
    ids = [one(5, 5 + d) for d in range(-2, 3)]
    for m, a0 in ((0, 0), (1, 0), (14, 12), (15, 12)):
        ids += [one(m, a) for a in range(a0, a0 + 4)]
    return np.stack(ids)


def layer_pack(li, inp, midx):
    p = np.arange(128)
    sp = np.zeros((128, NS), np.float32)
    sp[:, O_BADA:O_BADA + 48] = inp["b_ada"][li].reshape(48, 128).T
    sp[:, O_GMIX:O_GMIX + 8] = inp["g_mix"][li].reshape(8, 128).T
    sp[:, O_GFFN:O_GFFN + 8] = inp["g_ffn"][li].reshape(8, 128).T
    sp[:, O_DQN] = inp["diff_qn"][li][p % 32]
    sp[:, O_DKN] = inp["diff_kn"][li][p % 32]
    sp[:, O_SUBLN] = inp["diff_subln"][li][p % 64]
    sp[:, O_NQN] = inp["na_qn"][li][p % 64]
    sp[:, O_NKN] = inp["na_kn"][li][p % 64]
    lam_init = 0.8 - 0.6 * math.exp(-0.3 * li)
    sp[:, O_LAMI] = lam_init
    sp[:, O_OMLI] = 1.0 - lam_init
    sp[:, O_LAM:O_LAM + 128] = inp["diff_lam"][li].reshape(1, 128)
    sp[:, O_GN:O_GN + 256] = inp["gmlp_norm"][li].reshape(1, 256)
    gb = inp["gmlp_b"][li]
    for cj in range(2):
        sp[:, O_BST + cj * 128:O_BST + (cj + 1) * 128] = gb[cj * 2 + p // 64, :]
    sp[:, O_CW:O_CW + 132] = inp["ffn_conv"][li].reshape(3, 44, 128).transpose(2, 1, 0).reshape(128, 132)
    sp[:, O_CB:O_CB + 44] = inp["ffn_conv_b"][li].reshape(44, 128).T
    wsT = np.ascontiguousarray(inp["gmlp_ws"][li].transpose(2, 0, 1)).reshape(128, 512)
    rp = np.concatenate([inp["na_rpb"][li].reshape(4, 465), np.full((4, 1), -30000.0, np.float32)], axis=1)
    mb = rp[:, midx]
    mb = np.ascontiguousarray(mb.transpose(2, 0, 1, 3)).reshape(128, 4 * 21 * 128)
    return sp, wsT, mb


_CACHE = {}


def _get_prog(n_layers, ctx_last):
    key = (n_layers, ctx_last)
    if key not in _CACHE:
        _CACHE[key] = build(n_layers, ctx_last)[0]
    return _CACHE[key]


def _consts():
    if "c" not in _CACHE:
        _CACHE["c"] = host_constants() + (na_mask_index(),)
    return _CACHE["c"]


FUSED = True


def kernel(**inp):
    inp = {k: np.asarray(v, dtype=np.float32) for k, v in inp.items()}
    cbf, rope, CN, SN, midx = _consts()
    B = inp["x"].shape[0]
    packs = [layer_pack(li, inp, midx) for li in range(DEPTH)]
    xT = [np.ascontiguousarray(inp["x"][b].T) for b in range(B)]
    cT = [np.ascontiguousarray(inp["ctx"][b].T) for b in range(B)]
    cpk = []
    for b in range(B):
        a = np.stack([inp["c"][b].reshape(8, 128).T, inp["c_ctx"].reshape(8, 128).T], axis=-1)
        cpk.append(np.ascontiguousarray(a.reshape(128, 16)))

    def layer_inputs(lis):
        return {
            "spk": np.stack([packs[li][0] for li in lis]), "wsT": np.stack([packs[li][1] for li in lis]),
            "mb": np.stack([packs[li][2] for li in lis]),
            "w_ada": inp["w_ada"][lis], "w_in": inp["w_in"][lis], "w_out": inp["w_out"][lis],
            "w_up": inp["ffn_up"][lis], "w_dn": inp["ffn_down"][lis],
        }
    common = {"cbf": cbf, "rope": rope, "CN": CN, "SN": SN}
    if FUSED:
        nc = _get_prog(DEPTH, False)
        lw = layer_inputs(list(range(DEPTH)))
        in_maps = [dict(common, xT=xT[b], cxT=cT[b], cpk=cpk[b], **lw) for b in range(B)]
        res = run_bass_kernel_spmd(nc, in_maps, core_ids=list(range(B)))
        outs = [r["outT"] for r in res.results]
    else:
        nc = _get_prog(1, True)
        for li in range(DEPTH):
            lw = layer_inputs([li])
            in_maps = [dict(common, xT=xT[b], cxT=cT[b], cpk=cpk[b], **lw) for b in range(B)]
            res = run_bass_kernel_spmd(nc, in_maps, core_ids=list(range(B)))
            xT = [np.ascontiguousarray(r["outT"]) for r in res.results]
            cT = [np.ascontiguousarray(r["coutT"]) for r in res.results]
        outs = xT
    return np.stack([np.ascontiguousarray(o.T) for o in outs]).astype(np.float32)
```
